# Optimizing a Trainium2 kernel written in Bass

```python
import math
import jax, jax.numpy as jnp
from jax import lax
import numpy as np


D_MODEL = 1024
BATCH = 32
SEQ = 2048
DEPTH = 2

GRID_W = 64
CTX_LEN = 256
HEAD_DIM = 128
ROPE_THETA = 10000.0
EPS = 1e-6
N_MOD = 9
D_FF = 2816

A_HEADS = 4
A_KV_HEADS = 2
A_GROUP = A_HEADS // A_KV_HEADS
Q_BLOCK = 128
POOL_WINDOWS = (2, 4, 8, 16)
POOL_GROUP = 128
POOL_DIM = POOL_GROUP * len(POOL_WINDOWS)
A_Q_DIM = A_HEADS * HEAD_DIM
A_KV_DIM = A_KV_HEADS * HEAD_DIM
AB_IN = A_Q_DIM + 2 * A_KV_DIM + POOL_DIM
AB_OUT = A_Q_DIM + POOL_DIM

C_HEADS = 8
C_DIM = C_HEADS * HEAD_DIM
CONV_W = 3
CHUNK = 64
C_IN = 4 * C_DIM + 4 * C_HEADS

N_EVEN = (DEPTH + 1) // 2
N_ODD = DEPTH // 2

kernel_name = 'hybrid_dit_gqa_pool_gdn_macaron'


def rms_norm(t, g):
    tf = t.astype(jnp.float32)
    y = tf * lax.rsqrt(jnp.mean(tf * tf, axis=-1, keepdims=True) + EPS)
    return (y * g.astype(jnp.float32)).astype(t.dtype)


def l2_norm(t):
    return t * lax.rsqrt(jnp.sum(t * t, axis=-1, keepdims=True) + EPS)


def adaln(cond, w, b):
    m = jax.nn.silu(cond) @ w + b
    return m.reshape(cond.shape[0], 1, N_MOD, cond.shape[-1])


def modulate(t, g, m, s):
    return rms_norm(t, g) * (1 + m[:, :, 3 * s + 1]) + m[:, :, 3 * s]


def swiglu(t, wg, wu, wd):
    return (jax.nn.silu(t @ wg) * (t @ wu)) @ wd


def axial_rope_tables(rows):
    row = jnp.repeat(jnp.arange(rows), GRID_W).astype(jnp.float32)
    col = jnp.tile(jnp.arange(GRID_W), rows).astype(jnp.float32)
    half = HEAD_DIM // 2
    inv_freq = jnp.power(ROPE_THETA, -jnp.arange(0, half, 2, dtype=jnp.float32) / half)
    ang = jnp.stack([row[:, None] * inv_freq, col[:, None] * inv_freq], 0)
    return jnp.cos(ang), jnp.sin(ang)


def apply_axial_rope(t, cos, sin):
    tf = t.astype(jnp.float32)
    segs = tf.reshape(tf.shape[:-1] + (2, 2, HEAD_DIM // 4))
    x1, x2 = segs[..., 0, :], segs[..., 1, :]
    cb = cos.transpose(1, 0, 2)[None, :, None]
    sb = sin.transpose(1, 0, 2)[None, :, None]
    out = jnp.stack([x1 * cb - x2 * sb, x2 * cb + x1 * sb], axis=-2)
    return out.reshape(t.shape).astype(t.dtype)


def block_attention(q, k, v):
    B, N, Hq, Dh = q.shape
    nb = N // Q_BLOCK
    qb = q.reshape(B, nb, Q_BLOCK, A_KV_HEADS, A_GROUP, Dh).transpose(1, 0, 2, 3, 4, 5)
    scale = Dh ** -0.5

    def one_block(qblk):
        s = jnp.einsum('bqhgd,bkhd->bhgqk', qblk, k).astype(jnp.float32) * scale
        p = jax.nn.softmax(s, axis=-1).astype(v.dtype)
        return jnp.einsum('bhgqk,bkhd->bqhgd', p, v)

    o = lax.map(one_block, qb)
    return o.transpose(1, 0, 2, 3, 4, 5).reshape(B, N, Hq * Dh)


def multiscale_pool(u, pool_w, pool_scale):
    B, N, _ = u.shape
    uf = u.astype(jnp.float32).reshape(B, N, len(POOL_WINDOWS), POOL_GROUP)
    csum = jnp.concatenate([jnp.zeros_like(uf[:, :1]), jnp.cumsum(uf, axis=1)], axis=1)
    t = jnp.arange(N)
    means = []
    for gi, w in enumerate(POOL_WINDOWS):
        lo = jnp.clip(t - w // 2, 0, N)
        hi = jnp.clip(t + w - w // 2, 0, N)
        cg = csum[:, :, gi]
        cnt = (hi - lo).astype(jnp.float32)[None, :, None]
        means.append((cg[:, hi] - cg[:, lo]) / cnt)
    pooled = (jnp.stack(means, axis=2) - uf).astype(u.dtype)
    y = jnp.einsum('bngc,gcd->bngd', pooled, pool_w).reshape(B, N, POOL_DIM)
    return y * pool_scale


def mix_attn_pool(xn, hn, cos, sin, w_in, q_g, k_g, pool_w, pool_scale, w_out, ctx_out):
    def project(t):
        p = t @ w_in
        B, N, _ = p.shape
        q = p[..., :A_Q_DIM].reshape(B, N, A_HEADS, HEAD_DIM)
        k = p[..., A_Q_DIM:A_Q_DIM + A_KV_DIM].reshape(B, N, A_KV_HEADS, HEAD_DIM)
        v = p[..., A_Q_DIM + A_KV_DIM:A_Q_DIM + 2 * A_KV_DIM].reshape(B, N, A_KV_HEADS, HEAD_DIM)
        u = p[..., A_Q_DIM + 2 * A_KV_DIM:]
        return rms_norm(q, q_g), rms_norm(k, k_g), v, u

    qx, kx, vx, ux = project(xn)
    qh, kh, vh, uh = project(hn)
    qx = apply_axial_rope(qx, cos, sin)
    kx = apply_axial_rope(kx, cos, sin)
    k_all = jnp.concatenate([kh, kx], axis=1)
    v_all = jnp.concatenate([vh, vx], axis=1)
    yx = jnp.concatenate([block_attention(qx, k_all, v_all), multiscale_pool(ux, pool_w, pool_scale)], axis=-1) @ w_out
    yh = None
    if ctx_out:
        yh = jnp.concatenate([block_attention(qh, kh, vh), multiscale_pool(uh, pool_w, pool_scale)], axis=-1) @ w_out
    return yx, yh


def short_conv(t, w):
    return lax.conv_general_dilated(t, w[:, None, :], window_strides=(1,),
                                    padding=((CONV_W // 2, CONV_W // 2),),
                                    dimension_numbers=('NWC', 'WIO', 'NWC'),
                                    feature_group_count=t.shape[-1])


def gated_delta_chunked(q, k, v, g, beta, s0, need_out):
    B, N, H, Dk = q.shape
    Dv = v.shape[-1]
    nc = N // CHUNK

    def chunked(t):
        t = t.reshape((B, nc, CHUNK, H) + t.shape[3:])
        return jnp.moveaxis(t, 3, 2)

    q, k, v, g, beta = (chunked(t) for t in (q, k, v, g, beta))
    gc = jnp.cumsum(g, axis=-1)
    idx = jnp.arange(CHUNK)
    incl = idx[:, None] >= idx[None, :]
    strict = idx[:, None] > idx[None, :]
    decay = jnp.exp(jnp.where(incl, gc[..., :, None] - gc[..., None, :], -jnp.inf))
    kb = k * beta[..., None]
    a_mat = jnp.where(strict, jnp.einsum('bnhid,bnhjd->bnhij', kb, k) * decay, 0.0) + jnp.eye(CHUNK, dtype=jnp.float32)
    rhs = jnp.concatenate([v * beta[..., None], kb * jnp.exp(gc)[..., None]], axis=-1)
    sol = lax.linalg.triangular_solve(a_mat, rhs, left_side=True, lower=True, unit_diagonal=True)
    u, w = sol[..., :Dv], sol[..., Dv:]
    k_tail = k * jnp.exp(gc[..., -1:] - gc)[..., None]
    g_tot = jnp.exp(gc[..., -1])
    xs = [u, w, k_tail, g_tot]
    if need_out:
        qk = jnp.where(incl, jnp.einsum('bnhid,bnhjd->bnhij', q, k) * decay, 0.0)
        xs += [q * jnp.exp(gc)[..., None], qk]
    xs = tuple(jnp.moveaxis(t, 1, 0) for t in xs)

    def step(state, inp):
        u_c, w_c, kt_c, gt_c = inp[:4]
        v_new = u_c - jnp.einsum('bhcd,bhde->bhce', w_c, state)
        new_state = state * gt_c[..., None, None] + jnp.einsum('bhcd,bhce->bhde', kt_c, v_new)
        if need_out:
            qd_c, qk_c = inp[4:]
            o = jnp.einsum('bhcd,bhde->bhce', qd_c, state) + jnp.einsum('bhij,bhje->bhie', qk_c, v_new)
            return new_state, o
        return new_state, None

    s_fin, o = lax.scan(step, s0, xs)
    if need_out:
        o = jnp.moveaxis(jnp.moveaxis(o, 0, 1), 2, 3).reshape(B, N, H, Dv)
    return o, s_fin


def mix_gated_delta(xn, hn, w_in, conv_w, a_log, dt_bias, o_g, w_out, ctx_out):
    def prep(t):
        p = t @ w_in
        B, N, _ = p.shape
        qkv = jax.nn.silu(short_conv(p[..., :3 * C_DIM], conv_w)).astype(jnp.float32)
        qkv = qkv.reshape(B, N, 3, C_HEADS, HEAD_DIM)
        q = l2_norm(qkv[:, :, 0]) * (HEAD_DIM ** -0.5)
        k = l2_norm(qkv[:, :, 1])
        v = qkv[:, :, 2]
        z = p[..., 3 * C_DIM:4 * C_DIM].reshape(B, N, C_HEADS, HEAD_DIM)
        ab = p[..., 4 * C_DIM:].astype(jnp.float32).reshape(B, N, 2, 2, C_HEADS)
        g = -jnp.exp(a_log.astype(jnp.float32)) * jax.nn.softplus(ab[:, :, :, 0] + dt_bias.astype(jnp.float32))
        beta = jax.nn.sigmoid(ab[:, :, :, 1])
        return q, k, v, z, g, beta

    qx, kx, vx, zx, gx, bx = prep(xn)
    qh, kh, vh, zh, gh, bh = prep(hn)
    s0 = jnp.zeros((xn.shape[0], C_HEADS, HEAD_DIM, HEAD_DIM), jnp.float32)
    outs_x, outs_h = [], []
    for d in range(2):
        flip = (lambda t: jnp.flip(t, axis=1)) if d == 1 else (lambda t: t)
        o_h, s_h = gated_delta_chunked(flip(qh), flip(kh), flip(vh), flip(gh[:, :, d]), flip(bh[:, :, d]), s0, ctx_out)
        o_x, _ = gated_delta_chunked(flip(qx), flip(kx), flip(vx), flip(gx[:, :, d]), flip(bx[:, :, d]), s_h, True)
        outs_x.append(flip(o_x))
        if ctx_out:
            outs_h.append(flip(o_h))

    def finish(o, z, ref):
        y = rms_norm(o, o_g) * jax.nn.silu(z.astype(jnp.float32))
        return y.reshape(ref.shape[0], ref.shape[1], C_DIM).astype(ref.dtype) @ w_out

    yx = finish(outs_x[0] + outs_x[1], zx, xn)
    yh = finish(outs_h[0] + outs_h[1], zh, hn) if ctx_out else None
    return yx, yh


def setup_inputs(seed: int = 0) -> dict:
    key = jax.random.key(seed)
    ks = jax.random.split(key, 24)
    D, F = D_MODEL, D_FF

    def nrm(k, shape, s):
        return jax.random.normal(k, shape, jnp.float32) * s

    dt = jnp.exp(jax.random.uniform(ks[19], (N_ODD, 2, C_HEADS), jnp.float32,
                                    minval=math.log(1e-3), maxval=math.log(1e-1)))
    return {
        'x': nrm(ks[0], (BATCH, SEQ, D), 1.0),
        'c': nrm(ks[1], (BATCH, D), 1.0),
        'ctx': nrm(ks[2], (BATCH, CTX_LEN, D), 1.0),
        'c_ctx': nrm(ks[3], (D,), 1.0),
        'w_mod': nrm(ks[4], (DEPTH, D, N_MOD * D), 0.5 * D ** -0.5),
        'b_mod': nrm(ks[5], (DEPTH, N_MOD * D), 0.01),
        'norm_g': 1.0 + nrm(ks[6], (DEPTH, 3, D), 0.02),
        'ffn_wg': nrm(ks[7], (DEPTH, 2, D, F), D ** -0.5),
        'ffn_wu': nrm(ks[8], (DEPTH, 2, D, F), D ** -0.5),
        'ffn_wd': nrm(ks[9], (DEPTH, 2, F, D), F ** -0.5),
        'ab_w_in': nrm(ks[10], (N_EVEN, D, AB_IN), D ** -0.5),
        'ab_q_norm': 1.0 + nrm(ks[11], (N_EVEN, HEAD_DIM), 0.02),
        'ab_k_norm': 1.0 + nrm(ks[12], (N_EVEN, HEAD_DIM), 0.02),
        'pool_w': nrm(ks[13], (N_EVEN, len(POOL_WINDOWS), POOL_GROUP, POOL_GROUP), POOL_GROUP ** -0.5),
        'pool_scale': 1.0 + nrm(ks[14], (N_EVEN, POOL_DIM), 0.1),
        'ab_w_out': nrm(ks[15], (N_EVEN, AB_OUT, D), AB_OUT ** -0.5),
        'gdn_w_in': nrm(ks[16], (N_ODD, D, C_IN), D ** -0.5),
        'gdn_conv_w': nrm(ks[17], (N_ODD, CONV_W, 3 * C_DIM), CONV_W ** -0.5),
        'gdn_a_log': jnp.log(jax.random.uniform(ks[18], (N_ODD, 2, C_HEADS), jnp.float32, minval=1.0, maxval=16.0)),
        'gdn_dt_bias': dt + jnp.log(-jnp.expm1(-dt)),
        'gdn_o_norm': 1.0 + nrm(ks[20], (N_ODD, HEAD_DIM), 0.02),
        'gdn_w_out': nrm(ks[21], (N_ODD, C_DIM, D), C_DIM ** -0.5),
    }


def reference(x, c, ctx, c_ctx, w_mod, b_mod, norm_g, ffn_wg, ffn_wu, ffn_wd,
              ab_w_in, ab_q_norm, ab_k_norm, pool_w, pool_scale, ab_w_out,
              gdn_w_in, gdn_conv_w, gdn_a_log, gdn_dt_bias, gdn_o_norm, gdn_w_out):
    rows = x.shape[1] // GRID_W
    cos, sin = axial_rope_tables(rows)
    h = ctx
    for i in range(DEPTH):
        last = i == DEPTH - 1
        mx = adaln(c, w_mod[i], b_mod[i])
        mh = adaln(c_ctx[None], w_mod[i], b_mod[i])
        x = x + 0.5 * mx[:, :, 2] * swiglu(modulate(x, norm_g[i, 0], mx, 0), ffn_wg[i, 0], ffn_wu[i, 0], ffn_wd[i, 0])
        h = h + 0.5 * mh[:, :, 2] * swiglu(modulate(h, norm_g[i, 0], mh, 0), ffn_wg[i, 0], ffn_wu[i, 0], ffn_wd[i, 0])
        xn = modulate(x, norm_g[i, 1], mx, 1)
        hn = modulate(h, norm_g[i, 1], mh, 1)
        j = i // 2
        if i % 2 == 0:
            yx, yh = mix_attn_pool(xn, hn, cos, sin, ab_w_in[j], ab_q_norm[j], ab_k_norm[j],
                                   pool_w[j], pool_scale[j], ab_w_out[j], not last)
        else:
            yx, yh = mix_gated_delta(xn, hn, gdn_w_in[j], gdn_conv_w[j], gdn_a_log[j], gdn_dt_bias[j],
                                     gdn_o_norm[j], gdn_w_out[j], not last)
        x = x + mx[:, :, 5] * yx
        x = x + 0.5 * mx[:, :, 8] * swiglu(modulate(x, norm_g[i, 2], mx, 2), ffn_wg[i, 1], ffn_wu[i, 1], ffn_wd[i, 1])
        if not last:
            h = h + mh[:, :, 5] * yh
            h = h + 0.5 * mh[:, :, 8] * swiglu(modulate(h, norm_g[i, 2], mh, 2), ffn_wg[i, 1], ffn_wu[i, 1], ffn_wd[i, 1])
    return x
```

```python
import contextlib
import numpy as np
import concourse.bass as bass
import concourse.mybir as mybir
from concourse.bass_utils import run_bass_kernel_spmd

F32 = mybir.dt.float32
BF16 = mybir.dt.bfloat16
F32R = mybir.dt.float32r
AF = mybir.ActivationFunctionType
ALU = mybir.AluOpType

D = 1024
SEQ = 2048
CTX = 256
DFF = 2816
NF = DFF // 128
EPS = 1e-6
NCORES = 8
DBG = {"LN": 1, "sepbank": 1, "f32r": 1}


class T:
    __slots__ = ("ap", "w", "r", "dsem", "dcnt", "name")

    def __init__(self, ap, name=""):
        self.ap = ap
        self.w = None
        self.r = {}
        self.dsem = None
        self.dcnt = 0
        self.name = name

    def __getitem__(self, idx):
        return self.ap[idx]


class Eng:
    def __init__(self, name, obj, semid):
        self.name = name
        self.obj = obj
        self.semid = semid
        self.cnt = 0
        self.seen = {}


class Emitter:
    def __init__(self, nc, es):
        self.nc = nc
        self.es = es
        self.semh = []
        self.E = {}
        for name, obj in (("pe", nc.tensor), ("act", nc.scalar), ("dve", nc.vector),
                          ("pool", nc.gpsimd), ("sp", nc.sync)):
            self.E[name] = Eng(name, obj, self.new_sem("e_" + name))
        self.n_inst = 0
        self.dma_cnt = {}
        self.free_dsems = {}
        self.sem_kind = {}
        self.n_dsem = 0

    def new_sem(self, name):
        h = self.es.enter_context(self.nc.semaphore(name))
        self.semh.append(h)
        return len(self.semh) - 1

    def _deps(self, reads, writes):
        deps = {}
        for t in reads:
            if t.w is not None:
                s, v = t.w
                if deps.get(s, 0) < v:
                    deps[s] = v
        for t in writes:
            if t.w is not None:
                s, v = t.w
                if deps.get(s, 0) < v:
                    deps[s] = v
            for s, v in t.r.items():
                if deps.get(s, 0) < v:
                    deps[s] = v
        return deps

    def _wait(self, E, deps, skip_self=False):
        for s, v in deps.items():
            if skip_self and s == E.semid:
                continue
            if E.seen.get(s, 0) >= v:
                continue
            E.obj.wait_ge(self.semh[s], v)
            E.seen[s] = v

    def op(self, eng, fn, reads=(), writes=(), inc=True):
        E = self.E[eng]
        deps = self._deps(reads, writes)
        self._wait(E, deps, skip_self=(eng == "pe"))
        inst = fn(E.obj)
        val = E.cnt + 1
        if inc:
            inst.then_inc(self.semh[E.semid], 1)
            E.cnt = val
        for t in reads:
            if t.r.get(E.semid, 0) < val:
                t.r[E.semid] = val
        for t in writes:
            t.w = (E.semid, val)
            t.r = {}
        self.n_inst += 1
        return inst

    def dma(self, q, out, in_, reads, writes, semt):
        E = self.E[q]
        deps = self._deps(reads, writes)
        self._wait(E, deps)
        if semt.dsem is None:
            kind = "sw" if q == "pool" else "hw"
            fl = self.free_dsems.setdefault(kind, [])
            if fl:
                semt.dsem = fl.pop()
                semt.dcnt = self.dma_cnt[semt.dsem]
            else:
                self.n_dsem += 1
                semt.dsem = self.new_sem("dq%s%d" % (kind, self.n_dsem))
            self.sem_kind[semt.dsem] = kind
        semt.dcnt += 16
        self.dma_cnt[semt.dsem] = semt.dcnt
        E.obj.dma_start(out=out, in_=in_).then_inc(self.semh[semt.dsem], 16)
        for t in reads:
            if t.r.get(semt.dsem, 0) < semt.dcnt:
                t.r[semt.dsem] = semt.dcnt
        for t in writes:
            t.w = (semt.dsem, semt.dcnt)
            t.r = {}
        self.n_inst += 1

    def release(self, t):
        if t.dsem is not None:
            self.free_dsems.setdefault(self.sem_kind[t.dsem], []).append(t.dsem)
            t.dsem = None

    def barrier(self):
        deps = {}
        for e in self.E.values():
            if e.cnt:
                deps[e.semid] = e.cnt
        for sid, c in self.dma_cnt.items():
            if c:
                deps[sid] = c
        for e in self.E.values():
            self._wait(e, deps)


def interleave(gens):
    gens = list(gens)
    while gens:
        alive = []
        for g in gens:
            try:
                next(g)
                alive.append(g)
            except StopIteration:
                pass
        gens = alive


class Ring:
    def __init__(self, tiles):
        self.tiles = tiles
        self.i = 0

    def next(self):
        t = self.tiles[self.i % len(self.tiles)]
        self.i += 1
        return t


class Builder:
    def __init__(self, NB, stop_after=None, dbg_h=False):
        self.NB = NB
        self.stop_after = stop_after
        self.dbg_h = dbg_h
        self.nc = bass.Bass("TRN2", target_bir_lowering=False)
        self.es = contextlib.ExitStack()

    def dram_in(self, name, shape, dt=F32):
        return self.nc.dram_tensor(name, list(shape), dt, kind="ExternalInput").ap()

    def sb(self, st, name, shape, dt):
        self._uid = getattr(self, "_uid", 0) + 1
        t = st.enter_context(self.nc.sbuf_tensor("s%d_%s" % (self._uid, name), list(shape), dt))
        tt = T(t, name)
        if st is not self.es:
            st.callback(self.em.release, tt)
        return tt

    def ps(self, st, name, shape, dt=F32):
        self._uid = getattr(self, "_uid", 0) + 1
        t = st.enter_context(self.nc.psum_tensor("p%d_%s" % (self._uid, name), list(shape), dt))
        return T(t, name)

    def ring(self, st, name, n, shape, dt, psum=False):
        mk = self.ps if psum else self.sb
        return Ring([mk(st, f"{name}{i}", shape, dt) for i in range(n)])

    def build(self):
        nc = self.nc
        NB = self.NB
        R = NB + 1
        with self.es as es:
            em = self.em = Emitter(nc, es)
            x_d = self.dram_in("x", [NB, SEQ, D])
            ctx_d = self.dram_in("ctx", [NB, CTX, D])
            c_d = self.dram_in("c", [R, D])
            wmod_d = self.dram_in("w_mod", [2, D, 9 * D])
            bmod_d = self.dram_in("b_mod", [2, 9 * D])
            normg_d = self.dram_in("norm_g", [2, 3, D])
            wg_d = self.dram_in("ffn_wg", [2, 2, D, DFF])
            wu_d = self.dram_in("ffn_wu", [2, 2, D, DFF])
            wd_d = self.dram_in("ffn_wd", [2, 2, DFF, D])
            ident_d = self.dram_in("ident", [128, 128])
            self.dA = dict(w_in=self.dram_in("ab_w_in", [D, 1536]), qn=self.dram_in("ab_q_norm", [1, 128]),
                           kn=self.dram_in("ab_k_norm", [1, 128]), pw=self.dram_in("pool_w", [4, 128, 128]),
                           psc=self.dram_in("pool_scale", [1, 512]), w_out=self.dram_in("ab_w_out", [D, D]),
                           ropeC=self.dram_in("ropeC", [128, SEQ]), ropeS=self.dram_in("ropeS", [128, SEQ]),
                           rotm=self.dram_in("rotm", [128, 128]), edge=self.dram_in("pool_edge", [128, 4, 32]))
            self.dG = dict(w_in=self.dram_in("gdn_w_in", [D, 4128]), conv=self.dram_in("gdn_conv_w", [1, 3 * 3072]),
                           alog=self.dram_in("alog_rep", [1, 576]), dtb=self.dram_in("dtb_rep", [1, 576]),
                           og=self.dram_in("gdn_o_norm", [1, 128]), w_out=self.dram_in("gdn_w_out", [D, D]),
                           cum=self.dram_in("g_cum", [2, 64, 64]), mpos=self.dram_in("g_mpos", [2, 64, 64]),
                           mneg=self.dram_in("g_mneg", [2, 64, 64]))
            self.YC_d = nc.dram_tensor("YC", [8, 128, SEQ], BF16, kind="Internal").ap()
            self.ident_d = ident_d
            y_d = nc.dram_tensor("y", [NB, SEQ, D], F32, kind="ExternalOutput").ap()
            if self.dbg_h:
                yh_d = nc.dram_tensor("yh", [NB, CTX, D], F32, kind="ExternalOutput").ap()
            XT_d = nc.dram_tensor("XT", [NB, 8, 128, SEQ], F32, kind="Internal").ap()
            HT_d = nc.dram_tensor("HT", [NB, 8, 128, CTX], F32, kind="Internal").ap()
            self.d = dict(x=x_d, ctx=ctx_d, XT=XT_d, HT=HT_d, y=y_d)
            self.XTt = [[T(None, f"XT{b}_{t}") for t in range(4)] for b in range(NB)]
            self.HTt = [T(None, f"HT{b}") for b in range(NB)]
            self.YCt = T(None, "YCt")

            ident = self.sb(es, "ident", [128, 128], F32)
            ones_bf = self.sb(es, "ones_bf", [128, 128], BF16)
            ones_f = self.sb(es, "ones_f", [1, 128], F32)
            modT = self.sb(es, "modT", [128, 2, 72, R], F32)
            modA = self.sb(es, "modA", [128, 2, 3, 8, R], F32)
            modG = self.sb(es, "modG", [128, 2, 3, 8, R], F32)
            gT = self.sb(es, "gT", [128, 2, 3, 8], F32)
            self.ident, self.ones_bf, self.ones_f = ident, ones_bf, ones_f
            self.modT, self.modA, self.modG, self.gT = modT, modA, modG, gT

            em.dma("sp", ident[:], ident_d[:, :], [], [ident], ident)
            em.op("pool", lambda e: e.memset(ones_bf[:], 1.0), [], [ones_bf])
            em.op("pool", lambda e: e.memset(ones_f[:], 1.0), [], [ones_f])

            self.wsem = [T(None, f"ws{i}") for i in range(40)]
            self.eps_t = self.sb(es, "eps_t", [128, 1], F32)
            em.op("pool", lambda e: e.memset(self.eps_t[:], EPS), [], [self.eps_t])

            self.phase_adaln(c_d, wmod_d, bmod_d, normg_d)
            self.phase_tin()
            stages = []
            for i in range(2):
                stages.append(("f1", i))
                stages.append(("mix", i))
                stages.append(("f2", i))
            for (kind, i) in stages:
                if kind == "f1":
                    self.phase_ffn(i, 0, wg_d[i, 0], wu_d[i, 0], wd_d[i, 0], do_h=True)
                elif kind == "f2":
                    self.phase_ffn(i, 2, wg_d[i, 1], wu_d[i, 1], wd_d[i, 1], do_h=(i == 0))
                elif i == 0:
                    self.phase_mix_a()
                else:
                    self.phase_gdn()
                if self.stop_after == (kind, i):
                    break
            self.phase_tout(y_d, yh_d if self.dbg_h else None)
            em.barrier()
        return nc

    def phase_adaln(self, c_d, wmod_d, bmod_d, normg_d):
        em, nc, NB = self.em, self.nc, self.NB
        R = NB + 1
        with contextlib.ExitStack() as st:
            c_sb = self.sb(st, "c_sb", [R, D], F32)
            cs_sb = self.sb(st, "cs_sb", [R, D], F32)
            cT = self.sb(st, "cT", [128, 8, R], F32)
            bmr = self.ring(st, "bm_row", 2, [1, 1152], F32)
            gr = self.sb(st, "g_row", [1, 6 * D], F32)
            wring = self.ring(st, "wm", 2, [128, 8, 1152], F32)
            pring = self.ring(st, "ps_ad", 2, [128, 512], F32, psum=True)
            em.dma("sp", c_sb[:], c_d[:, :], [], [c_sb], c_sb)
            em.dma("sp", gr[:], normg_d.rearrange("(o l) s n -> o (l s n)", o=1), [], [gr], gr)
            em.op("act", lambda e: e.activation(out=cs_sb[:], in_=c_sb[:], func=AF.Silu), [c_sb], [cs_sb])
            pt = pring.next()
            for k in range(8):
                em.op("pe", lambda e, k=k: e.transpose(out=pt[:, k * R:(k + 1) * R], in_=cs_sb[:, k * 128:(k + 1) * 128],
                                                       identity=self.ident[0:R, 0:R]),
                      [cs_sb, self.ident], [pt], inc=(k == 7))
            em.op("dve", lambda e: e.tensor_copy(out=cT[:].rearrange("p k r -> p (k r)"), in_=pt[:, 0:8 * R]), [pt], [cT])
            pt = pring.next()
            for j in range(48):
                em.op("pe", lambda e, j=j: e.matmul(out=pt[:, j:j + 1], lhsT=gr[0:1, j * 128:(j + 1) * 128],
                                                   rhs=self.ones_f[0:1, 0:1], start=True, stop=True),
                      [gr, self.ones_f], [pt], inc=(j == 47))
            em.op("dve", lambda e: e.tensor_copy(out=self.gT[:].rearrange("p l s c -> p (l s c)"), in_=pt[:, 0:48]),
                  [pt], [self.gT])
            for l in range(2):
                for mg in range(8):
                    wt = wring.next()
                    em.dma("sp", wt[:], wmod_d[l, :, mg * 1152:(mg + 1) * 1152].rearrange("(k p) n -> p k n", p=128),
                           [], [wt], wt)
                    bm = bmr.next()
                    em.dma("sp", bm[:], bmod_d[l:l + 1, mg * 1152:(mg + 1) * 1152], [], [bm], bm)
                    pt = pring.next()
                    for mm in range(9):
                        m = mg * 9 + mm
                        o = pt[:, mm * R:(mm + 1) * R]
                        for k in range(8):
                            em.op("pe", lambda e, k=k, mm=mm, o=o: e.matmul(out=o, lhsT=wt[:, k, mm * 128:(mm + 1) * 128],
                                                                         rhs=cT[:, k, :], start=(k == 0), stop=False),
                                  [wt, cT], [pt], inc=False)
                        em.op("pe", lambda e, m=m, o=o, l=l: e.matmul(out=o, lhsT=bm[0:1, mm * 128:(mm + 1) * 128],
                                                                   rhs=self.ones_f[0:1, 0:R], start=False, stop=True),
                              [bm, self.ones_f], [pt], inc=(mm == 8))
                    em.op("dve", lambda e, l=l, mg=mg: e.tensor_copy(
                        out=self.modT[:, l, mg * 9:(mg + 1) * 9, :].rearrange("p m r -> p (m r)"), in_=pt[:, 0:9 * R]),
                        [pt], [self.modT])
            for l in range(2):
                for s in range(3):
                    for c in range(8):
                        em.op("dve", lambda e, l=l, s=s, c=c: e.tensor_scalar(
                            out=self.modA[:, l, s, c, :], in0=self.modT[:, l, (3 * s + 1) * 8 + c, :],
                            scalar1=1.0, scalar2=self.gT[:, l, s, c:c + 1], op0=ALU.add, op1=ALU.mult),
                            [self.modT, self.gT], [self.modA])
                        gs = 1.0 if s == 1 else 0.5
                        em.op("dve", lambda e, l=l, s=s, c=c, gs=gs: e.tensor_scalar(
                            out=self.modG[:, l, s, c, :], in0=self.modT[:, l, (3 * s + 2) * 8 + c, :],
                            scalar1=gs, scalar2=None, op0=ALU.mult),
                            [self.modT], [self.modG])
            em.barrier()

    def modB(self, l, s, c, r):
        return self.modT[:, l, (3 * s) * 8 + c, r:r + 1]

    def tiles(self, do_h=True):
        out = []
        for b in range(self.NB):
            if do_h:
                out.append(("h", b, 0, CTX, self.d["HT"][b].rearrange("c p n -> p c n"), self.HTt[b], self.NB))
            for t in range(4):
                out.append(("x", b, t, 512, self.d["XT"][b, :, :, t * 512:(t + 1) * 512].rearrange("c p n -> p c n"),
                            self.XTt[b][t], b))
        return out

    def phase_tin(self):
        em = self.em
        with contextlib.ExitStack() as st:
            inr = self.ring(st, "tin_in", 2, [128, 4, D], F32)
            outr = self.ring(st, "tin_out", 2, [128, 8, 512], F32)
            pr = self.ring(st, "ps_tin", 4, [128, 512], F32, psum=True)
            for (kind, b, t, n, dap, trk, r) in self.tiles(True):
                nsub = n // 128
                it = inr.next()
                src = self.d["x"][b, t * 512:(t + 1) * 512, :] if kind == "x" else self.d["ctx"][b]
                em.dma("sp", it[:, 0:nsub, :], src.rearrange("(j p) d -> p j d", p=128), [], [it], it)
                ot = outr.next()
                for c in range(8):
                    pt = pr.next()
                    for j in range(nsub):
                        em.op("pe", lambda e, c=c, j=j, pt=pt, it=it: e.transpose(
                            out=pt[:, j * 128:(j + 1) * 128], in_=it[:, j, c * 128:(c + 1) * 128], identity=self.ident[:]),
                            [it, self.ident], [pt], inc=(j == nsub - 1))
                    eng = "dve" if c % 2 == 0 else "act"
                    if eng == "dve":
                        em.op("dve", lambda e, c=c, pt=pt, ot=ot: e.tensor_copy(out=ot[:, c, 0:n], in_=pt[:, 0:n]), [pt], [ot])
                    else:
                        em.op("act", lambda e, c=c, pt=pt, ot=ot: e.copy(out=ot[:, c, 0:n], in_=pt[:, 0:n]), [pt], [ot])
                em.dma("sp", dap, ot[:, :, 0:n], [ot], [trk], ot)
            em.barrier()

    def phase_tout(self, y_d, yh_d):
        em = self.em
        with contextlib.ExitStack() as st:
            inr = self.ring(st, "to_in", 2, [128, 8, 512], F32)
            outr = self.ring(st, "to_out", 2, [128, 4, D], F32)
            pr = self.ring(st, "ps_to", 4, [128, 512], F32, psum=True)
            for (kind, b, t, n, dap, trk, r) in self.tiles(yh_d is not None):
                nsub = n // 128
                it = inr.next()
                em.dma("sp", it[:, :, 0:n], dap, [trk], [it], it)
                ot = outr.next()
                for j in range(nsub):
                    for hh in range(2):
                        pt = pr.next()
                        for cc in range(4):
                            c = hh * 4 + cc
                            em.op("pe", lambda e, c=c, cc=cc, j=j, pt=pt, it=it: e.transpose(
                                out=pt[:, cc * 128:(cc + 1) * 128], in_=it[:, c, j * 128:(j + 1) * 128], identity=self.ident[:]),
                                [it, self.ident], [pt], inc=(cc == 3))
                        if hh == 0:
                            em.op("dve", lambda e, j=j, pt=pt, ot=ot: e.tensor_copy(out=ot[:, j, 0:512], in_=pt[:]), [pt], [ot])
                        else:
                            em.op("act", lambda e, j=j, pt=pt, ot=ot: e.copy(out=ot[:, j, 512:1024], in_=pt[:]), [pt], [ot])
                dst = y_d[b, t * 512:(t + 1) * 512, :] if kind == "x" else yh_d[b]
                em.dma("sp", dst.rearrange("(j p) d -> p j d", p=128), ot[:, 0:nsub, :], [ot], [], ot)
            em.barrier()

    def norm_mod(self, st_rings, xt, n, l, s, r, xn):
        em = self.em
        sqr, ssr, sdr, rsr, tmr = st_rings
        ss = ssr.next()
        for c in range(8):
            sq = sqr.next()
            em.op("act", lambda e, c=c, sq=sq: e.activation(out=sq[:, 0:n], in_=xt[:, c, 0:n], func=AF.Square), [xt], [sq])
            em.op("pe", lambda e, c=c, sq=sq: e.matmul(out=ss[:, 0:n], lhsT=self.ones_bf[:], rhs=sq[:, 0:n],
                                                     start=(c == 0), stop=(c == 7)),
                  [sq, self.ones_bf], [ss])
        sd = sdr.next()
        em.op("act", lambda e: e.activation(out=sd[:, 0:n], in_=ss[:, 0:n], func=AF.Sqrt, scale=1.0 / D, bias=self.eps_t[:, 0:1]),
              [ss, self.eps_t], [sd])
        rs = rsr.next()
        em.op("dve", lambda e: e.reciprocal(out=rs[:, 0:n], in_=sd[:, 0:n]), [sd], [rs])
        for c in range(8):
            tm = tmr.next()
            em.op("dve", lambda e, c=c, tm=tm: e.scalar_tensor_tensor(
                out=tm[:, 0:n], in0=xt[:, c, 0:n], scalar=self.modA[:, l, s, c, r:r + 1], in1=rs[:, 0:n],
                op0=ALU.mult, op1=ALU.mult), [xt, self.modA, rs], [tm])
            em.op("act", lambda e, c=c, tm=tm: e.activation(
                out=xn[:, c, 0:n], in_=tm[:, 0:n], func=AF.Identity, bias=self.modB(l, s, c, r), scale=1.0),
                [tm, self.modT], [xn])

    def phase_ffn(self, l, s, wg, wu, wd, do_h):
        em = self.em
        with contextlib.ExitStack() as st:
            AR = st.enter_context(self.nc.sbuf_tensor("arena_f%d%d" % (l, s), [128, 66 * 1024], BF16))
            WG = [T(AR[:, k * DFF:(k + 1) * DFF], f"wg{k}") for k in range(8)]
            WU = [T(AR[:, 22528 + k * DFF: 22528 + (k + 1) * DFF], f"wu{k}") for k in range(8)]
            WD = [T(AR[:, 45056 + f * D: 45056 + (f + 1) * D], f"wd{f}") for f in range(NF)]
            for k in range(8):
                em.dma("pool", WG[k][:], wg[k * 128:(k + 1) * 128, :], [], [WG[k]], self.wsem[k])
                em.dma("pool", WU[k][:], wu[k * 128:(k + 1) * 128, :], [], [WU[k]], self.wsem[8 + k])
            for f in range(NF):
                em.dma("pool", WD[f][:], wd[f * 128:(f + 1) * 128, :], [], [WD[f]], self.wsem[16 + f])
            xtr = self.ring(st, "f_xt", 1, [128, 8, 512], F32)
            xrr = self.ring(st, "f_xr", 4, [128, 512], F32)
            xnr = self.ring(st, "f_xn", 1, [128, 8, 512], BF16)
            hid = [self.sb(st, f"f_hid{f}", [128, 512], BF16) for f in range(NF)]
            rings = (self.ring(st, "f_sq", 2, [128, 512], BF16),
                     self.ring(st, "ps_ss", 2, [128, 512], F32, psum=True),
                     self.ring(st, "f_sd", 1, [128, 512], F32),
                     self.ring(st, "f_rs", 1, [128, 512], F32),
                     self.ring(st, "f_tm", 2, [128, 512], F32))
            sgr = self.ring(st, "f_sg", 2, [128, 512], F32)
            pg = self.ring(st, "ps_g", 2, [128, 512], F32, psum=True)
            pu = self.ring(st, "ps_u", 2, [128, 512], F32, psum=True)
            py = self.ring(st, "ps_y", 2, [128, 512], F32, psum=True)
            tls = self.tiles(do_h)

            def load_norm(i):
                (kind, b, t, n, dap, trk, r) = tls[i]
                xt = xtr.next()
                em.dma("sp", xt[:, :, 0:n], dap, [trk], [xt], xt)
                xn = xnr.next()
                self.norm_mod(rings, xt, n, l, s, r, xn)
                return xn
            xn_next = load_norm(0)
            for i, (kind, b, t, n, dap, trk, r) in enumerate(tls):
                xn = xn_next
                for f in range(NF):
                    g_ps = pg.next()
                    u_ps = pu.next()
                    for k in range(8):
                        em.op("pe", lambda e, k=k, f=f, g_ps=g_ps: e.matmul(
                            out=g_ps[:, 0:n], lhsT=WG[k][:, f * 128:(f + 1) * 128], rhs=xn[:, k, 0:n],
                            start=(k == 0), stop=(k == 7)), [WG[k], xn], [g_ps], inc=(k == 7))
                    for k in range(8):
                        em.op("pe", lambda e, k=k, f=f, u_ps=u_ps: e.matmul(
                            out=u_ps[:, 0:n], lhsT=WU[k][:, f * 128:(f + 1) * 128], rhs=xn[:, k, 0:n],
                            start=(k == 0), stop=(k == 7)), [WU[k], xn], [u_ps], inc=(k == 7))
                    sg = sgr.next()
                    em.op("act", lambda e, sg=sg, g_ps=g_ps: e.activation(out=sg[:, 0:n], in_=g_ps[:, 0:n], func=AF.Silu),
                          [g_ps], [sg])
                    em.op("dve", lambda e, sg=sg, u_ps=u_ps, f=f: e.tensor_tensor(
                        out=hid[f][:, 0:n], in0=u_ps[:, 0:n], in1=sg[:, 0:n], op=ALU.mult), [u_ps, sg], [hid[f]])
                if i + 1 < len(tls):
                    xn_next = load_norm(i + 1)
                for dd in range(8):
                    xr = xrr.next()
                    em.dma("sp", xr[:, 0:n], dap[:, dd, :], [trk], [xr], xr)
                    y_ps = py.next()
                    for f in range(NF):
                        em.op("pe", lambda e, f=f, dd=dd, y_ps=y_ps: e.matmul(
                            out=y_ps[:, 0:n], lhsT=WD[f][:, dd * 128:(dd + 1) * 128], rhs=hid[f][:, 0:n],
                            start=(f == 0), stop=(f == NF - 1)), [WD[f], hid[f]], [y_ps], inc=(f == NF - 1))
                    em.op("dve", lambda e, dd=dd, y_ps=y_ps, xr=xr: e.scalar_tensor_tensor(
                        out=xr[:, 0:n], in0=y_ps[:, 0:n], scalar=self.modG[:, l, s, dd, r:r + 1], in1=xr[:, 0:n],
                        op0=ALU.mult, op1=ALU.add), [y_ps, self.modG, xr], [xr])
                    em.dma("sp", dap[:, dd, :], xr[:, 0:n], [xr], [trk], xr)
            em.barrier()

    def phase_mix_a(self):
        em, nc, NB = self.em, self.nc, self.NB
        dA = self.dA
        l, s = 0, 1
        SC = 128 ** -0.5
        NK = CTX + SEQ
        with contextlib.ExitStack() as so:
            vecA = self.sb(so, "a_vec", [128, 8], F32)
            KT = self.sb(so, "a_KT", [128, 2, NK], BF16)
            Vt = self.sb(so, "a_V", [128, 18, 256], BF16)
            QT = self.sb(so, "a_QT", [128, 4, NK], BF16)
            PL = self.sb(so, "a_PL", [128, 4, NK], BF16)
            rotm = self.sb(so, "a_rotm", [128, 128], BF16)
            edge = self.sb(so, "a_edge", [128, 4, 32], F32)
            em.dma("pool", rotm[:], dA["rotm"][:, :], [], [rotm], rotm)
            em.dma("sp", edge[:], dA["edge"][:, :, :], [], [edge], edge)
            with contextlib.ExitStack() as st:
                row = self.sb(st, "a_row", [1, 768], F32)
                pr = self.ring(st, "ps_av", 1, [128, 512], F32, psum=True)
                em.dma("sp", row[:, 0:128], dA["qn"][:, :], [], [row], row)
                em.dma("sp", row[:, 128:256], dA["kn"][:, :], [], [row], row)
                em.dma("sp", row[:, 256:768], dA["psc"][:, :], [], [row], row)
                pt = pr.next()
                for j in range(6):
                    em.op("pe", lambda e, j=j: e.matmul(out=pt[:, j:j + 1], lhsT=row[0:1, j * 128:(j + 1) * 128],
                                                       rhs=self.ones_f[0:1, 0:1], start=True, stop=True),
                          [row, self.ones_f], [pt], inc=(j == 5))
                em.op("dve", lambda e: e.tensor_copy(out=vecA[:, 0:6], in_=pt[:, 0:6]), [pt], [vecA])
                em.barrier()
            for b in range(NB):
                with contextlib.ExitStack() as st:
                    AR = st.enter_context(nc.sbuf_tensor("arena_a%d" % b, [128, 8 * 1536], BF16))
                    Win = [T(AR[:, k * 1536:(k + 1) * 1536], f"win{k}") for k in range(8)]
                    for k in range(8):
                        em.dma("pool", Win[k][:], dA["w_in"][k * 128:(k + 1) * 128, :], [], [Win[k]], self.wsem[k])
                    Ux = self.sb(st, "a_Ux", [128, 4, SEQ + 32], F32)
                    Uh = self.sb(st, "a_Uh", [128, 4, CTX + 32], F32)
                    Ta = self.sb(st, "a_Ta", [128, SEQ + 32], F32)
                    Tb = self.sb(st, "a_Tb", [128, SEQ + 32], F32)
                    for (U, N) in ((Ux, SEQ), (Uh, CTX)):
                        em.op("pool", lambda e, U=U: e.memset(U[:, :, 0:16], 0.0), [], [U])
                        em.op("pool", lambda e, U=U, N=N: e.memset(U[:, :, 16 + N:32 + N], 0.0), [], [U])
                    xtr = self.ring(st, "a_xt", 1, [128, 8, 512], F32)
                    xnr = self.ring(st, "a_xn", 1, [128, 8, 512], BF16)
                    rings = (self.ring(st, "a_sq", 2, [128, 512], BF16),
                             self.ring(st, "ps_ass", 2, [128, 512], F32, psum=True),
                             self.ring(st, "a_sd", 1, [128, 512], F32),
                             self.ring(st, "a_rs", 1, [128, 512], F32),
                             self.ring(st, "a_tm", 2, [128, 512], F32))
                    pP = self.ring(st, "ps_aP", 2, [128, 512], F32, psum=True)
                    pS = self.ring(st, "ps_aS", 2, [128, 512], F32, psum=True)
                    pV = self.ring(st, "ps_aV", 2, [128, 512], F32, psum=True)
                    sq2 = self.ring(st, "a_sq2", 2, [128, 512], BF16)
                    sd2 = self.ring(st, "a_sd2", 2, [128, 512], F32)
                    rs2 = self.ring(st, "a_rs2", 2, [128, 512], F32)
                    qnr = self.ring(st, "a_qn", 2, [128, 512], BF16)
                    t1r = self.ring(st, "a_t1", 2, [128, 512], F32)
                    t2r = self.ring(st, "a_t2", 2, [128, 512], F32)
                    rcr = self.ring(st, "a_rc", 2, [128, 512], F32)
                    rsr = self.ring(st, "a_rsn", 2, [128, 512], F32)
                    tl = [("h", b, 0, CTX, self.d["HT"][b].rearrange("c p n -> p c n"), self.HTt[b], NB)]
                    for t in range(4):
                        tl.append(("x", b, t, 512, self.d["XT"][b, :, :, t * 512:(t + 1) * 512].rearrange("c p n -> p c n"),
                                   self.XTt[b][t], b))
                    for (kind, _b, t, n, dap, trk, r) in tl:
                        o = 0 if kind == "h" else CTX + t * 512
                        xt = xtr.next()
                        em.dma("sp", xt[:, :, 0:n], dap, [trk], [xt], xt)
                        xn = xnr.next()
                        self.norm_mod(rings, xt, n, l, s, r, xn)
                        if kind == "x":
                            rc = rcr.next()
                            rsn = rsr.next()
                            em.dma("sp", rc[:, 0:n], dA["ropeC"][:, t * 512:(t + 1) * 512], [], [rc], rc)
                            em.dma("sp", rsn[:, 0:n], dA["ropeS"][:, t * 512:(t + 1) * 512], [], [rsn], rsn)
                        for j in range(6):
                            P = pP.next()
                            for k in range(8):
                                em.op("pe", lambda e, k=k, j=j, P=P: e.matmul(
                                    out=P[:, 0:n], lhsT=Win[k][:, j * 128:(j + 1) * 128], rhs=xn[:, k, 0:n],
                                    start=(k == 0), stop=(k == 7)), [Win[k], xn], [P], inc=(k == 7))
                            sq = sq2.next()
                            em.op("act", lambda e, sq=sq, P=P: e.activation(out=sq[:, 0:n], in_=P[:, 0:n], func=AF.Square), [P], [sq])
                            S_ = pS.next()
                            em.op("pe", lambda e, sq=sq, S_=S_: e.matmul(out=S_[:, 0:n], lhsT=self.ones_bf[:], rhs=sq[:, 0:n],
                                                                     start=True, stop=True), [sq, self.ones_bf], [S_])
                            sd = sd2.next()
                            em.op("act", lambda e, sd=sd, S_=S_: e.activation(out=sd[:, 0:n], in_=S_[:, 0:n], func=AF.Sqrt,
                                                                            scale=1.0 / 128, bias=self.eps_t[:, 0:1]),
                                  [S_, self.eps_t], [sd])
                            rs = rs2.next()
                            em.op("dve", lambda e, sd=sd, rs=rs: e.reciprocal(out=rs[:, 0:n], in_=sd[:, 0:n]), [sd], [rs])
                            gcol = vecA[:, 0:1] if j < 4 else vecA[:, 1:2]
                            dstT = QT if j < 4 else KT
                            dst = dstT[:, j if j < 4 else j - 4, o:o + n]
                            if kind == "h":
                                em.op("dve", lambda e, P=P, rs=rs, gcol=gcol, dst=dst: e.scalar_tensor_tensor(
                                    out=dst, in0=P[:, 0:n], scalar=gcol, in1=rs[:, 0:n], op0=ALU.mult, op1=ALU.mult),
                                    [P, vecA, rs], [dstT])
                            else:
                                qn = qnr.next()
                                em.op("dve", lambda e, P=P, rs=rs, gcol=gcol, qn=qn: e.scalar_tensor_tensor(
                                    out=qn[:, 0:n], in0=P[:, 0:n], scalar=gcol, in1=rs[:, 0:n], op0=ALU.mult, op1=ALU.mult),
                                    [P, vecA, rs], [qn])
                                R_ = pS.next()
                                em.op("pe", lambda e, qn=qn, R_=R_: e.matmul(out=R_[:, 0:n], lhsT=rotm[:], rhs=qn[:, 0:n],
                                                                         start=True, stop=True), [qn, rotm], [R_])
                                t1 = t1r.next()
                                em.op("pool", lambda e, t1=t1, qn=qn, rc=rc: e.tensor_tensor(
                                    out=t1[:, 0:n], in0=qn[:, 0:n], in1=rc[:, 0:n], op=ALU.mult), [qn, rc], [t1])
                                t2 = t2r.next()
                                em.op("dve", lambda e, t2=t2, R_=R_, rsn=rsn: e.tensor_tensor(
                                    out=t2[:, 0:n], in0=R_[:, 0:n], in1=rsn[:, 0:n], op=ALU.mult), [R_, rsn], [t2])
                                em.op("pool", lambda e, t1=t1, t2=t2, dst=dst: e.tensor_tensor(
                                    out=dst, in0=t1[:, 0:n], in1=t2[:, 0:n], op=ALU.add), [t1, t2], [dstT])
                        for jb in range(n // 128):
                            blk = (o // 128) + jb
                            Pv = pV.next()
                            for k in range(8):
                                em.op("pe", lambda e, k=k, jb=jb, Pv=Pv: e.matmul(
                                    out=Pv[:, 0:256], lhsT=xn[:, k, jb * 128:(jb + 1) * 128], rhs=Win[k][:, 768:1024],
                                    start=(k == 0), stop=(k == 7)), [Win[k], xn], [Pv], inc=(k == 7))
                            em.op("act", lambda e, Pv=Pv, blk=blk: e.copy(out=Vt[:, blk, :], in_=Pv[:, 0:256]), [Pv], [Vt])
                        U = Uh if kind == "h" else Ux
                        uo = 16 + (0 if kind == "h" else t * 512)
                        for g in range(4):
                            Pu = pV.next()
                            for k in range(8):
                                em.op("pe", lambda e, k=k, g=g, Pu=Pu: e.matmul(
                                    out=Pu[:, 0:n], lhsT=Win[k][:, 1024 + g * 128:1024 + (g + 1) * 128], rhs=xn[:, k, 0:n],
                                    start=(k == 0), stop=(k == 7)), [Win[k], xn], [Pu], inc=(k == 7))
                            em.op("act", lambda e, Pu=Pu, g=g, U=U, uo=uo: e.copy(out=U[:, g, uo:uo + n], in_=Pu[:, 0:n]), [Pu], [U])
                    for (U, N, o) in ((Uh, CTX, 0), (Ux, SEQ, CTX)):
                        W_ = N + 32
                        for g, w in enumerate((2, 4, 8, 16)):
                            src = None
                            bufs = [Ta, Tb]
                            cur = bufs[0]
                            em.op("pool", lambda e, cur=cur, U=U, g=g: e.tensor_tensor(
                                out=cur[:, 1:W_], in0=U[:, g, 0:W_ - 1], in1=U[:, g, 1:W_], op=ALU.add), [U], [cur])
                            lo, hi, sh = 1, W_, 1
                            nb_ = 1
                            while (1 << nb_) < w + 0 and (2 << (nb_ - 1)) < w:
                                nxt = bufs[nb_ % 2]
                                lo2, hi2 = lo + sh, hi - sh
                                em.op("pool", lambda e, cur=cur, nxt=nxt, lo2=lo2, hi2=hi2, sh=sh: e.tensor_tensor(
                                    out=nxt[:, lo2:hi2], in0=cur[:, lo2 - sh:hi2 - sh], in1=cur[:, lo2 + sh:hi2 + sh], op=ALU.add),
                                    [cur], [nxt])
                                cur, lo, hi, sh = nxt, lo2, hi2, sh * 2
                                nb_ += 1
                            em.op("pool", lambda e, cur=cur, g=g: e.tensor_tensor(
                                out=cur[:, 16:32], in0=cur[:, 16:32], in1=edge[:, g, 0:16], op=ALU.mult), [cur, edge], [cur])
                            em.op("pool", lambda e, cur=cur, g=g, N=N: e.tensor_tensor(
                                out=cur[:, N:N + 16], in0=cur[:, N:N + 16], in1=edge[:, g, 16:32], op=ALU.mult), [cur, edge], [cur])
                            em.op("dve", lambda e, cur=cur, g=g, U=U, N=N, o=o, w=w: e.scalar_tensor_tensor(
                                out=PL[:, g, o:o + N], in0=cur[:, 16:16 + N], scalar=1.0 / w, in1=U[:, g, 16:16 + N],
                                op0=ALU.mult, op1=ALU.subtract), [cur, U], [PL])
                    em.barrier()
                with contextlib.ExitStack() as st:
                    AR = st.enter_context(nc.sbuf_tensor("arena_b%d" % b, [128, 8 * 1024 + 512], BF16))
                    Wout = [T(AR[:, k * 1024:(k + 1) * 1024], f"wout{k}") for k in range(8)]
                    Pw = T(AR[:, 8192:8704], "poolw")
                    for k in range(8):
                        em.dma("pool", Wout[k][:], dA["w_out"][k * 128:(k + 1) * 128, :], [], [Wout[k]], self.wsem[k])
                    em.dma("pool", Pw[:].rearrange("p (g d) -> p g d", g=4), dA["pw"].rearrange("g c d -> c g d"), [], [Pw], self.wsem[8])
                    cat = [self.sb(st, f"a_cat{c}", [128, 512], BF16) for c in range(8)]
                    ptr = self.ring(st, "a_pt", 3, [128, 512], BF16)
                    rdr = self.ring(st, "a_rd", 2, [128, 512], F32)
                    xrr = self.ring(st, "a_xr", 4, [128, 512], F32)
                    pST = self.ring(st, "ps_bS", 2, [128, 512], F32, psum=True)
                    pO = self.ring(st, "ps_bO", 2, [128, 512], F32, psum=True)
                    pD = self.ring(st, "ps_bD", 2, [128, 512], F32, psum=True)
                    pY = self.ring(st, "ps_bY", 2, [128, 512], F32, psum=True)
                    for (kind, _b, t, n, dap, trk, r) in tl:
                        o = 0 if kind == "h" else CTX + t * 512
                        nblk = 2 if kind == "h" else 18
                        for h in range(4):
                            kvh = h // 2
                            O_ = pO.next()
                            D_ = pD.next()

                            def emit_st(blk, h=h, kvh=kvh):
                                S_ = pST.next()
                                em.op("pe", lambda e: e.matmul(out=S_[:, 0:n], lhsT=KT[:, kvh, blk * 128:(blk + 1) * 128],
                                                               rhs=QT[:, h, o:o + n], start=True, stop=True), [KT, QT], [S_])
                                return S_
                            S_next = emit_st(0)
                            for blk in range(nblk):
                                S_ = S_next
                                if blk + 1 < nblk:
                                    S_next = emit_st(blk + 1)
                                pt = ptr.next()
                                em.op("act", lambda e, pt=pt, S_=S_: e.activation(out=pt[:, 0:n], in_=S_[:, 0:n], func=AF.Exp, scale=SC),
                                      [S_], [pt])
                                em.op("pe", lambda e, pt=pt, blk=blk, kvh=kvh, O_=O_: e.matmul(
                                    out=O_[:, 0:n], lhsT=Vt[:, blk, kvh * 128:(kvh + 1) * 128], rhs=pt[:, 0:n],
                                    start=(blk == 0), stop=(blk == nblk - 1)), [Vt, pt], [O_])
                                em.op("pe", lambda e, pt=pt, blk=blk, D_=D_: e.matmul(
                                    out=D_[:, 0:n], lhsT=self.ones_bf[:], rhs=pt[:, 0:n],
                                    start=(blk == 0), stop=(blk == nblk - 1)), [self.ones_bf, pt], [D_])
                            rd = rdr.next()
                            em.op("dve", lambda e, rd=rd, D_=D_: e.reciprocal(out=rd[:, 0:n], in_=D_[:, 0:n]), [D_], [rd])
                            em.op("dve", lambda e, rd=rd, O_=O_, h=h: e.tensor_tensor(
                                out=cat[h][:, 0:n], in0=O_[:, 0:n], in1=rd[:, 0:n], op=ALU.mult), [O_, rd], [cat[h]])
                        for g in range(4):
                            Y_ = pY.next()
                            em.op("pe", lambda e, g=g, Y_=Y_: e.matmul(out=Y_[:, 0:n], lhsT=Pw[:, g * 128:(g + 1) * 128],
                                                                     rhs=PL[:, g, o:o + n], start=True, stop=True), [Pw, PL], [Y_])
                            em.op("act", lambda e, g=g, Y_=Y_: e.activation(out=cat[4 + g][:, 0:n], in_=Y_[:, 0:n], func=AF.Copy,
                                                                          scale=vecA[:, 2 + g:3 + g]), [Y_, vecA], [cat[4 + g]])
                        for dd in range(8):
                            xr = xrr.next()
                            em.dma("sp", xr[:, 0:n], dap[:, dd, :], [trk], [xr], xr)
                            Y_ = pY.next()
                            for c in range(8):
                                em.op("pe", lambda e, c=c, dd=dd, Y_=Y_: e.matmul(
                                    out=Y_[:, 0:n], lhsT=Wout[c][:, dd * 128:(dd + 1) * 128], rhs=cat[c][:, 0:n],
                                    start=(c == 0), stop=(c == 7)), [Wout[c], cat[c]], [Y_], inc=(c == 7))
                            em.op("dve", lambda e, dd=dd, Y_=Y_, xr=xr: e.scalar_tensor_tensor(
                                out=xr[:, 0:n], in0=Y_[:, 0:n], scalar=self.modG[:, l, s, dd, r:r + 1], in1=xr[:, 0:n],
                                op0=ALU.mult, op1=ALU.add), [Y_, self.modG, xr], [xr])
                            em.dma("sp", dap[:, dd, :], xr[:, 0:n], [xr], [trk], xr)
                    em.barrier()

    def phase_gdn(self):
        em, nc, NB = self.em, self.nc, self.NB
        dG = self.dG
        l, s = 1, 1
        NT = CTX + SEQ
        NC_ = NT // 64
        NCH = CTX // 64
        mm = lambda e, **kw: e.matmul(**kw)
        with contextlib.ExitStack() as so:
            cum = self.sb(so, "g_cum", [64, 2, 64], F32)
            mpos = self.sb(so, "g_mpos", [64, 2, 64], F32)
            mneg = self.sb(so, "g_mneg", [64, 2, 64], F32)
            identb = self.sb(so, "g_identb", [128, 128], BF16)
            onesf = self.sb(so, "g_onesf", [64, 128], F32)
            one_t = self.sb(so, "g_one", [128, 1], F32)
            CW = self.sb(so, "g_cw", [128, 72], F32)
            OG = self.sb(so, "g_og", [128, 1], F32)
            DTB = self.sb(so, "g_dtb", [64, 576], F32)
            NEGA = self.sb(so, "g_nega", [64, 576], F32)
            for d in range(2):
                em.dma("sp", cum[:, d, :], dG["cum"][d], [], [cum], cum)
                em.dma("sp", mpos[:, d, :], dG["mpos"][d], [], [mpos], mpos)
                em.dma("sp", mneg[:, d, :], dG["mneg"][d], [], [mneg], mneg)
            em.dma("pool", identb[:], self.ident_d[:, :], [], [identb], identb)
            em.op("pool", lambda e: e.memset(onesf[:], 1.0), [], [onesf])
            em.op("pool", lambda e: e.memset(one_t[:], 1.0), [], [one_t])
            with contextlib.ExitStack() as st:
                row = self.sb(st, "g_row", [1, 9216 + 128], F32)
                r2 = self.sb(st, "g_row2", [1, 1152], F32)
                pr = self.ring(st, "ps_gv", 2, [128, 512], F32, psum=True)
                em.dma("sp", row[:, 0:9216], dG["conv"][:, :], [], [row], row)
                em.dma("sp", row[:, 9216:9344], dG["og"][:, :], [], [row], row)
                em.dma("sp", r2[:, 0:576], dG["dtb"][:, :], [], [r2], r2)
                em.dma("sp", r2[:, 576:1152], dG["alog"][:, :], [], [r2], r2)
                pt = pr.next()
                for j in range(73):
                    em.op("pe", lambda e, j=j: e.matmul(out=pt[:, j:j + 1], lhsT=row[0:1, j * 128:(j + 1) * 128],
                                                       rhs=self.ones_f[0:1, 0:1], start=True, stop=True),
                          [row, self.ones_f], [pt], inc=(j == 72))
                em.op("dve", lambda e: e.tensor_copy(out=CW[:], in_=pt[:, 0:72]), [pt], [CW])
                em.op("dve", lambda e: e.tensor_copy(out=OG[:], in_=pt[:, 72:73]), [pt], [OG])
                for half, dst in ((0, DTB), (1, NEGA)):
                    for q in range(2):
                        pt = pr.next()
                        em.op("pe", lambda e, half=half, q=q, pt=pt: e.matmul(
                            out=pt[0:64, 0:288], lhsT=self.ones_f[0:1, 0:64],
                            rhs=r2[0:1, half * 576 + q * 288: half * 576 + (q + 1) * 288], start=True, stop=True),
                            [r2, self.ones_f], [pt])
                        if half == 0:
                            em.op("dve", lambda e, q=q, pt=pt: e.tensor_copy(out=DTB[:, q * 288:(q + 1) * 288], in_=pt[0:64, 0:288]), [pt], [DTB])
                        else:
                            em.op("act", lambda e, q=q, pt=pt: e.activation(out=NEGA[:, q * 288:(q + 1) * 288], in_=pt[0:64, 0:288], func=AF.Exp), [pt], [NEGA])
                em.op("dve", lambda e: e.tensor_scalar(out=NEGA[:], in0=NEGA[:], scalar1=-1.0, scalar2=None, op0=ALU.mult), [NEGA], [NEGA])
                em.barrier()

            for b in range(NB):
                with contextlib.ExitStack() as sb_:
                    XN = self.sb(sb_, "g_XN", [128, 8, NT], BF16)
                    Gt = self.sb(sb_, "g_Gt", [64, 576], F32)
                    Bt = self.sb(sb_, "g_Bt", [64, 576], F32)
                    nBt = self.sb(sb_, "g_nBt", [64, 576], F32)
                    GCt = self.sb(sb_, "g_GCt", [64, 576], F32)
                    ETL = self.sb(sb_, "g_ETL", [64, 576], F32)
                    NBEG = self.sb(sb_, "g_NBEG", [64, 576], F32)
                    EGT = self.sb(sb_, "g_EGT", [128, 576], F32)
                    v4 = lambda t: t[:].rearrange("p (c d h) -> p c d h", c=NC_, d=2)
                    with contextlib.ExitStack() as st:
                        xtr = self.ring(st, "g_xt", 1, [128, 8, 512], F32)
                        rings = (self.ring(st, "g_sq", 2, [128, 512], BF16),
                                 self.ring(st, "ps_gss", 2, [128, 512], F32, psum=True),
                                 self.ring(st, "g_sd", 1, [128, 512], F32),
                                 self.ring(st, "g_rs", 1, [128, 512], F32),
                                 self.ring(st, "g_tm", 2, [128, 512], F32))
                        Wab = self.sb(st, "g_Wab", [128, 8, 32], BF16)
                        ABt = self.sb(st, "g_ABt", [64, NC_ * 32], F32)
                        tmpA = self.sb(st, "g_tmpA", [64, 576], F32)
                        tmpB = self.sb(st, "g_tmpB", [64, 576], F32)
                        GTt = self.sb(st, "g_GTt", [64, 576], F32)
                        pA = self.ring(st, "ps_gA", 2, [128, 512], F32, psum=True)
                        em.dma("pool", Wab[:], dG["w_in"][:, 4096:4128].rearrange("(k p) n -> p k n", p=128), [], [Wab], self.wsem[0])
                        tl = [("h", 0, CTX, self.d["HT"][b].rearrange("c p n -> p c n"), self.HTt[b], NB)]
                        for t in range(4):
                            tl.append(("x", t, 512, self.d["XT"][b, :, :, t * 512:(t + 1) * 512].rearrange("c p n -> p c n"),
                                       self.XTt[b][t], b))

                        class _V:
                            pass
                        for (kind, t, n, dap, trk, r) in tl:
                            o = 0 if kind == "h" else CTX + t * 512
                            xt = xtr.next()
                            em.dma("sp", xt[:, :, 0:n], dap, [trk], [xt], xt)
                            xnv = T(XN[:, :, o:o + n], "xnv")
                            xnv.w, xnv.r = XN.w, XN.r
                            self.norm_mod(rings, xt, n, l, s, r, xnv)
                            XN.w = xnv.w
                            XN.r = {}
                        for grp in range((NC_ + 15) // 16):
                            pa = pA.next()
                            c0 = grp * 16
                            c1 = min(NC_, c0 + 16)
                            for cc in range(c0, c1):
                                for k in range(8):
                                    em.op("pe", lambda e, cc=cc, k=k, pa=pa, c0=c0: e.matmul(
                                        out=pa[0:64, (cc - c0) * 32:(cc - c0 + 1) * 32], lhsT=XN[:, k, cc * 64:(cc + 1) * 64],
                                        rhs=Wab[:, k, :], start=(k == 0), stop=(k == 7)), [XN, Wab], [pa], inc=(k == 7 and cc == c1 - 1))
                            em.op("dve", lambda e, pa=pa, c0=c0, c1=c1: e.tensor_copy(
                                out=ABt[:, c0 * 32:c1 * 32], in_=pa[0:64, 0:(c1 - c0) * 32]), [pa], [ABt])
                        ab5 = ABt[:].rearrange("p (c d a h) -> p c d a h", c=NC_, d=2, a=2)
                        em.op("dve", lambda e: e.tensor_tensor(out=v4(tmpA), in0=ab5[:, :, :, 0, :], in1=v4(DTB), op=ALU.add), [ABt, DTB], [tmpA])
                        em.op("act", lambda e: e.activation(out=tmpA[:], in_=tmpA[:], func=AF.Exp), [tmpA], [tmpA])
                        em.op("act", lambda e: e.activation(out=tmpA[:], in_=tmpA[:], func=AF.Ln, bias=one_t[0:64, 0:1], scale=1.0), [tmpA, one_t], [tmpA])
                        em.op("dve", lambda e: e.tensor_tensor(out=Gt[:], in0=tmpA[:], in1=NEGA[:], op=ALU.mult), [tmpA, NEGA], [Gt])
                        em.op("act", lambda e: e.activation(out=v4(Bt), in_=ab5[:, :, :, 1, :], func=AF.Sigmoid), [ABt], [Bt])
                        em.op("dve", lambda e: e.tensor_scalar(out=nBt[:], in0=Bt[:], scalar1=-1.0, scalar2=None, op0=ALU.mult), [Bt], [nBt])
                        for d in range(2):
                            pa = pA.next()
                            em.op("pe", lambda e, d=d, pa=pa: e.matmul(
                                out=pa[0:64, 0:288].rearrange("p (c h) -> p c h", c=NC_), lhsT=cum[:, d, :], rhs=v4(Gt)[:, :, d, :],
                                start=True, stop=True), [cum, Gt], [pa])
                            em.op("dve", lambda e, d=d, pa=pa: e.tensor_copy(
                                out=v4(GCt)[:, :, d, :], in_=pa[0:64, 0:288].rearrange("p (c h) -> p c h", c=NC_)), [pa], [GCt])
                        for q in range(2):
                            pa = pA.next()
                            em.op("pe", lambda e, q=q, pa=pa: e.matmul(out=pa[:, 0:288], lhsT=onesf[:, :], rhs=Gt[:, q * 288:(q + 1) * 288],
                                                                     start=True, stop=True), [onesf, Gt], [pa])
                            em.op("act", lambda e, q=q, pa=pa: e.activation(out=EGT[:, q * 288:(q + 1) * 288], in_=pa[:, 0:288], func=AF.Exp), [pa], [EGT])
                            em.op("dve", lambda e, q=q, pa=pa: e.tensor_copy(out=GTt[:, q * 288:(q + 1) * 288], in_=pa[0:64, 0:288]), [pa], [GTt])
                        em.op("dve", lambda e: e.tensor_tensor(out=tmpB[:], in0=GTt[:], in1=GCt[:], op=ALU.subtract), [GTt, GCt], [tmpB])
                        em.op("act", lambda e: e.activation(out=ETL[:], in_=tmpB[:], func=AF.Exp), [tmpB], [ETL])
                        em.op("act", lambda e: e.activation(out=tmpA[:], in_=GCt[:], func=AF.Exp), [GCt], [tmpA])
                        em.op("dve", lambda e: e.tensor_tensor(out=NBEG[:], in0=tmpA[:], in1=nBt[:], op=ALU.mult), [tmpA, nBt], [NBEG])
                        em.barrier()
                    for hd in range(8):
                        self.gdn_head(b, hd, XN, Gt, Bt, nBt, GCt, ETL, NBEG, EGT, cum, mpos, mneg, identb, onesf, CW, OG)
                    with contextlib.ExitStack() as st:
                        AR = st.enter_context(nc.sbuf_tensor("arena_go%d" % b, [128, 8 * 1024], BF16))
                        Wout = [T(AR[:, k * 1024:(k + 1) * 1024], f"gwout{k}") for k in range(8)]
                        for k in range(8):
                            em.dma("pool", Wout[k][:], dG["w_out"][k * 128:(k + 1) * 128, :], [], [Wout[k]], self.wsem[k])
                        ycr = self.ring(st, "g_yc", 2, [128, 8, 512], BF16)
                        xrr = self.ring(st, "g_xr", 4, [128, 512], F32)
                        pY = self.ring(st, "ps_gY", 2, [128, 512], F32, psum=True)
                        for t in range(4):
                            n = 512
                            dap = self.d["XT"][b, :, :, t * 512:(t + 1) * 512].rearrange("c p n -> p c n")
                            trk = self.XTt[b][t]
                            yc = ycr.next()
                            em.dma("sp", yc[:], self.YC_d[:, :, t * 512:(t + 1) * 512].rearrange("c p n -> p c n"), [self.YCt], [yc], yc)
                            for dd in range(8):
                                xr = xrr.next()
                                em.dma("sp", xr[:, 0:n], dap[:, dd, :], [trk], [xr], xr)
                                Y_ = pY.next()
                                for c in range(8):
                                    em.op("pe", lambda e, c=c, dd=dd, Y_=Y_, yc=yc: e.matmul(
                                        out=Y_[:, 0:n], lhsT=Wout[c][:, dd * 128:(dd + 1) * 128], rhs=yc[:, c, :],
                                        start=(c == 0), stop=(c == 7)), [Wout[c], yc], [Y_], inc=(c == 7))
                                em.op("dve", lambda e, dd=dd, Y_=Y_, xr=xr: e.scalar_tensor_tensor(
                                    out=xr[:, 0:n], in0=Y_[:, 0:n], scalar=self.modG[:, l, s, dd, b:b + 1], in1=xr[:, 0:n],
                                    op0=ALU.mult, op1=ALU.add), [Y_, self.modG, xr], [xr])
                                em.dma("sp", dap[:, dd, :], xr[:, 0:n], [xr], [trk], xr)
                        em.barrier()

    def gdn_head(self, b, hd, XN, Gt, Bt, nBt, GCt, ETL, NBEG, EGT, cum, mpos, mneg, identb, onesf, CW, OG):
        em, nc = self.em, self.nc
        dG = self.dG
        NT = CTX + SEQ
        NC_ = NT // 64
        NCH = CTX // 64
        col = lambda t, cc, d: t[:, cc * 16 + d * 8 + hd: cc * 16 + d * 8 + hd + 1]
        with contextlib.ExitStack() as st:
            kT = self.sb(st, "h_kT", [128, NT], BF16)
            qT = self.sb(st, "h_qT", [128, SEQ], BF16)
            vT = self.sb(st, "h_vT", [128, NT], BF16)
            zS = self.sb(st, "h_zS", [128, SEQ], BF16)
            KTL = [self.sb(st, f"h_KTL{d}", [64, NC_, 128], BF16) for d in range(2)]
            BV = [self.sb(st, f"h_BV{d}", [64, NC_, 128], BF16) for d in range(2)]
            Yv = [self.sb(st, f"h_Y{d}", [64, NC_, 64], BF16) for d in range(2)]
            QK = [self.sb(st, f"h_QK{d}", [64, NC_ - NCH, 64], BF16) for d in range(2)]
            QD = [self.sb(st, f"h_QD{d}", [128, SEQ], BF16) for d in range(2)]
            OA = [self.sb(st, f"h_OA{d}", [128, SEQ], F32) for d in range(2)]
            stp = contextlib.ExitStack()
            pP = self.ring(stp, "ps_hP", 2, [128, 512], F32, psum=True)
            pQ = self.ring(stp, "ps_hQ", 2, [128, 512], F32, psum=True)
            pR = self.ring(stp, "ps_hR", 2, [128, 512], F32, psum=True)
            pS_ = self.ring(stp, "ps_hS", 2, [128, 512], F32, psum=True)
            st1 = contextlib.ExitStack()
            wr = self.ring(st1, "h_w", 2, [128, 8, 128], BF16)
            PB = self.sb(st1, "h_PB", [128, NT + 4], F32)
            CO = self.sb(st1, "h_CO", [128, NT + 4], F32)
            sqr = self.ring(st1, "h_sq", 2, [128, 512], BF16)
            sdr = self.ring(st1, "h_sd", 2, [128, 512], F32)
            rsr = self.ring(st1, "h_rs", 2, [128, 512], F32)
            em.op("pool", lambda e: e.memset(PB[:, 0:1], 0.0), [], [PB])
            em.op("pool", lambda e: e.memset(PB[:, 257:259], 0.0), [], [PB])
            em.op("pool", lambda e: e.memset(PB[:, NT + 3:NT + 4], 0.0), [], [PB])
            pcol = lambda tok: tok + 1 if tok < CTX else tok + 3
            tiles = [(0, CTX)] + [(CTX + t * 512, 512) for t in range(4)]
            for ty in (1, 2, 0, 3):
                w = wr.next()
                c0 = ty * 1024 + hd * 128
                em.dma("pool", w[:], dG["w_in"][:, c0:c0 + 128].rearrange("(k p) n -> p k n", p=128), [], [w], self.wsem[ty])
                for (o, n) in tiles:
                    if ty in (0, 3) and o < CTX:
                        continue
                    P = pP.next()
                    for k in range(8):
                        em.op("pe", lambda e, k=k, P=P, w=w, o=o, n=n: e.matmul(out=P[:, 0:n], lhsT=w[:, k, :], rhs=XN[:, k, o:o + n],
                                                                         start=(k == 0), stop=(k == 7)), [w, XN], [P], inc=(k == 7))
                    if ty == 3:
                        em.op("act", lambda e, P=P, o=o, n=n: e.activation(out=zS[:, o - CTX:o - CTX + n], in_=P[:, 0:n], func=AF.Silu), [P], [zS])
                    else:
                        em.op("act", lambda e, P=P, o=o, n=n: e.copy(out=PB[:, pcol(o):pcol(o) + n], in_=P[:, 0:n]), [P], [PB])
                if ty == 3:
                    continue
                lo = 1 if ty != 0 else 259
                hi = NT + 3
                cw = lambda tap: CW[:, tap * 24 + ty * 8 + hd: tap * 24 + ty * 8 + hd + 1]
                em.op("dve", lambda e: e.tensor_scalar(out=CO[:, lo:hi], in0=PB[:, lo - 1:hi - 1], scalar1=cw(0), scalar2=None, op0=ALU.mult),
                      [PB, CW], [CO])
                em.op("dve", lambda e: e.scalar_tensor_tensor(out=CO[:, lo:hi], in0=PB[:, lo:hi], scalar=cw(1), in1=CO[:, lo:hi],
                                                              op0=ALU.mult, op1=ALU.add), [PB, CW, CO], [CO])
                em.op("dve", lambda e: e.scalar_tensor_tensor(out=CO[:, lo:hi], in0=PB[:, lo + 1:hi + 1], scalar=cw(2), in1=CO[:, lo:hi],
                                                              op0=ALU.mult, op1=ALU.add), [PB, CW, CO], [CO])
                for (o, n) in tiles:
                    if ty == 0 and o < CTX:
                        continue
                    pc = pcol(o)
                    if ty == 2:
                        em.op("act", lambda e, o=o, n=n, pc=pc: e.activation(out=vT[:, o:o + n], in_=CO[:, pc:pc + n], func=AF.Silu), [CO], [vT])
                        continue
                    em.op("act", lambda e, n=n, pc=pc: e.activation(out=CO[:, pc:pc + n], in_=CO[:, pc:pc + n], func=AF.Silu), [CO], [CO])
                    sq = sqr.next()
                    em.op("act", lambda e, sq=sq, n=n, pc=pc: e.activation(out=sq[:, 0:n], in_=CO[:, pc:pc + n], func=AF.Square), [CO], [sq])
                    S_ = pQ.next()
                    em.op("pe", lambda e, sq=sq, S_=S_, n=n: e.matmul(out=S_[:, 0:n], lhsT=self.ones_bf[:], rhs=sq[:, 0:n], start=True, stop=True),
                          [sq, self.ones_bf], [S_])
                    sd = sdr.next()
                    em.op("act", lambda e, sd=sd, S_=S_, n=n: e.activation(out=sd[:, 0:n], in_=S_[:, 0:n], func=AF.Sqrt, scale=1.0,
                                                                        bias=self.eps_t[:, 0:1]), [S_, self.eps_t], [sd])
                    rs = rsr.next()
                    em.op("dve", lambda e, sd=sd, rs=rs, n=n: e.reciprocal(out=rs[:, 0:n], in_=sd[:, 0:n]), [sd], [rs])
                    if ty == 1:
                        em.op("dve", lambda e, rs=rs, o=o, n=n, pc=pc: e.tensor_tensor(out=kT[:, o:o + n], in0=CO[:, pc:pc + n], in1=rs[:, 0:n],
                                                                                  op=ALU.mult), [CO, rs], [kT])
                    else:
                        em.op("dve", lambda e, rs=rs, o=o, n=n, pc=pc: e.scalar_tensor_tensor(
                            out=qT[:, o - CTX:o - CTX + n], in0=CO[:, pc:pc + n], scalar=128 ** -0.5, in1=rs[:, 0:n],
                            op0=ALU.mult, op1=ALU.mult), [CO, rs], [qT])
            em.barrier()
            st1.close()
            sqr = self.ring(st, "h_sq_b", 2, [128, 512], BF16)
            sdr = self.ring(st, "h_sd_b", 1, [128, 512], F32)
            rsr = self.ring(st, "h_rs_b", 1, [128, 512], F32)
            for g4 in range(NC_ // 4):
                Pk = pP.next()
                Pv = pQ.next()
                for j in range(4):
                    cc = g4 * 4 + j
                    em.op("pe", lambda e, cc=cc, j=j, Pk=Pk: e.matmul(out=Pk[0:64, j * 128:(j + 1) * 128], lhsT=kT[:, cc * 64:(cc + 1) * 64],
                                                                   rhs=identb[:], start=True, stop=True), [kT, identb], [Pk], inc=(j == 3))
                for j in range(4):
                    cc = g4 * 4 + j
                    em.op("pe", lambda e, cc=cc, j=j, Pv=Pv: e.matmul(out=Pv[0:64, j * 128:(j + 1) * 128], lhsT=vT[:, cc * 64:(cc + 1) * 64],
                                                                   rhs=identb[:], start=True, stop=True), [vT, identb], [Pv], inc=(j == 3))
                if DBG.get('bc_tok'):
                    c0 = g4 * 4
                    cb4 = lambda t, d: t[:, :].rearrange("p (c k) -> p c k", k=16)[:, c0:c0 + 4, d * 8 + hd:d * 8 + hd + 1].broadcast_to([64, 4, 128])
                    for d in range(2):
                        em.op("dve", lambda e, d=d, Pk=Pk: e.tensor_tensor(out=KTL[d][:, c0:c0 + 4, :], in0=Pk[0:64, 0:512].rearrange("p (g n) -> p g n", g=4),
                                                                       in1=cb4(ETL, d), op=ALU.mult), [Pk, ETL], [KTL[d]])
                        em.op("dve", lambda e, d=d, Pv=Pv: e.tensor_tensor(out=BV[d][:, c0:c0 + 4, :], in0=Pv[0:64, 0:512].rearrange("p (g n) -> p g n", g=4),
                                                                       in1=cb4(Bt, d), op=ALU.mult), [Pv, Bt], [BV[d]])
                else:
                    for j in range(4):
                        cc = g4 * 4 + j
                        for d in range(2):
                            em.op("act", lambda e, cc=cc, j=j, d=d, Pk=Pk: e.activation(
                                out=KTL[d][:, cc, :], in_=Pk[0:64, j * 128:(j + 1) * 128], func=AF.Copy, scale=col(ETL, cc, d)), [Pk, ETL], [KTL[d]])
                            em.op("dve", lambda e, cc=cc, j=j, d=d, Pv=Pv: e.tensor_scalar(
                                out=BV[d][:, cc, :], in0=Pv[0:64, j * 128:(j + 1) * 128], scalar1=col(Bt, cc, d), scalar2=None, op0=ALU.mult),
                                [Pv, Bt], [BV[d]])
            em.barrier()
            stp.close()
            GSM = DBG.get("GSM", 8)
            GROUPS = [(0, NCH)] + [(g, GSM) for g in range(NCH, NC_, GSM)]
            st2 = contextlib.ExitStack()
            LN = DBG.get("LN", 2)

            def mkres(ln):
                R_ = {}
                SEP = DBG.get("sepbank", 0)
                banks = [self.ps(st2, f"ps_l{ln}_{i}", [128, 512], F32) for i in range(8 if (SEP or DBG.get("sepnames")) else 4)]
                R_["KK"] = banks[0]
                R_["KQ"] = banks[1] if len(banks) == 8 else banks[0]
                R_["pr"] = Ring(banks[2:] if len(banks) == 8 else banks[1:])
                f = lambda nm, n, p=64: self.ring(st2, f"l{ln}_{nm}", n, [p, GSM, 64], F32)
                R_["gs"], R_["e2"], R_["e1"] = f("gs", 2), f("e2", 2), f("e1", 2)
                R_["N"], R_["M"], R_["P"] = f("N", 4), f("M", 4), f("P", 4)
                R_["eg"] = f("eg", 2, 128)
                R_["Xb"] = self.ring(st2, f"l{ln}_Xb", 2, [64, GSM, 64], BF16)
                return R_
            res = [mkres(ln) for ln in range(LN)]

            def grp_gen(g0, gsz, R_):
                ccs = list(range(g0, g0 + gsz))
                flat = lambda t: t[:, 0:gsz, :].rearrange("p g n -> p (g n)")
                isx = g0 >= NCH
                KK, KQ = R_["KK"], R_["KQ"]
                QO = 0 if KQ is not KK else 256
                SEP = DBG.get("sepbank", 0)
                DO = 0 if SEP else 256
                DOF = lambda nm: 0 if (SEP or nm in DBG.get('sepnames', [])) else 256

                SEPN = DBG.get("sepnames", [])

                class _BK:
                    def __init__(self, nm=""):
                        self.sep = SEP or (nm in SEPN)
                        self.b = None if self.sep else R_["pr"].next()

                    def get(self):
                        return R_["pr"].next() if self.sep else self.b
                for j, cc in enumerate(ccs):
                    em.op("pe", lambda e, j=j, cc=cc: e.matmul(out=KK[0:64, j * 64:(j + 1) * 64], lhsT=kT[:, cc * 64:(cc + 1) * 64],
                                                             rhs=kT[:, cc * 64:(cc + 1) * 64], start=True, stop=True), [kT], [KK], inc=(j == gsz - 1))
                if isx:
                    for j, cc in enumerate(ccs):
                        em.op("pe", lambda e, j=j, cc=cc: e.matmul(out=KQ[0:64, QO + j * 64:QO + (j + 1) * 64], lhsT=kT[:, cc * 64:(cc + 1) * 64],
                                                                 rhs=qT[:, cc * 64 - CTX:(cc + 1) * 64 - CTX], start=True, stop=True),
                              [kT, qT], [KQ], inc=(j == gsz - 1))
                fr = (lambda ap: ap.bitcast(F32R)) if DBG.get('f32r') else (lambda ap: ap)
                SH = [64, gsz, 64]
                colb = lambda t, d: t[:, :].rearrange("p (c k) -> p c k", k=16)[:, g0:g0 + gsz, d * 8 + hd:d * 8 + hd + 1].broadcast_to(SH)
                gb3 = lambda m, d: m[:, d:d + 1, :].broadcast_to(SH)
                ps3 = lambda P_, off=0: P_[0:64, off:off + gsz * 64].rearrange("p (g n) -> p g n", g=gsz)
                id3 = self.ident[0:64, 0:64].rearrange("p (o n) -> p o n", o=1).broadcast_to(SH)
                C = [dict() for _ in range(2)]
                KYM = DBG.get("kym", 1)
                for d in range(2):
                    gs = C[d]["gs"] = R_["gs"].next()
                    if DBG.get('bA'):
                        em.op("dve", lambda e, gs=gs, d=d: e.tensor_tensor(out=gs[:, 0:gsz, :], in0=gb3(cum, d), in1=colb(Gt, d), op=ALU.mult),
                              [cum, Gt], [gs])
                    else:
                        for j, cc in enumerate(ccs):
                            em.op("dve", lambda e, j=j, cc=cc, gs=gs, d=d: e.tensor_scalar(out=gs[:, j, :], in0=cum[:, d, :], scalar1=col(Gt, cc, d),
                                                                                     scalar2=None, op0=ALU.mult), [cum, Gt], [gs])
                yield
                BK_GB = _BK("GB")
                for d in range(2):
                    GB = C[d]["GB"] = BK_GB.get()
                    gs = C[d]["gs"]
                    em.op("pe", lambda e, gs=gs, GB=GB: e.matmul(out=GB[:, d * DOF('GB'):d * DOF('GB') + gsz * 64], lhsT=onesf[:, :], rhs=flat(gs), start=True, stop=True),
                          [onesf, gs], [GB])
                yield
                for d in range(2):
                    GB = C[d]["GB"]
                    e2 = C[d]["e2"] = R_["e2"].next()
                    e1 = C[d]["e1"] = R_["e1"].next()
                    if DBG.get('bB'):
                        em.op("dve", lambda e, e2=e2, GB=GB, d=d: e.tensor_tensor(out=e2[:, 0:gsz, :], in0=ps3(GB, d * DOF('GB')), in1=colb(GCt, d),
                                                                           op=ALU.subtract), [GB, GCt], [e2])
                        if isx:
                            em.op("dve", lambda e, e1=e1, e2=e2, d=d: e.tensor_tensor(out=e1[:, 0:gsz, :], in0=e2[:, 0:gsz, :], in1=gb3(mneg, d), op=ALU.min),
                                  [e2, mneg], [e1])
                        em.op("dve", lambda e, e2=e2, d=d: e.tensor_tensor(out=e2[:, 0:gsz, :], in0=e2[:, 0:gsz, :], in1=gb3(mpos, d), op=ALU.max),
                              [e2, mpos], [e2])
                    else:
                        for j, cc in enumerate(ccs):
                            em.op("dve", lambda e, j=j, cc=cc, e2=e2, GB=GB, d=d: e.scalar_tensor_tensor(
                                out=e2[:, j, :], in0=GB[0:64, d * DOF('GB') + j * 64:d * DOF('GB') + (j + 1) * 64], scalar=col(GCt, cc, d), in1=mpos[:, d, :],
                                op0=ALU.subtract, op1=ALU.max), [GB, GCt, mpos], [e2])
                            if isx:
                                em.op("dve", lambda e, j=j, cc=cc, e1=e1, GB=GB, d=d: e.scalar_tensor_tensor(
                                    out=e1[:, j, :], in0=GB[0:64, d * DOF('GB') + j * 64:d * DOF('GB') + (j + 1) * 64], scalar=col(GCt, cc, d), in1=mneg[:, d, :],
                                    op0=ALU.subtract, op1=ALU.min), [GB, GCt, mneg], [e1])
                yield
                for d in range(2):
                    GB, e2, e1 = C[d]["GB"], C[d]["e2"], C[d]["e1"]
                    em.op("act", lambda e, e2=e2: e.activation(out=e2[:, 0:gsz, :], in_=e2[:, 0:gsz, :], func=AF.Exp, scale=-1.0), [e2], [e2])
                    if isx:
                        em.op("act", lambda e, e1=e1: e.activation(out=e1[:, 0:gsz, :], in_=e1[:, 0:gsz, :], func=AF.Exp), [e1], [e1])
                        eg = C[d]["eg"] = R_["eg"].next()
                        em.op("act", lambda e, eg=eg, GB=GB: e.activation(out=flat(eg), in_=GB[:, d * DOF('GB'):d * DOF('GB') + gsz * 64], func=AF.Exp), [GB], [eg])
                yield
                for d in range(2):
                    e2, e1 = C[d]["e2"], C[d]["e1"]
                    Nn = C[d]["N"] = R_["N"].next()
                    if DBG.get('bB'):
                        em.op("dve", lambda e, e2=e2, d=d: e.tensor_tensor(out=e2[:, 0:gsz, :], in0=e2[:, 0:gsz, :], in1=colb(nBt, d), op=ALU.mult),
                              [e2, nBt], [e2])
                        em.op("dve", lambda e, Nn=Nn, e2=e2: e.tensor_tensor(out=fr(Nn[:, 0:gsz, :]), in0=ps3(KK), in1=e2[:, 0:gsz, :], op=ALU.mult),
                              [KK, e2], [Nn])
                    else:
                        for j, cc in enumerate(ccs):
                            em.op("dve", lambda e, j=j, cc=cc, Nn=Nn, e2=e2, d=d: e.scalar_tensor_tensor(
                                out=fr(Nn[:, j, :]), in0=KK[0:64, j * 64:(j + 1) * 64], scalar=col(nBt, cc, d), in1=e2[:, j, :],
                                op0=ALU.mult, op1=ALU.mult), [KK, nBt, e2], [Nn])
                    if isx:
                        eg = C[d]["eg"]
                        xo = g0 * 64 - CTX
                        em.op("pool", lambda e, eg=eg, d=d, xo=xo: e.tensor_tensor(
                            out=QD[d][:, xo:xo + gsz * 64], in0=qT[:, xo:xo + gsz * 64], in1=flat(eg), op=ALU.mult), [qT, eg], [QD[d]])
                        em.op("dve", lambda e, e1=e1, d=d: e.tensor_tensor(
                            out=QK[d][:, g0 - NCH:g0 - NCH + gsz, :], in0=KQ[0:64, QO:QO + gsz * 64].rearrange("p (g n) -> p g n", g=gsz), in1=e1[:, 0:gsz, :], op=ALU.mult),
                            [KQ, e1], [QK[d]])
                yield
                BK_PT = _BK("PT")
                for d in range(2):
                    Nn = C[d]["N"]
                    PT = C[d]["PT"] = BK_PT.get()
                    for j in range(gsz):
                        em.op("pe", lambda e, j=j, Nn=Nn, PT=PT: e.transpose(out=PT[0:64, d * DOF('PT') + j * 64:d * DOF('PT') + (j + 1) * 64], in_=Nn[:, j, :],
                                                                          identity=self.ident[0:64, 0:64]), [Nn, self.ident], [PT], inc=(j == gsz - 1))
                yield
                for d in range(2):
                    PT = C[d]["PT"]
                    Mm = C[d]["M"] = R_["M"].next()
                    Pc = C[d]["P"] = R_["P"].next()
                    em.op("act", lambda e, Mm=Mm, PT=PT: e.copy(out=fr(flat(Mm)), in_=PT[0:64, d * DOF('PT'):d * DOF('PT') + gsz * 64]), [PT], [Mm])
                    if DBG.get('bC'):
                        em.op("dve", lambda e, Pc=Pc, PT=PT, d=d: e.tensor_tensor(out=fr(Pc[:, 0:gsz, :]), in0=ps3(PT, d * DOF('PT')), in1=id3, op=ALU.add),
                              [PT, self.ident], [Pc])
                    else:
                        for j in range(gsz):
                            em.op("dve", lambda e, j=j, Pc=Pc, PT=PT: e.tensor_tensor(out=fr(Pc[:, j, :]), in0=PT[0:64, d * DOF('PT') + j * 64:d * DOF('PT') + (j + 1) * 64],
                                                                                   in1=self.ident[0:64, 0:64], op=ALU.add), [PT, self.ident], [Pc])
                yield
                for lev in range(5 if not DBG.get("skip_inv") else 0):
                    BK_PN = _BK("PN")
                    BK_PM = _BK("PM") if lev < 4 else None
                    for d in range(2):
                        Nn, Mm = C[d]["N"], C[d]["M"]
                        PN = C[d]["PN"] = BK_PN.get()
                        for j in range(gsz):
                            em.op("pe", lambda e, j=j, PN=PN, Mm=Mm, Nn=Nn: e.matmul(out=PN[0:64, d * DOF('PN') + j * 64:d * DOF('PN') + (j + 1) * 64], lhsT=fr(Mm[:, j, :]), rhs=fr(Nn[:, j, :]),
                                                                                start=True, stop=True), [Mm, Nn], [PN], inc=(j == gsz - 1))
                        if lev < 4:
                            PM = C[d]["PM"] = BK_PM.get()
                            for j in range(gsz):
                                em.op("pe", lambda e, j=j, PM=PM, Mm=Mm, Nn=Nn: e.matmul(out=PM[0:64, d * DOF('PM') + j * 64:d * DOF('PM') + (j + 1) * 64], lhsT=fr(Nn[:, j, :]), rhs=fr(Mm[:, j, :]),
                                                                                    start=True, stop=True), [Mm, Nn], [PM], inc=(j == gsz - 1))
                    yield
                    for d in range(2):
                        N2 = C[d]["N"] = R_["N"].next()
                        PN = C[d]["PN"]
                        em.op("act", lambda e, N2=N2, PN=PN: e.copy(out=fr(flat(N2)), in_=PN[0:64, d * DOF('PN'):d * DOF('PN') + gsz * 64]), [PN], [N2])
                        if lev < 4:
                            M2 = C[d]["M"] = R_["M"].next()
                            PM = C[d]["PM"]
                            em.op("dve", lambda e, M2=M2, PM=PM: e.tensor_copy(out=fr(flat(M2)), in_=PM[0:64, d * DOF('PM'):d * DOF('PM') + gsz * 64]), [PM], [M2])
                    yield
                    BK_PP = _BK("PP")
                    for d in range(2):
                        N2, Pc = C[d]["N"], C[d]["P"]
                        PP = C[d]["PP"] = BK_PP.get()
                        for j in range(gsz):
                            em.op("pe", lambda e, j=j, PP=PP, N2=N2, Pc=Pc: e.matmul(out=PP[0:64, d * DOF('PP') + j * 64:d * DOF('PP') + (j + 1) * 64], lhsT=fr(N2[:, j, :]), rhs=fr(Pc[:, j, :]),
                                                                                start=True, stop=True), [N2, Pc], [PP], inc=(j == gsz - 1))
                    yield
                    for d in range(2):
                        PP, Pc = C[d]["PP"], C[d]["P"]
                        if lev < 4 or KYM:
                            Pn = C[d]["P"] = R_["P"].next()
                            em.op("dve", lambda e, Pn=Pn, PP=PP, Pc=Pc: e.tensor_tensor(out=fr(flat(Pn)), in0=PP[0:64, d * DOF('PP'):d * DOF('PP') + gsz * 64], in1=flat(Pc), op=ALU.add),
                                  [PP, Pc], [Pn])
                            if lev == 4:
                                em.op("act", lambda e, Pn=Pn, d=d: e.copy(out=Yv[d][:, g0:g0 + gsz, :], in_=Pn[:, 0:gsz, :]), [Pn], [Yv[d]])
                        else:
                            em.op("dve", lambda e, PP=PP, Pc=Pc, d=d: e.tensor_tensor(
                                out=Yv[d][:, g0:g0 + gsz, :], in0=PP[0:64, d * DOF('PP'):d * DOF('PP') + gsz * 64].rearrange("p (g n) -> p g n", g=gsz), in1=Pc[:, 0:gsz, :], op=ALU.add),
                                [PP, Pc], [Yv[d]])
                    yield
                if KYM and not DBG.get("skip_inv"):
                    for d in range(2):
                        Pn = C[d]["P"]
                        XT_ = C[d]["XT"] = R_["pr"].next()
                        for j in range(gsz):
                            em.op("pe", lambda e, j=j, Pn=Pn, XT_=XT_: e.transpose(out=XT_[0:64, j * 64:(j + 1) * 64], in_=Pn[:, j, :],
                                                                              identity=self.ident[0:64, 0:64]), [Pn, self.ident], [XT_], inc=(j == gsz - 1))
                    yield
                    for d in range(2):
                        XT_ = C[d]["XT"]
                        Xb = C[d]["Xb"] = R_["Xb"].next()
                        em.op("act", lambda e, Xb=Xb, XT_=XT_: e.copy(out=Xb[:, 0:gsz, :].rearrange("p g n -> p (g n)"), in_=XT_[0:64, 0:gsz * 64]), [XT_], [Xb])
                    yield
                    for d in range(2):
                        Xb = C[d]["Xb"]
                        C[d]["KYp"] = []
                        for h4 in range(0, gsz, 4):
                            KYp = R_["pr"].next()
                            C[d]["KYp"].append((h4, KYp))
                            for j in range(h4, min(gsz, h4 + 4)):
                                em.op("pe", lambda e, j=j, h4=h4, KYp=KYp, Xb=Xb, d=d: e.matmul(
                                    out=KYp[0:64, (j - h4) * 128:(j - h4 + 1) * 128], lhsT=Xb[:, j, :], rhs=KTL[d][:, g0 + j, :], start=True, stop=True),
                                    [Xb, KTL[d]], [KYp], inc=(j == min(gsz, h4 + 4) - 1))
                    yield
                    for d in range(2):
                        for (h4, KYp) in C[d]["KYp"]:
                            n4 = min(gsz, h4 + 4) - h4
                            eng = "dve" if h4 == 0 else "act"
                            if eng == "dve":
                                em.op("dve", lambda e, KYp=KYp, h4=h4, n4=n4, d=d: e.tensor_copy(
                                    out=KTL[d][:, g0 + h4:g0 + h4 + n4, :], in_=KYp[0:64, 0:n4 * 128].rearrange("p (g n) -> p g n", g=n4)), [KYp], [KTL[d]])
                            else:
                                em.op("act", lambda e, KYp=KYp, h4=h4, n4=n4, d=d: e.copy(
                                    out=KTL[d][:, g0 + h4:g0 + h4 + n4, :], in_=KYp[0:64, 0:n4 * 128].rearrange("p (g n) -> p g n", g=n4)), [KYp], [KTL[d]])
                    yield
                if DBG.get("skip_inv"):
                    for d in range(2):
                        Pc = C[d]["P"]
                        em.op("dve", lambda e, Pc=Pc, d=d: e.tensor_copy(out=Yv[d][:, g0:g0 + gsz, :], in_=Pc[:, 0:gsz, :]), [Pc], [Yv[d]])

            def lane_gen(ln):
                for (g0, gsz) in GROUPS[ln::LN]:
                    yield from grp_gen(g0, gsz, res[ln])
            interleave([lane_gen(ln) for ln in range(LN)])
            em.barrier()
            st2.close()
            pP = self.ring(st, "ps_sP", 2, [128, 512], F32, psum=True)
            pQ = self.ring(st, "ps_sQ", 2, [128, 512], F32, psum=True)
            pR = self.ring(st, "ps_sR", 2, [128, 512], F32, psum=True)
            pS_ = self.ring(st, "ps_sS", 2, [128, 512], F32, psum=True)
            Sf = [self.ring(st, f"h_S{d}", 2, [128, 128], F32) for d in range(2)]
            Sbr = [self.ring(st, f"h_Sb{d}", 2, [128, 128], BF16) for d in range(2)]
            rr = self.ring(st, "h_r", 4, [64, 128], BF16)
            vnr = self.ring(st, "h_vn", 4, [64, 128], BF16)
            S_cur, Sb_cur = [], []
            for d in range(2):
                S0 = Sf[d].next()
                Sb0 = Sbr[d].next()
                em.op("pool", lambda e, S0=S0: e.memset(S0[:], 0.0), [], [S0])
                em.op("pool", lambda e, Sb0=Sb0: e.memset(Sb0[:], 0.0), [], [Sb0])
                S_cur.append(S0)
                Sb_cur.append(Sb0)
            order = [list(range(NC_)), list(range(NCH - 1, -1, -1)) + list(range(NC_ - 1, NCH - 1, -1))]
            for step in range(NC_ if not DBG.get("skip_scan") else 0):
                for d in range(2):
                    cc = order[d][step]
                    S_old, Sb_old = S_cur[d], Sb_cur[d]
                    M1 = pP.next()
                    em.op("pe", lambda e, cc=cc, M1=M1, Sb_old=Sb_old: e.matmul(out=M1[0:64, 0:128], lhsT=kT[:, cc * 64:(cc + 1) * 64], rhs=Sb_old[:],
                                                                          start=True, stop=True), [kT, Sb_old], [M1])
                    r_ = rr.next()
                    em.op("dve", lambda e, cc=cc, d=d, M1=M1, r_=r_: e.scalar_tensor_tensor(
                        out=r_[:], in0=M1[0:64, 0:128], scalar=col(NBEG, cc, d), in1=BV[d][:, cc, :], op0=ALU.mult, op1=ALU.add),
                        [M1, NBEG, BV[d]], [r_])
                    dS = pR.next()
                    em.op("pe", lambda e, cc=cc, d=d, dS=dS, r_=r_: e.matmul(out=dS[:, 0:128], lhsT=KTL[d][:, cc, :], rhs=r_[:], start=True, stop=True),
                          [KTL[d], r_], [dS])
                    S_new = Sf[d].next()
                    Sb_new = Sbr[d].next()
                    em.op("dve", lambda e, cc=cc, d=d, Sb_new=Sb_new, S_old=S_old, dS=dS: e.scalar_tensor_tensor(
                        out=Sb_new[:], in0=S_old[:], scalar=col(EGT, cc, d), in1=dS[:, 0:128], op0=ALU.mult, op1=ALU.add),
                        [S_old, EGT, dS], [Sb_new])
                    em.op("dve", lambda e, cc=cc, d=d, S_new=S_new, S_old=S_old, dS=dS: e.scalar_tensor_tensor(
                        out=S_new[:], in0=S_old[:], scalar=col(EGT, cc, d), in1=dS[:, 0:128], op0=ALU.mult, op1=ALU.add),
                        [S_old, EGT, dS], [S_new])
                    if cc >= NCH:
                        VN = pQ.next()
                        em.op("pe", lambda e, cc=cc, d=d, VN=VN, r_=r_: e.matmul(out=VN[0:64, 0:128], lhsT=Yv[d][:, cc, :], rhs=r_[:], start=True, stop=True),
                              [Yv[d], r_], [VN])
                        vn = vnr.next()
                        em.op("act", lambda e, vn=vn, VN=VN: e.copy(out=vn[:], in_=VN[0:64, 0:128]), [VN], [vn])
                        OT = pS_.next()
                        xo = cc * 64 - CTX
                        em.op("pe", lambda e, OT=OT, Sb_old=Sb_old, d=d, xo=xo: e.matmul(out=OT[:, 0:64], lhsT=Sb_old[:], rhs=QD[d][:, xo:xo + 64],
                                                                                   start=True, stop=False), [Sb_old, QD[d]], [OT])
                        em.op("pe", lambda e, OT=OT, vn=vn, d=d, cc=cc: e.matmul(out=OT[:, 0:64], lhsT=vn[:], rhs=QK[d][:, cc - NCH, :],
                                                                             start=False, stop=True), [vn, QK[d]], [OT])
                        em.op("act", lambda e, OT=OT, d=d, xo=xo: e.copy(out=OA[d][:, xo:xo + 64], in_=OT[:, 0:64]), [OT], [OA[d]])
                    S_cur[d], Sb_cur[d] = S_new, Sb_new
            ycr = self.ring(st, "h_yc", 2, [128, 512], BF16)
            tmr = self.ring(st, "h_tm", 2, [128, 512], F32)
            for t in range(4):
                sl = slice(t * 512, (t + 1) * 512)
                em.op("dve", lambda e, sl=sl: e.tensor_tensor(out=OA[0][:, sl], in0=OA[0][:, sl], in1=OA[1][:, sl], op=ALU.add), [OA[0], OA[1]], [OA[0]])
                sq = sqr.next()
                em.op("act", lambda e, sq=sq, sl=sl: e.activation(out=sq[:], in_=OA[0][:, sl], func=AF.Square), [OA[0]], [sq])
                S_ = pQ.next()
                em.op("pe", lambda e, sq=sq, S_=S_: e.matmul(out=S_[:], lhsT=self.ones_bf[:], rhs=sq[:], start=True, stop=True), [sq, self.ones_bf], [S_])
                sd = sdr.next()
                em.op("act", lambda e, sd=sd, S_=S_: e.activation(out=sd[:], in_=S_[:], func=AF.Sqrt, scale=1.0 / 128, bias=self.eps_t[:, 0:1]),
                      [S_, self.eps_t], [sd])
                rs = rsr.next()
                em.op("dve", lambda e, sd=sd, rs=rs: e.reciprocal(out=rs[:], in_=sd[:]), [sd], [rs])
                tm = tmr.next()
                em.op("dve", lambda e, tm=tm, rs=rs, sl=sl: e.scalar_tensor_tensor(out=tm[:], in0=OA[0][:, sl], scalar=OG[:, 0:1], in1=rs[:],
                                                                                 op0=ALU.mult, op1=ALU.mult), [OA[0], OG, rs], [tm])
                yc = ycr.next()
                em.op("pool", lambda e, tm=tm, yc=yc, sl=sl: e.tensor_tensor(out=yc[:], in0=tm[:], in1=zS[:, sl], op=ALU.mult), [tm, zS], [yc])
                em.dma("sp", self.YC_d[hd, :, sl], yc[:], [yc], [self.YCt], yc)
            em.barrier()


def make_consts():
    c = {"ident": np.eye(128, dtype=np.float32)}
    n = np.arange(SEQ)
    rowi = (n // 64).astype(np.float32)
    coli = (n % 64).astype(np.float32)
    inv = (10000.0 ** (-(np.arange(0, 64, 2, dtype=np.float32)) / 64)).astype(np.float32)
    C = np.zeros((128, SEQ), np.float32)
    S = np.zeros((128, SEQ), np.float32)
    rot = np.zeros((128, 128), np.float32)
    for p in range(128):
        a, h, i = p // 64, (p // 32) % 2, p % 32
        ang = (rowi if a == 0 else coli) * inv[i]
        C[p] = np.cos(ang)
        S[p] = np.sin(ang)
        if h == 0:
            rot[p + 32, p] = -1.0
        else:
            rot[p - 32, p] = 1.0
    c["ropeC"], c["ropeS"], c["rotm"] = C, S, rot
    ed = np.ones((4, 32), np.float32)
    for g, w in enumerate((2, 4, 8, 16)):
        L = w // 2
        for j in range(16):
            ed[g, j] = w / (min(j + L, 10 ** 6) - max(j - L, 0))
            tt = 16 - j
            ed[g, 16 + j] = w / (min(L, tt) + L) if True else 1.0
    c["pool_edge"] = np.ascontiguousarray(np.broadcast_to(ed[None], (128, 4, 32))).astype(np.float32)
    ii = np.arange(64)
    cum = np.zeros((2, 64, 64), np.float32)
    mpos = np.zeros((2, 64, 64), np.float32)
    mneg = np.zeros((2, 64, 64), np.float32)
    cum[0] = (ii[:, None] <= ii[None, :])
    cum[1] = (ii[:, None] >= ii[None, :])
    mpos[0] = np.where(ii[:, None] > ii[None, :], 0.0, 1e4)
    mpos[1] = np.where(ii[:, None] < ii[None, :], 0.0, 1e4)
    mneg[0] = np.where(ii[None, :] >= ii[:, None], 0.0, -1e4)
    mneg[1] = np.where(ii[None, :] <= ii[:, None], 0.0, -1e4)
    c["g_cum"], c["g_mpos"], c["g_mneg"] = cum, mpos, mneg
    return c


_CACHE = {}


def get_program(NB, stop_after=None, dbg_h=False):
    key = (NB, stop_after, dbg_h)
    if key not in _CACHE:
        _CACHE[key] = Builder(NB, stop_after, dbg_h).build()
    return _CACHE[key]


def core_inputs(inputs, b0, NB):
    m = {
        "x": np.ascontiguousarray(inputs["x"][b0:b0 + NB]),
        "ctx": np.ascontiguousarray(inputs["ctx"][b0:b0 + NB]),
        "c": np.ascontiguousarray(np.concatenate([inputs["c"][b0:b0 + NB], inputs["c_ctx"][None]], 0)),
    }
    for k in ("w_mod", "b_mod", "norm_g", "ffn_wg", "ffn_wu", "ffn_wd"):
        m[k] = np.ascontiguousarray(inputs[k])
    m["ab_w_in"] = np.ascontiguousarray(inputs["ab_w_in"][0])
    m["ab_q_norm"] = np.ascontiguousarray(inputs["ab_q_norm"])
    m["ab_k_norm"] = np.ascontiguousarray(inputs["ab_k_norm"])
    m["pool_w"] = np.ascontiguousarray(inputs["pool_w"][0])
    m["pool_scale"] = np.ascontiguousarray(inputs["pool_scale"])
    m["ab_w_out"] = np.ascontiguousarray(inputs["ab_w_out"][0])
    m["gdn_w_in"] = np.ascontiguousarray(inputs["gdn_w_in"][0])
    m["gdn_conv_w"] = np.ascontiguousarray(inputs["gdn_conv_w"][0].reshape(1, 3 * 3072))
    m["alog_rep"] = np.ascontiguousarray(np.tile(inputs["gdn_a_log"][0].reshape(16), 36)[None])
    m["dtb_rep"] = np.ascontiguousarray(np.tile(inputs["gdn_dt_bias"][0].reshape(16), 36)[None])
    m["gdn_o_norm"] = np.ascontiguousarray(inputs["gdn_o_norm"])
    m["gdn_w_out"] = np.ascontiguousarray(inputs["gdn_w_out"][0])
    m.update(make_consts())
    return m


def kernel(**inputs):
    inputs = {k: np.asarray(v) for k, v in inputs.items()}
    B = inputs["x"].shape[0]
    NB = B // NCORES
    nc = get_program(NB)
    in_maps = [core_inputs(inputs, i * NB, NB) for i in range(NCORES)]
    res = run_bass_kernel_spmd(nc, in_maps, core_ids=list(range(NCORES)))
    return np.concatenate([r["y"] for r in res.results], axis=0)
```

```python
import contextlib
import numpy as np
import concourse.bass as bass
import concourse.mybir as mybir
from concourse.bass_utils import run_bass_kernel_spmd

F32 = mybir.dt.float32
BF16 = mybir.dt.bfloat16
F32R = mybir.dt.float32r
AF = mybir.ActivationFunctionType
ALU = mybir.AluOpType

D = 1024
SEQ = 2048
CTX = 256
DFF = 2816
NF = DFF // 128
EPS = 1e-6
NCORES = 8
DBG = {"LN": 1, "sepbank": 1, "f32r": 1}


class T:
    __slots__ = ("ap", "w", "r", "dsem", "dcnt", "name")

    def __init__(self, ap, name=""):
        self.ap = ap
        self.w = None
        self.r = {}
        self.dsem = None
        self.dcnt = 0
        self.name = name

    def __getitem__(self, idx):
        return self.ap[idx]


class Eng:
    def __init__(self, name, obj, semid):
        self.name = name
        self.obj = obj
        self.semid = semid
        self.cnt = 0
        self.seen = {}


class Emitter:
    def __init__(self, nc, es):
        self.nc = nc
        self.es = es
        self.semh = []
        self.E = {}
        for name, obj in (("pe", nc.tensor), ("act", nc.scalar), ("dve", nc.vector),
                          ("pool", nc.gpsimd), ("sp", nc.sync)):
            self.E[name] = Eng(name, obj, self.new_sem("e_" + name))
        self.n_inst = 0
        self.dma_cnt = {}
        self.free_dsems = {}
        self.sem_kind = {}
        self.n_dsem = 0

    def new_sem(self, name):
        h = self.es.enter_context(self.nc.semaphore(name))
        self.semh.append(h)
        return len(self.semh) - 1

    def _deps(self, reads, writes):
        deps = {}
        for t in reads:
            if t.w is not None:
                s, v = t.w
                if deps.get(s, 0) < v:
                    deps[s] = v
        for t in writes:
            if t.w is not None:
                s, v = t.w
                if deps.get(s, 0) < v:
                    deps[s] = v
            for s, v in t.r.items():
                if deps.get(s, 0) < v:
                    deps[s] = v
        return deps

    def _wait(self, E, deps, skip_self=False):
        for s, v in deps.items():
            if skip_self and s == E.semid:
                continue
            if E.seen.get(s, 0) >= v:
                continue
            E.obj.wait_ge(self.semh[s], v)
            E.seen[s] = v

    def op(self, eng, fn, reads=(), writes=(), inc=True):
        E = self.E[eng]
        deps = self._deps(reads, writes)
        self._wait(E, deps, skip_self=(eng == "pe"))
        inst = fn(E.obj)
        val = E.cnt + 1
        if inc:
            inst.then_inc(self.semh[E.semid], 1)
            E.cnt = val
        for t in reads:
            if t.r.get(E.semid, 0) < val:
                t.r[E.semid] = val
        for t in writes:
            t.w = (E.semid, val)
            t.r = {}
        self.n_inst += 1
        return inst

    def dma(self, q, out, in_, reads, writes, semt):
        E = self.E[q]
        deps = self._deps(reads, writes)
        self._wait(E, deps)
        if semt.dsem is None:
            kind = "sw" if q == "pool" else "hw"
            fl = self.free_dsems.setdefault(kind, [])
            if fl:
                semt.dsem = fl.pop()
                semt.dcnt = self.dma_cnt[semt.dsem]
            else:
                self.n_dsem += 1
                semt.dsem = self.new_sem("dq%s%d" % (kind, self.n_dsem))
            self.sem_kind[semt.dsem] = kind
        semt.dcnt += 16
        self.dma_cnt[semt.dsem] = semt.dcnt
        E.obj.dma_start(out=out, in_=in_).then_inc(self.semh[semt.dsem], 16)
        for t in reads:
            if t.r.get(semt.dsem, 0) < semt.dcnt:
                t.r[semt.dsem] = semt.dcnt
        for t in writes:
            t.w = (semt.dsem, semt.dcnt)
            t.r = {}
        self.n_inst += 1

    def release(self, t):
        if t.dsem is not None:
            self.free_dsems.setdefault(self.sem_kind[t.dsem], []).append(t.dsem)
            t.dsem = None

    def barrier(self):
        deps = {}
        for e in self.E.values():
            if e.cnt:
                deps[e.semid] = e.cnt
        for sid, c in self.dma_cnt.items():
            if c:
                deps[sid] = c
        for e in self.E.values():
            self._wait(e, deps)


def interleave(gens):
    gens = list(gens)
    while gens:
        alive = []
        for g in gens:
            try:
                next(g)
                alive.append(g)
            except StopIteration:
                pass
        gens = alive


class Ring:
    def __init__(self, tiles):
        self.tiles = tiles
        self.i = 0

    def next(self):
        t = self.tiles[self.i % len(self.tiles)]
        self.i += 1
        return t


class Builder:
    def __init__(self, NB, stop_after=None, dbg_h=False):
        self.NB = NB
        self.stop_after = stop_after
        self.dbg_h = dbg_h
        self.nc = bass.Bass("TRN2", target_bir_lowering=False)
        self.es = contextlib.ExitStack()

    def dram_in(self, name, shape, dt=F32):
        return self.nc.dram_tensor(name, list(shape), dt, kind="ExternalInput").ap()

    def sb(self, st, name, shape, dt):
        self._uid = getattr(self, "_uid", 0) + 1
        t = st.enter_context(self.nc.sbuf_tensor("s%d_%s" % (self._uid, name), list(shape), dt))
        tt = T(t, name)
        if st is not self.es:
            st.callback(self.em.release, tt)
        return tt

    def ps(self, st, name, shape, dt=F32):
        self._uid = getattr(self, "_uid", 0) + 1
        t = st.enter_context(self.nc.psum_tensor("p%d_%s" % (self._uid, name), list(shape), dt))
        return T(t, name)

    def ring(self, st, name, n, shape, dt, psum=False):
        mk = self.ps if psum else self.sb
        return Ring([mk(st, f"{name}{i}", shape, dt) for i in range(n)])

    def build(self):
        nc = self.nc
        NB = self.NB
        R = NB + 1
        with self.es as es:
            em = self.em = Emitter(nc, es)
            x_d = self.dram_in("x", [NB, SEQ, D])
            ctx_d = self.dram_in("ctx", [NB, CTX, D])
            c_d = self.dram_in("c", [R, D])
            wmod_d = self.dram_in("w_mod", [2, D, 9 * D])
            bmod_d = self.dram_in("b_mod", [2, 9 * D])
            normg_d = self.dram_in("norm_g", [2, 3, D])
            wg_d = self.dram_in("ffn_wg", [2, 2, D, DFF])
            wu_d = self.dram_in("ffn_wu", [2, 2, D, DFF])
            wd_d = self.dram_in("ffn_wd", [2, 2, DFF, D])
            ident_d = self.dram_in("ident", [128, 128])
            self.dA = dict(w_in=self.dram_in("ab_w_in", [D, 1536]), qn=self.dram_in("ab_q_norm", [1, 128]),
                           kn=self.dram_in("ab_k_norm", [1, 128]), pw=self.dram_in("pool_w", [4, 128, 128]),
                           psc=self.dram_in("pool_scale", [1, 512]), w_out=self.dram_in("ab_w_out", [D, D]),
                           ropeC=self.dram_in("ropeC", [128, SEQ]), ropeS=self.dram_in("ropeS", [128, SEQ]),
                           rotm=self.dram_in("rotm", [128, 128]), edge=self.dram_in("pool_edge", [128, 4, 32]))
            self.dG = dict(w_in=self.dram_in("gdn_w_in", [D, 4128]), conv=self.dram_in("gdn_conv_w", [1, 3 * 3072]),
                           alog=self.dram_in("alog_rep", [1, 576]), dtb=self.dram_in("dtb_rep", [1, 576]),
                           og=self.dram_in("gdn_o_norm", [1, 128]), w_out=self.dram_in("gdn_w_out", [D, D]),
                           cum=self.dram_in("g_cum", [2, 64, 64]), mpos=self.dram_in("g_mpos", [2, 64, 64]),
                           mneg=self.dram_in("g_mneg", [2, 64, 64]))
            self.YC_d = nc.dram_tensor("YC", [8, 128, SEQ], BF16, kind="Internal").ap()
            self.ident_d = ident_d
            y_d = nc.dram_tensor("y", [NB, SEQ, D], F32, kind="ExternalOutput").ap()
            if self.dbg_h:
                yh_d = nc.dram_tensor("yh", [NB, CTX, D], F32, kind="ExternalOutput").ap()
            XT_d = nc.dram_tensor("XT", [NB, 8, 128, SEQ], F32, kind="Internal").ap()
            HT_d = nc.dram_tensor("HT", [NB, 8, 128, CTX], F32, kind="Internal").ap()
            self.d = dict(x=x_d, ctx=ctx_d, XT=XT_d, HT=HT_d, y=y_d)
            self.XTt = [[T(None, f"XT{b}_{t}") for t in range(4)] for b in range(NB)]
            self.HTt = [T(None, f"HT{b}") for b in range(NB)]
            self.YCt = T(None, "YCt")

            ident = self.sb(es, "ident", [128, 128], F32)
            ones_bf = self.sb(es, "ones_bf", [128, 128], BF16)
            ones_f = self.sb(es, "ones_f", [1, 128], F32)
            modT = self.sb(es, "modT", [128, 2, 72, R], F32)
            modA = self.sb(es, "modA", [128, 2, 3, 8, R], F32)
            modG = self.sb(es, "modG", [128, 2, 3, 8, R], F32)
            gT = self.sb(es, "gT", [128, 2, 3, 8], F32)
            self.ident, self.ones_bf, self.ones_f = ident, ones_bf, ones_f
            self.modT, self.modA, self.modG, self.gT = modT, modA, modG, gT

            em.dma("sp", ident[:], ident_d[:, :], [], [ident], ident)
            em.op("pool", lambda e: e.memset(ones_bf[:], 1.0), [], [ones_bf])
            em.op("pool", lambda e: e.memset(ones_f[:], 1.0), [], [ones_f])

            self.wsem = [T(None, f"ws{i}") for i in range(40)]
            self.eps_t = self.sb(es, "eps_t", [128, 1], F32)
            em.op("pool", lambda e: e.memset(self.eps_t[:], EPS), [], [self.eps_t])

            self.phase_adaln(c_d, wmod_d, bmod_d, normg_d)
            self.phase_tin()
            stages = []
            for i in range(2):
                stages.append(("f1", i))
                stages.append(("mix", i))
                stages.append(("f2", i))
            for (kind, i) in stages:
                if kind == "f1":
                    self.phase_ffn(i, 0, wg_d[i, 0], wu_d[i, 0], wd_d[i, 0], do_h=True)
                elif kind == "f2":
                    self.phase_ffn(i, 2, wg_d[i, 1], wu_d[i, 1], wd_d[i, 1], do_h=(i == 0))
                elif i == 0:
                    self.phase_mix_a()
                else:
                    self.phase_gdn()
                if self.stop_after == (kind, i):
                    break
            self.phase_tout(y_d, yh_d if self.dbg_h else None)
            em.barrier()
        return nc

    def phase_adaln(self, c_d, wmod_d, bmod_d, normg_d):
        em, nc, NB = self.em, self.nc, self.NB
        R = NB + 1
        with contextlib.ExitStack() as st:
            c_sb = self.sb(st, "c_sb", [R, D], F32)
            cs_sb = self.sb(st, "cs_sb", [R, D], F32)
            cT = self.sb(st, "cT", [128, 8, R], F32)
            bmr = self.ring(st, "bm_row", 2, [1, 1152], F32)
            gr = self.sb(st, "g_row", [1, 6 * D], F32)
            wring = self.ring(st, "wm", 2, [128, 8, 1152], F32)
            pring = self.ring(st, "ps_ad", 2, [128, 512], F32, psum=True)
            em.dma("sp", c_sb[:], c_d[:, :], [], [c_sb], c_sb)
            em.dma("sp", gr[:], normg_d.rearrange("(o l) s n -> o (l s n)", o=1), [], [gr], gr)
            em.op("act", lambda e: e.activation(out=cs_sb[:], in_=c_sb[:], func=AF.Silu), [c_sb], [cs_sb])
            pt = pring.next()
            for k in range(8):
                em.op("pe", lambda e, k=k: e.transpose(out=pt[:, k * R:(k + 1) * R], in_=cs_sb[:, k * 128:(k + 1) * 128],
                                                       identity=self.ident[0:R, 0:R]),
                      [cs_sb, self.ident], [pt], inc=(k == 7))
            em.op("dve", lambda e: e.tensor_copy(out=cT[:].rearrange("p k r -> p (k r)"), in_=pt[:, 0:8 * R]), [pt], [cT])
            pt = pring.next()
            for j in range(48):
                em.op("pe", lambda e, j=j: e.matmul(out=pt[:, j:j + 1], lhsT=gr[0:1, j * 128:(j + 1) * 128],
                                                   rhs=self.ones_f[0:1, 0:1], start=True, stop=True),
                      [gr, self.ones_f], [pt], inc=(j == 47))
            em.op("dve", lambda e: e.tensor_copy(out=self.gT[:].rearrange("p l s c -> p (l s c)"), in_=pt[:, 0:48]),
                  [pt], [self.gT])
            for l in range(2):
                for mg in range(8):
                    wt = wring.next()
                    em.dma("sp", wt[:], wmod_d[l, :, mg * 1152:(mg + 1) * 1152].rearrange("(k p) n -> p k n", p=128),
                           [], [wt], wt)
                    bm = bmr.next()
                    em.dma("sp", bm[:], bmod_d[l:l + 1, mg * 1152:(mg + 1) * 1152], [], [bm], bm)
                    pt = pring.next()
                    for mm in range(9):
                        m = mg * 9 + mm
                        o = pt[:, mm * R:(mm + 1) * R]
                        for k in range(8):
                            em.op("pe", lambda e, k=k, mm=mm, o=o: e.matmul(out=o, lhsT=wt[:, k, mm * 128:(mm + 1) * 128],
                                                                         rhs=cT[:, k, :], start=(k == 0), stop=False),
                                  [wt, cT], [pt], inc=False)
                        em.op("pe", lambda e, m=m, o=o, l=l: e.matmul(out=o, lhsT=bm[0:1, mm * 128:(mm + 1) * 128],
                                                                   rhs=self.ones_f[0:1, 0:R], start=False, stop=True),
                              [bm, self.ones_f], [pt], inc=(mm == 8))
                    em.op("dve", lambda e, l=l, mg=mg: e.tensor_copy(
                        out=self.modT[:, l, mg * 9:(mg + 1) * 9, :].rearrange("p m r -> p (m r)"), in_=pt[:, 0:9 * R]),
                        [pt], [self.modT])
            for l in range(2):
                for s in range(3):
                    for c in range(8):
                        em.op("dve", lambda e, l=l, s=s, c=c: e.tensor_scalar(
                            out=self.modA[:, l, s, c, :], in0=self.modT[:, l, (3 * s + 1) * 8 + c, :],
                            scalar1=1.0, scalar2=self.gT[:, l, s, c:c + 1], op0=ALU.add, op1=ALU.mult),
                            [self.modT, self.gT], [self.modA])
                        gs = 1.0 if s == 1 else 0.5
                        em.op("dve", lambda e, l=l, s=s, c=c, gs=gs: e.tensor_scalar(
                            out=self.modG[:, l, s, c, :], in0=self.modT[:, l, (3 * s + 2) * 8 + c, :],
                            scalar1=gs, scalar2=None, op0=ALU.mult),
                            [self.modT], [self.modG])
            em.barrier()

    def modB(self, l, s, c, r):
        return self.modT[:, l, (3 * s) * 8 + c, r:r + 1]

    def tiles(self, do_h=True):
        out = []
        for b in range(self.NB):
            if do_h:
                out.append(("h", b, 0, CTX, self.d["HT"][b].rearrange("c p n -> p c n"), self.HTt[b], self.NB))
            for t in range(4):
                out.append(("x", b, t, 512, self.d["XT"][b, :, :, t * 512:(t + 1) * 512].rearrange("c p n -> p c n"),
                            self.XTt[b][t], b))
        return out

    def phase_tin(self):
        em = self.em
        with contextlib.ExitStack() as st:
            inr = self.ring(st, "tin_in", 2, [128, 4, D], F32)
            outr = self.ring(st, "tin_out", 2, [128, 8, 512], F32)
            pr = self.ring(st, "ps_tin", 4, [128, 512], F32, psum=True)
            for (kind, b, t, n, dap, trk, r) in self.tiles(True):
                nsub = n // 128
                it = inr.next()
                src = self.d["x"][b, t * 512:(t + 1) * 512, :] if kind == "x" else self.d["ctx"][b]
                em.dma("sp", it[:, 0:nsub, :], src.rearrange("(j p) d -> p j d", p=128), [], [it], it)
                ot = outr.next()
                for c in range(8):
                    pt = pr.next()
                    for j in range(nsub):
                        em.op("pe", lambda e, c=c, j=j, pt=pt, it=it: e.transpose(
                            out=pt[:, j * 128:(j + 1) * 128], in_=it[:, j, c * 128:(c + 1) * 128], identity=self.ident[:]),
                            [it, self.ident], [pt], inc=(j == nsub - 1))
                    eng = "dve" if c % 2 == 0 else "act"
                    if eng == "dve":
                        em.op("dve", lambda e, c=c, pt=pt, ot=ot: e.tensor_copy(out=ot[:, c, 0:n], in_=pt[:, 0:n]), [pt], [ot])
                    else:
                        em.op("act", lambda e, c=c, pt=pt, ot=ot: e.copy(out=ot[:, c, 0:n], in_=pt[:, 0:n]), [pt], [ot])
                em.dma("sp", dap, ot[:, :, 0:n], [ot], [trk], ot)
            em.barrier()

    def phase_tout(self, y_d, yh_d):
        em = self.em
        with contextlib.ExitStack() as st:
            inr = self.ring(st, "to_in", 2, [128, 8, 512], F32)
            outr = self.ring(st, "to_out", 2, [128, 4, D], F32)
            pr = self.ring(st, "ps_to", 4, [128, 512], F32, psum=True)
            for (kind, b, t, n, dap, trk, r) in self.tiles(yh_d is not None):
                nsub = n // 128
                it = inr.next()
                em.dma("sp", it[:, :, 0:n], dap, [trk], [it], it)
                ot = outr.next()
                for j in range(nsub):
                    for hh in range(2):
                        pt = pr.next()
                        for cc in range(4):
                            c = hh * 4 + cc
                            em.op("pe", lambda e, c=c, cc=cc, j=j, pt=pt, it=it: e.transpose(
                                out=pt[:, cc * 128:(cc + 1) * 128], in_=it[:, c, j * 128:(j + 1) * 128], identity=self.ident[:]),
                                [it, self.ident], [pt], inc=(cc == 3))
                        if hh == 0:
                            em.op("dve", lambda e, j=j, pt=pt, ot=ot: e.tensor_copy(out=ot[:, j, 0:512], in_=pt[:]), [pt], [ot])
                        else:
                            em.op("act", lambda e, j=j, pt=pt, ot=ot: e.copy(out=ot[:, j, 512:1024], in_=pt[:]), [pt], [ot])
                dst = y_d[b, t * 512:(t + 1) * 512, :] if kind == "x" else yh_d[b]
                em.dma("sp", dst.rearrange("(j p) d -> p j d", p=128), ot[:, 0:nsub, :], [ot], [], ot)
            em.barrier()

    def norm_mod(self, st_rings, xt, n, l, s, r, xn):
        em = self.em
        sqr, ssr, sdr, rsr, tmr = st_rings
        ss = ssr.next()
        for c in range(8):
            sq = sqr.next()
            em.op("act", lambda e, c=c, sq=sq: e.activation(out=sq[:, 0:n], in_=xt[:, c, 0:n], func=AF.Square), [xt], [sq])
            em.op("pe", lambda e, c=c, sq=sq: e.matmul(out=ss[:, 0:n], lhsT=self.ones_bf[:], rhs=sq[:, 0:n],
                                                     start=(c == 0), stop=(c == 7)),
                  [sq, self.ones_bf], [ss])
        sd = sdr.next()
        em.op("act", lambda e: e.activation(out=sd[:, 0:n], in_=ss[:, 0:n], func=AF.Ln, scale=1.0 / D, bias=self.eps_t[:, 0:1]),
              [ss, self.eps_t], [sd])
        rs = rsr.next()
        em.op("act", lambda e: e.activation(out=rs[:, 0:n], in_=sd[:, 0:n], func=AF.Exp, scale=-0.5), [sd], [rs])
        for c in range(8):
            tm = tmr.next()
            em.op("dve", lambda e, c=c, tm=tm: e.scalar_tensor_tensor(
                out=tm[:, 0:n], in0=xt[:, c, 0:n], scalar=self.modA[:, l, s, c, r:r + 1], in1=rs[:, 0:n],
                op0=ALU.mult, op1=ALU.mult), [xt, self.modA, rs], [tm])
            em.op("act", lambda e, c=c, tm=tm: e.activation(
                out=xn[:, c, 0:n], in_=tm[:, 0:n], func=AF.Identity, bias=self.modB(l, s, c, r), scale=1.0),
                [tm, self.modT], [xn])

    def phase_ffn(self, l, s, wg, wu, wd, do_h):
        em = self.em
        with contextlib.ExitStack() as st:
            AR = st.enter_context(self.nc.sbuf_tensor("arena_f%d%d" % (l, s), [128, 66 * 1024], BF16))
            WG = [T(AR[:, k * DFF:(k + 1) * DFF], f"wg{k}") for k in range(8)]
            WU = [T(AR[:, 22528 + k * DFF: 22528 + (k + 1) * DFF], f"wu{k}") for k in range(8)]
            WD = [T(AR[:, 45056 + f * D: 45056 + (f + 1) * D], f"wd{f}") for f in range(NF)]
            for k in range(8):
                em.dma("pool", WG[k][:], wg[k * 128:(k + 1) * 128, :], [], [WG[k]], self.wsem[k])
                em.dma("pool", WU[k][:], wu[k * 128:(k + 1) * 128, :], [], [WU[k]], self.wsem[8 + k])
            for f in range(NF):
                em.dma("pool", WD[f][:], wd[f * 128:(f + 1) * 128, :], [], [WD[f]], self.wsem[16 + f])
            xtr = self.ring(st, "f_xt", 1, [128, 8, 512], F32)
            xrr = self.ring(st, "f_xr", 4, [128, 512], F32)
            xnr = self.ring(st, "f_xn", 1, [128, 8, 512], BF16)
            hid = [self.sb(st, f"f_hid{f}", [128, 512], BF16) for f in range(NF)]
            rings = (self.ring(st, "f_sq", 2, [128, 512], BF16),
                     self.ring(st, "ps_ss", 2, [128, 512], F32, psum=True),
                     self.ring(st, "f_sd", 1, [128, 512], F32),
                     self.ring(st, "f_rs", 1, [128, 512], F32),
                     self.ring(st, "f_tm", 2, [128, 512], F32))
            sgr = self.ring(st, "f_sg", 2, [128, 512], F32)
            pg = self.ring(st, "ps_g", 2, [128, 512], F32, psum=True)
            pu = self.ring(st, "ps_u", 2, [128, 512], F32, psum=True)
            py = self.ring(st, "ps_y", 2, [128, 512], F32, psum=True)
            tls = self.tiles(do_h)

            def load_norm(i):
                (kind, b, t, n, dap, trk, r) = tls[i]
                xt = xtr.next()
                em.dma("sp", xt[:, :, 0:n], dap, [trk], [xt], xt)
                xn = xnr.next()
                self.norm_mod(rings, xt, n, l, s, r, xn)
                return xn
            xn_next = load_norm(0)
            for i, (kind, b, t, n, dap, trk, r) in enumerate(tls):
                xn = xn_next
                for f in range(NF):
                    g_ps = pg.next()
                    u_ps = pu.next()
                    for k in range(8):
                        em.op("pe", lambda e, k=k, f=f, g_ps=g_ps: e.matmul(
                            out=g_ps[:, 0:n], lhsT=WG[k][:, f * 128:(f + 1) * 128], rhs=xn[:, k, 0:n],
                            start=(k == 0), stop=(k == 7)), [WG[k], xn], [g_ps], inc=(k == 7))
                    for k in range(8):
                        em.op("pe", lambda e, k=k, f=f, u_ps=u_ps: e.matmul(
                            out=u_ps[:, 0:n], lhsT=WU[k][:, f * 128:(f + 1) * 128], rhs=xn[:, k, 0:n],
                            start=(k == 0), stop=(k == 7)), [WU[k], xn], [u_ps], inc=(k == 7))
                    sg = sgr.next()
                    em.op("act", lambda e, sg=sg, g_ps=g_ps: e.activation(out=sg[:, 0:n], in_=g_ps[:, 0:n], func=AF.Silu),
                          [g_ps], [sg])
                    em.op("dve", lambda e, sg=sg, u_ps=u_ps, f=f: e.tensor_tensor(
                        out=hid[f][:, 0:n], in0=u_ps[:, 0:n], in1=sg[:, 0:n], op=ALU.mult), [u_ps, sg], [hid[f]])
                if i + 1 < len(tls):
                    xn_next = load_norm(i + 1)
                for dd in range(8):
                    xr = xrr.next()
                    em.dma("sp", xr[:, 0:n], dap[:, dd, :], [trk], [xr], xr)
                    y_ps = py.next()
                    for f in range(NF):
                        em.op("pe", lambda e, f=f, dd=dd, y_ps=y_ps: e.matmul(
                            out=y_ps[:, 0:n], lhsT=WD[f][:, dd * 128:(dd + 1) * 128], rhs=hid[f][:, 0:n],
                            start=(f == 0), stop=(f == NF - 1)), [WD[f], hid[f]], [y_ps], inc=(f == NF - 1))
                    em.op("dve", lambda e, dd=dd, y_ps=y_ps, xr=xr: e.scalar_tensor_tensor(
                        out=xr[:, 0:n], in0=y_ps[:, 0:n], scalar=self.modG[:, l, s, dd, r:r + 1], in1=xr[:, 0:n],
                        op0=ALU.mult, op1=ALU.add), [y_ps, self.modG, xr], [xr])
                    em.dma("sp", dap[:, dd, :], xr[:, 0:n], [xr], [trk], xr)
            em.barrier()

    def phase_mix_a(self):
        em, nc, NB = self.em, self.nc, self.NB
        dA = self.dA
        l, s = 0, 1
        SC = 128 ** -0.5
        NK = CTX + SEQ
        with contextlib.ExitStack() as so:
            vecA = self.sb(so, "a_vec", [128, 8], F32)
            KT = self.sb(so, "a_KT", [128, 2, NK], BF16)
            Vt = self.sb(so, "a_V", [128, 18, 256], BF16)
            QT = self.sb(so, "a_QT", [128, 4, NK], BF16)
            PL = self.sb(so, "a_PL", [128, 4, NK], BF16)
            rotm = self.sb(so, "a_rotm", [128, 128], BF16)
            edge = self.sb(so, "a_edge", [128, 4, 32], F32)
            em.dma("pool", rotm[:], dA["rotm"][:, :], [], [rotm], rotm)
            em.dma("sp", edge[:], dA["edge"][:, :, :], [], [edge], edge)
            with contextlib.ExitStack() as st:
                row = self.sb(st, "a_row", [1, 768], F32)
                pr = self.ring(st, "ps_av", 1, [128, 512], F32, psum=True)
                em.dma("sp", row[:, 0:128], dA["qn"][:, :], [], [row], row)
                em.dma("sp", row[:, 128:256], dA["kn"][:, :], [], [row], row)
                em.dma("sp", row[:, 256:768], dA["psc"][:, :], [], [row], row)
                pt = pr.next()
                for j in range(6):
                    em.op("pe", lambda e, j=j: e.matmul(out=pt[:, j:j + 1], lhsT=row[0:1, j * 128:(j + 1) * 128],
                                                       rhs=self.ones_f[0:1, 0:1], start=True, stop=True),
                          [row, self.ones_f], [pt], inc=(j == 5))
                em.op("dve", lambda e: e.tensor_copy(out=vecA[:, 0:6], in_=pt[:, 0:6]), [pt], [vecA])
                em.barrier()
            for b in range(NB):
                with contextlib.ExitStack() as st:
                    AR = st.enter_context(nc.sbuf_tensor("arena_a%d" % b, [128, 8 * 1536], BF16))
                    Win = [T(AR[:, k * 1536:(k + 1) * 1536], f"win{k}") for k in range(8)]
                    for k in range(8):
                        em.dma("pool", Win[k][:], dA["w_in"][k * 128:(k + 1) * 128, :], [], [Win[k]], self.wsem[k])
                    Ux = self.sb(st, "a_Ux", [128, 4, SEQ + 32], F32)
                    Uh = self.sb(st, "a_Uh", [128, 4, CTX + 32], F32)
                    Ta = self.sb(st, "a_Ta", [128, SEQ + 32], F32)
                    Tb = self.sb(st, "a_Tb", [128, SEQ + 32], F32)
                    for (U, N) in ((Ux, SEQ), (Uh, CTX)):
                        em.op("pool", lambda e, U=U: e.memset(U[:, :, 0:16], 0.0), [], [U])
                        em.op("pool", lambda e, U=U, N=N: e.memset(U[:, :, 16 + N:32 + N], 0.0), [], [U])
                    xtr = self.ring(st, "a_xt", 1, [128, 8, 512], F32)
                    xnr = self.ring(st, "a_xn", 1, [128, 8, 512], BF16)
                    rings = (self.ring(st, "a_sq", 2, [128, 512], BF16),
                             self.ring(st, "ps_ass", 2, [128, 512], F32, psum=True),
                             self.ring(st, "a_sd", 1, [128, 512], F32),
                             self.ring(st, "a_rs", 1, [128, 512], F32),
                             self.ring(st, "a_tm", 2, [128, 512], F32))
                    pP = self.ring(st, "ps_aP", 2, [128, 512], F32, psum=True)
                    pS = self.ring(st, "ps_aS", 2, [128, 512], F32, psum=True)
                    pV = self.ring(st, "ps_aV", 2, [128, 512], F32, psum=True)
                    sq2 = self.ring(st, "a_sq2", 2, [128, 512], BF16)
                    sd2 = self.ring(st, "a_sd2", 2, [128, 512], F32)
                    rs2 = self.ring(st, "a_rs2", 2, [128, 512], F32)
                    qnr = self.ring(st, "a_qn", 2, [128, 512], BF16)
                    t1r = self.ring(st, "a_t1", 2, [128, 512], F32)
                    t2r = self.ring(st, "a_t2", 2, [128, 512], F32)
                    rcr = self.ring(st, "a_rc", 2, [128, 512], F32)
                    rsr = self.ring(st, "a_rsn", 2, [128, 512], F32)
                    tl = [("h", b, 0, CTX, self.d["HT"][b].rearrange("c p n -> p c n"), self.HTt[b], NB)]
                    for t in range(4):
                        tl.append(("x", b, t, 512, self.d["XT"][b, :, :, t * 512:(t + 1) * 512].rearrange("c p n -> p c n"),
                                   self.XTt[b][t], b))
                    for (kind, _b, t, n, dap, trk, r) in tl:
                        o = 0 if kind == "h" else CTX + t * 512
                        xt = xtr.next()
                        em.dma("sp", xt[:, :, 0:n], dap, [trk], [xt], xt)
                        xn = xnr.next()
                        self.norm_mod(rings, xt, n, l, s, r, xn)
                        if kind == "x":
                            rc = rcr.next()
                            rsn = rsr.next()
                            em.dma("sp", rc[:, 0:n], dA["ropeC"][:, t * 512:(t + 1) * 512], [], [rc], rc)
                            em.dma("sp", rsn[:, 0:n], dA["ropeS"][:, t * 512:(t + 1) * 512], [], [rsn], rsn)
                        for j in range(6):
                            P = pP.next()
                            for k in range(8):
                                em.op("pe", lambda e, k=k, j=j, P=P: e.matmul(
                                    out=P[:, 0:n], lhsT=Win[k][:, j * 128:(j + 1) * 128], rhs=xn[:, k, 0:n],
                                    start=(k == 0), stop=(k == 7)), [Win[k], xn], [P], inc=(k == 7))
                            sq = sq2.next()
                            em.op("act", lambda e, sq=sq, P=P: e.activation(out=sq[:, 0:n], in_=P[:, 0:n], func=AF.Square), [P], [sq])
                            S_ = pS.next()
                            em.op("pe", lambda e, sq=sq, S_=S_: e.matmul(out=S_[:, 0:n], lhsT=self.ones_bf[:], rhs=sq[:, 0:n],
                                                                     start=True, stop=True), [sq, self.ones_bf], [S_])
                            sd = sd2.next()
                            em.op("act", lambda e, sd=sd, S_=S_: e.activation(out=sd[:, 0:n], in_=S_[:, 0:n], func=AF.Ln,
                                                                            scale=1.0 / 128, bias=self.eps_t[:, 0:1]),
                                  [S_, self.eps_t], [sd])
                            rs = rs2.next()
                            em.op("act", lambda e, sd=sd, rs=rs: e.activation(out=rs[:, 0:n], in_=sd[:, 0:n], func=AF.Exp, scale=-0.5), [sd], [rs])
                            gcol = vecA[:, 0:1] if j < 4 else vecA[:, 1:2]
                            dstT = QT if j < 4 else KT
                            dst = dstT[:, j if j < 4 else j - 4, o:o + n]
                            if kind == "h":
                                em.op("dve", lambda e, P=P, rs=rs, gcol=gcol, dst=dst: e.scalar_tensor_tensor(
                                    out=dst, in0=P[:, 0:n], scalar=gcol, in1=rs[:, 0:n], op0=ALU.mult, op1=ALU.mult),
                                    [P, vecA, rs], [dstT])
                            else:
                                qn = qnr.next()
                                em.op("dve", lambda e, P=P, rs=rs, gcol=gcol, qn=qn: e.scalar_tensor_tensor(
                                    out=qn[:, 0:n], in0=P[:, 0:n], scalar=gcol, in1=rs[:, 0:n], op0=ALU.mult, op1=ALU.mult),
                                    [P, vecA, rs], [qn])
                                R_ = pS.next()
                                em.op("pe", lambda e, qn=qn, R_=R_: e.matmul(out=R_[:, 0:n], lhsT=rotm[:], rhs=qn[:, 0:n],
                                                                         start=True, stop=True), [qn, rotm], [R_])
                                t1 = t1r.next()
                                em.op("pool", lambda e, t1=t1, qn=qn, rc=rc: e.tensor_tensor(
                                    out=t1[:, 0:n], in0=qn[:, 0:n], in1=rc[:, 0:n], op=ALU.mult), [qn, rc], [t1])
                                t2 = t2r.next()
                                em.op("dve", lambda e, t2=t2, R_=R_, rsn=rsn: e.tensor_tensor(
                                    out=t2[:, 0:n], in0=R_[:, 0:n], in1=rsn[:, 0:n], op=ALU.mult), [R_, rsn], [t2])
                                em.op("pool", lambda e, t1=t1, t2=t2, dst=dst: e.tensor_tensor(
                                    out=dst, in0=t1[:, 0:n], in1=t2[:, 0:n], op=ALU.add), [t1, t2], [dstT])
                        for jb in range(n // 128):
                            blk = (o // 128) + jb
                            Pv = pV.next()
                            for k in range(8):
                                em.op("pe", lambda e, k=k, jb=jb, Pv=Pv: e.matmul(
                                    out=Pv[:, 0:256], lhsT=xn[:, k, jb * 128:(jb + 1) * 128], rhs=Win[k][:, 768:1024],
                                    start=(k == 0), stop=(k == 7)), [Win[k], xn], [Pv], inc=(k == 7))
                            em.op("act", lambda e, Pv=Pv, blk=blk: e.copy(out=Vt[:, blk, :], in_=Pv[:, 0:256]), [Pv], [Vt])
                        U = Uh if kind == "h" else Ux
                        uo = 16 + (0 if kind == "h" else t * 512)
                        for g in range(4):
                            Pu = pV.next()
                            for k in range(8):
                                em.op("pe", lambda e, k=k, g=g, Pu=Pu: e.matmul(
                                    out=Pu[:, 0:n], lhsT=Win[k][:, 1024 + g * 128:1024 + (g + 1) * 128], rhs=xn[:, k, 0:n],
                                    start=(k == 0), stop=(k == 7)), [Win[k], xn], [Pu], inc=(k == 7))
                            em.op("act", lambda e, Pu=Pu, g=g, U=U, uo=uo: e.copy(out=U[:, g, uo:uo + n], in_=Pu[:, 0:n]), [Pu], [U])
                    for (U, N, o) in ((Uh, CTX, 0), (Ux, SEQ, CTX)):
                        W_ = N + 32
                        for g, w in enumerate((2, 4, 8, 16)):
                            src = None
                            bufs = [Ta, Tb]
                            cur = bufs[0]
                            em.op("pool", lambda e, cur=cur, U=U, g=g: e.tensor_tensor(
                                out=cur[:, 1:W_], in0=U[:, g, 0:W_ - 1], in1=U[:, g, 1:W_], op=ALU.add), [U], [cur])
                            lo, hi, sh = 1, W_, 1
                            nb_ = 1
                            while (1 << nb_) < w + 0 and (2 << (nb_ - 1)) < w:
                                nxt = bufs[nb_ % 2]
                                lo2, hi2 = lo + sh, hi - sh
                                em.op("pool", lambda e, cur=cur, nxt=nxt, lo2=lo2, hi2=hi2, sh=sh: e.tensor_tensor(
                                    out=nxt[:, lo2:hi2], in0=cur[:, lo2 - sh:hi2 - sh], in1=cur[:, lo2 + sh:hi2 + sh], op=ALU.add),
                                    [cur], [nxt])
                                cur, lo, hi, sh = nxt, lo2, hi2, sh * 2
                                nb_ += 1
                            em.op("pool", lambda e, cur=cur, g=g: e.tensor_tensor(
                                out=cur[:, 16:32], in0=cur[:, 16:32], in1=edge[:, g, 0:16], op=ALU.mult), [cur, edge], [cur])
                            em.op("pool", lambda e, cur=cur, g=g, N=N: e.tensor_tensor(
                                out=cur[:, N:N + 16], in0=cur[:, N:N + 16], in1=edge[:, g, 16:32], op=ALU.mult), [cur, edge], [cur])
                            em.op("dve", lambda e, cur=cur, g=g, U=U, N=N, o=o, w=w: e.scalar_tensor_tensor(
                                out=PL[:, g, o:o + N], in0=cur[:, 16:16 + N], scalar=1.0 / w, in1=U[:, g, 16:16 + N],
                                op0=ALU.mult, op1=ALU.subtract), [cur, U], [PL])
                    em.barrier()
                with contextlib.ExitStack() as st:
                    AR = st.enter_context(nc.sbuf_tensor("arena_b%d" % b, [128, 8 * 1024 + 512], BF16))
                    Wout = [T(AR[:, k * 1024:(k + 1) * 1024], f"wout{k}") for k in range(8)]
                    Pw = T(AR[:, 8192:8704], "poolw")
                    for k in range(8):
                        em.dma("pool", Wout[k][:], dA["w_out"][k * 128:(k + 1) * 128, :], [], [Wout[k]], self.wsem[k])
                    em.dma("pool", Pw[:].rearrange("p (g d) -> p g d", g=4), dA["pw"].rearrange("g c d -> c g d"), [], [Pw], self.wsem[8])
                    cat = [self.sb(st, f"a_cat{c}", [128, 512], BF16) for c in range(8)]
                    ptr = self.ring(st, "a_pt", 3, [128, 512], BF16)
                    rdr = self.ring(st, "a_rd", 2, [128, 512], F32)
                    xrr = self.ring(st, "a_xr", 4, [128, 512], F32)
                    pST = self.ring(st, "ps_bS", 2, [128, 512], F32, psum=True)
                    pO = self.ring(st, "ps_bO", 2, [128, 512], F32, psum=True)
                    pD = self.ring(st, "ps_bD", 2, [128, 512], F32, psum=True)
                    pY = self.ring(st, "ps_bY", 2, [128, 512], F32, psum=True)
                    for (kind, _b, t, n, dap, trk, r) in tl:
                        o = 0 if kind == "h" else CTX + t * 512
                        nblk = 2 if kind == "h" else 18
                        for h in range(4):
                            kvh = h // 2
                            O_ = pO.next()
                            D_ = pD.next()

                            def emit_st(blk, h=h, kvh=kvh):
                                S_ = pST.next()
                                em.op("pe", lambda e: e.matmul(out=S_[:, 0:n], lhsT=KT[:, kvh, blk * 128:(blk + 1) * 128],
                                                               rhs=QT[:, h, o:o + n], start=True, stop=True), [KT, QT], [S_])
                                return S_
                            S_next = emit_st(0)
                            for blk in range(nblk):
                                S_ = S_next
                                if blk + 1 < nblk:
                                    S_next = emit_st(blk + 1)
                                pt = ptr.next()
                                em.op("act", lambda e, pt=pt, S_=S_: e.activation(out=pt[:, 0:n], in_=S_[:, 0:n], func=AF.Exp, scale=SC),
                                      [S_], [pt])
                                em.op("pe", lambda e, pt=pt, blk=blk, kvh=kvh, O_=O_: e.matmul(
                                    out=O_[:, 0:n], lhsT=Vt[:, blk, kvh * 128:(kvh + 1) * 128], rhs=pt[:, 0:n],
                                    start=(blk == 0), stop=(blk == nblk - 1)), [Vt, pt], [O_])
                                em.op("pe", lambda e, pt=pt, blk=blk, D_=D_: e.matmul(
                                    out=D_[:, 0:n], lhsT=self.ones_bf[:], rhs=pt[:, 0:n],
                                    start=(blk == 0), stop=(blk == nblk - 1)), [self.ones_bf, pt], [D_])
                            rd = rdr.next()
                            em.op("dve", lambda e, rd=rd, D_=D_: e.reciprocal(out=rd[:, 0:n], in_=D_[:, 0:n]), [D_], [rd])
                            em.op("dve", lambda e, rd=rd, O_=O_, h=h: e.tensor_tensor(
                                out=cat[h][:, 0:n], in0=O_[:, 0:n], in1=rd[:, 0:n], op=ALU.mult), [O_, rd], [cat[h]])
                        for g in range(4):
                            Y_ = pY.next()
                            em.op("pe", lambda e, g=g, Y_=Y_: e.matmul(out=Y_[:, 0:n], lhsT=Pw[:, g * 128:(g + 1) * 128],
                                                                     rhs=PL[:, g, o:o + n], start=True, stop=True), [Pw, PL], [Y_])
                            em.op("act", lambda e, g=g, Y_=Y_: e.activation(out=cat[4 + g][:, 0:n], in_=Y_[:, 0:n], func=AF.Copy,
                                                                          scale=vecA[:, 2 + g:3 + g]), [Y_, vecA], [cat[4 + g]])
                        for dd in range(8):
                            xr = xrr.next()
                            em.dma("sp", xr[:, 0:n], dap[:, dd, :], [trk], [xr], xr)
                            Y_ = pY.next()
                            for c in range(8):
                                em.op("pe", lambda e, c=c, dd=dd, Y_=Y_: e.matmul(
                                    out=Y_[:, 0:n], lhsT=Wout[c][:, dd * 128:(dd + 1) * 128], rhs=cat[c][:, 0:n],
                                    start=(c == 0), stop=(c == 7)), [Wout[c], cat[c]], [Y_], inc=(c == 7))
                            em.op("dve", lambda e, dd=dd, Y_=Y_, xr=xr: e.scalar_tensor_tensor(
                                out=xr[:, 0:n], in0=Y_[:, 0:n], scalar=self.modG[:, l, s, dd, r:r + 1], in1=xr[:, 0:n],
                                op0=ALU.mult, op1=ALU.add), [Y_, self.modG, xr], [xr])
                            em.dma("sp", dap[:, dd, :], xr[:, 0:n], [xr], [trk], xr)
                    em.barrier()

    def phase_gdn(self):
        em, nc, NB = self.em, self.nc, self.NB
        dG = self.dG
        l, s = 1, 1
        NT = CTX + SEQ
        NC_ = NT // 64
        NCH = CTX // 64
        mm = lambda e, **kw: e.matmul(**kw)
        with contextlib.ExitStack() as so:
            cum = self.sb(so, "g_cum", [64, 2, 64], F32)
            mpos = self.sb(so, "g_mpos", [64, 2, 64], F32)
            mneg = self.sb(so, "g_mneg", [64, 2, 64], F32)
            identb = self.sb(so, "g_identb", [128, 128], BF16)
            onesf = self.sb(so, "g_onesf", [64, 128], F32)
            one_t = self.sb(so, "g_one", [128, 1], F32)
            CW = self.sb(so, "g_cw", [128, 72], F32)
            OG = self.sb(so, "g_og", [128, 1], F32)
            DTB = self.sb(so, "g_dtb", [64, 576], F32)
            NEGA = self.sb(so, "g_nega", [64, 576], F32)
            for d in range(2):
                em.dma("sp", cum[:, d, :], dG["cum"][d], [], [cum], cum)
                em.dma("sp", mpos[:, d, :], dG["mpos"][d], [], [mpos], mpos)
                em.dma("sp", mneg[:, d, :], dG["mneg"][d], [], [mneg], mneg)
            em.dma("pool", identb[:], self.ident_d[:, :], [], [identb], identb)
            em.op("pool", lambda e: e.memset(onesf[:], 1.0), [], [onesf])
            em.op("pool", lambda e: e.memset(one_t[:], 1.0), [], [one_t])
            with contextlib.ExitStack() as st:
                row = self.sb(st, "g_row", [1, 9216 + 128], F32)
                r2 = self.sb(st, "g_row2", [1, 1152], F32)
                pr = self.ring(st, "ps_gv", 2, [128, 512], F32, psum=True)
                em.dma("sp", row[:, 0:9216], dG["conv"][:, :], [], [row], row)
                em.dma("sp", row[:, 9216:9344], dG["og"][:, :], [], [row], row)
                em.dma("sp", r2[:, 0:576], dG["dtb"][:, :], [], [r2], r2)
                em.dma("sp", r2[:, 576:1152], dG["alog"][:, :], [], [r2], r2)
                pt = pr.next()
                for j in range(73):
                    em.op("pe", lambda e, j=j: e.matmul(out=pt[:, j:j + 1], lhsT=row[0:1, j * 128:(j + 1) * 128],
                                                       rhs=self.ones_f[0:1, 0:1], start=True, stop=True),
                          [row, self.ones_f], [pt], inc=(j == 72))
                em.op("dve", lambda e: e.tensor_copy(out=CW[:], in_=pt[:, 0:72]), [pt], [CW])
                em.op("dve", lambda e: e.tensor_copy(out=OG[:], in_=pt[:, 72:73]), [pt], [OG])
                for half, dst in ((0, DTB), (1, NEGA)):
                    for q in range(2):
                        pt = pr.next()
                        em.op("pe", lambda e, half=half, q=q, pt=pt: e.matmul(
                            out=pt[0:64, 0:288], lhsT=self.ones_f[0:1, 0:64],
                            rhs=r2[0:1, half * 576 + q * 288: half * 576 + (q + 1) * 288], start=True, stop=True),
                            [r2, self.ones_f], [pt])
                        if half == 0:
                            em.op("dve", lambda e, q=q, pt=pt: e.tensor_copy(out=DTB[:, q * 288:(q + 1) * 288], in_=pt[0:64, 0:288]), [pt], [DTB])
                        else:
                            em.op("act", lambda e, q=q, pt=pt: e.activation(out=NEGA[:, q * 288:(q + 1) * 288], in_=pt[0:64, 0:288], func=AF.Exp), [pt], [NEGA])
                em.op("dve", lambda e: e.tensor_scalar(out=NEGA[:], in0=NEGA[:], scalar1=-1.0, scalar2=None, op0=ALU.mult), [NEGA], [NEGA])
                em.barrier()

            for b in range(NB):
                with contextlib.ExitStack() as sb_:
                    XN = self.sb(sb_, "g_XN", [128, 8, NT], BF16)
                    Gt = self.sb(sb_, "g_Gt", [64, 576], F32)
                    Bt = self.sb(sb_, "g_Bt", [64, 576], F32)
                    nBt = self.sb(sb_, "g_nBt", [64, 576], F32)
                    GCt = self.sb(sb_, "g_GCt", [64, 576], F32)
                    ETL = self.sb(sb_, "g_ETL", [64, 576], F32)
                    NBEG = self.sb(sb_, "g_NBEG", [64, 576], F32)
                    EGT = self.sb(sb_, "g_EGT", [128, 576], F32)
                    v4 = lambda t: t[:].rearrange("p (c d h) -> p c d h", c=NC_, d=2)
                    with contextlib.ExitStack() as st:
                        xtr = self.ring(st, "g_xt", 1, [128, 8, 512], F32)
                        rings = (self.ring(st, "g_sq", 2, [128, 512], BF16),
                                 self.ring(st, "ps_gss", 2, [128, 512], F32, psum=True),
                                 self.ring(st, "g_sd", 1, [128, 512], F32),
                                 self.ring(st, "g_rs", 1, [128, 512], F32),
                                 self.ring(st, "g_tm", 2, [128, 512], F32))
                        Wab = self.sb(st, "g_Wab", [128, 8, 32], BF16)
                        ABt = self.sb(st, "g_ABt", [64, NC_ * 32], F32)
                        tmpA = self.sb(st, "g_tmpA", [64, 576], F32)
                        tmpB = self.sb(st, "g_tmpB", [64, 576], F32)
                        GTt = self.sb(st, "g_GTt", [64, 576], F32)
                        pA = self.ring(st, "ps_gA", 2, [128, 512], F32, psum=True)
                        em.dma("pool", Wab[:], dG["w_in"][:, 4096:4128].rearrange("(k p) n -> p k n", p=128), [], [Wab], self.wsem[0])
                        tl = [("h", 0, CTX, self.d["HT"][b].rearrange("c p n -> p c n"), self.HTt[b], NB)]
                        for t in range(4):
                            tl.append(("x", t, 512, self.d["XT"][b, :, :, t * 512:(t + 1) * 512].rearrange("c p n -> p c n"),
                                       self.XTt[b][t], b))

                        class _V:
                            pass
                        for (kind, t, n, dap, trk, r) in tl:
                            o = 0 if kind == "h" else CTX + t * 512
                            xt = xtr.next()
                            em.dma("sp", xt[:, :, 0:n], dap, [trk], [xt], xt)
                            xnv = T(XN[:, :, o:o + n], "xnv")
                            xnv.w, xnv.r = XN.w, XN.r
                            self.norm_mod(rings, xt, n, l, s, r, xnv)
                            XN.w = xnv.w
                            XN.r = {}
                        for grp in range((NC_ + 15) // 16):
                            pa = pA.next()
                            c0 = grp * 16
                            c1 = min(NC_, c0 + 16)
                            for cc in range(c0, c1):
                                for k in range(8):
                                    em.op("pe", lambda e, cc=cc, k=k, pa=pa, c0=c0: e.matmul(
                                        out=pa[0:64, (cc - c0) * 32:(cc - c0 + 1) * 32], lhsT=XN[:, k, cc * 64:(cc + 1) * 64],
                                        rhs=Wab[:, k, :], start=(k == 0), stop=(k == 7)), [XN, Wab], [pa], inc=(k == 7 and cc == c1 - 1))
                            em.op("dve", lambda e, pa=pa, c0=c0, c1=c1: e.tensor_copy(
                                out=ABt[:, c0 * 32:c1 * 32], in_=pa[0:64, 0:(c1 - c0) * 32]), [pa], [ABt])
                        ab5 = ABt[:].rearrange("p (c d a h) -> p c d a h", c=NC_, d=2, a=2)
                        em.op("dve", lambda e: e.tensor_tensor(out=v4(tmpA), in0=ab5[:, :, :, 0, :], in1=v4(DTB), op=ALU.add), [ABt, DTB], [tmpA])
                        em.op("act", lambda e: e.activation(out=tmpA[:], in_=tmpA[:], func=AF.Exp), [tmpA], [tmpA])
                        em.op("act", lambda e: e.activation(out=tmpA[:], in_=tmpA[:], func=AF.Ln, bias=one_t[0:64, 0:1], scale=1.0), [tmpA, one_t], [tmpA])
                        em.op("dve", lambda e: e.tensor_tensor(out=Gt[:], in0=tmpA[:], in1=NEGA[:], op=ALU.mult), [tmpA, NEGA], [Gt])
                        em.op("act", lambda e: e.activation(out=v4(Bt), in_=ab5[:, :, :, 1, :], func=AF.Sigmoid), [ABt], [Bt])
                        em.op("dve", lambda e: e.tensor_scalar(out=nBt[:], in0=Bt[:], scalar1=-1.0, scalar2=None, op0=ALU.mult), [Bt], [nBt])
                        for d in range(2):
                            pa = pA.next()
                            em.op("pe", lambda e, d=d, pa=pa: e.matmul(
                                out=pa[0:64, 0:288].rearrange("p (c h) -> p c h", c=NC_), lhsT=cum[:, d, :], rhs=v4(Gt)[:, :, d, :],
                                start=True, stop=True), [cum, Gt], [pa])
                            em.op("dve", lambda e, d=d, pa=pa: e.tensor_copy(
                                out=v4(GCt)[:, :, d, :], in_=pa[0:64, 0:288].rearrange("p (c h) -> p c h", c=NC_)), [pa], [GCt])
                        for q in range(2):
                            pa = pA.next()
                            em.op("pe", lambda e, q=q, pa=pa: e.matmul(out=pa[:, 0:288], lhsT=onesf[:, :], rhs=Gt[:, q * 288:(q + 1) * 288],
                                                                     start=True, stop=True), [onesf, Gt], [pa])
                            em.op("act", lambda e, q=q, pa=pa: e.activation(out=EGT[:, q * 288:(q + 1) * 288], in_=pa[:, 0:288], func=AF.Exp), [pa], [EGT])
                            em.op("dve", lambda e, q=q, pa=pa: e.tensor_copy(out=GTt[:, q * 288:(q + 1) * 288], in_=pa[0:64, 0:288]), [pa], [GTt])
                        em.op("dve", lambda e: e.tensor_tensor(out=tmpB[:], in0=GTt[:], in1=GCt[:], op=ALU.subtract), [GTt, GCt], [tmpB])
                        em.op("act", lambda e: e.activation(out=ETL[:], in_=tmpB[:], func=AF.Exp), [tmpB], [ETL])
                        em.op("act", lambda e: e.activation(out=tmpA[:], in_=GCt[:], func=AF.Exp), [GCt], [tmpA])
                        em.op("dve", lambda e: e.tensor_tensor(out=NBEG[:], in0=tmpA[:], in1=nBt[:], op=ALU.mult), [tmpA, nBt], [NBEG])
                        em.barrier()
                    for hd in range(8):
                        self.gdn_head(b, hd, XN, Gt, Bt, nBt, GCt, ETL, NBEG, EGT, cum, mpos, mneg, identb, onesf, CW, OG)
                    with contextlib.ExitStack() as st:
                        AR = st.enter_context(nc.sbuf_tensor("arena_go%d" % b, [128, 8 * 1024], BF16))
                        Wout = [T(AR[:, k * 1024:(k + 1) * 1024], f"gwout{k}") for k in range(8)]
                        for k in range(8):
                            em.dma("pool", Wout[k][:], dG["w_out"][k * 128:(k + 1) * 128, :], [], [Wout[k]], self.wsem[k])
                        ycr = self.ring(st, "g_yc", 2, [128, 8, 512], BF16)
                        xrr = self.ring(st, "g_xr", 4, [128, 512], F32)
                        pY = self.ring(st, "ps_gY", 2, [128, 512], F32, psum=True)
                        for t in range(4):
                            n = 512
                            dap = self.d["XT"][b, :, :, t * 512:(t + 1) * 512].rearrange("c p n -> p c n")
                            trk = self.XTt[b][t]
                            yc = ycr.next()
                            em.dma("sp", yc[:], self.YC_d[:, :, t * 512:(t + 1) * 512].rearrange("c p n -> p c n"), [self.YCt], [yc], yc)
                            for dd in range(8):
                                xr = xrr.next()
                                em.dma("sp", xr[:, 0:n], dap[:, dd, :], [trk], [xr], xr)
                                Y_ = pY.next()
                                for c in range(8):
                                    em.op("pe", lambda e, c=c, dd=dd, Y_=Y_, yc=yc: e.matmul(
                                        out=Y_[:, 0:n], lhsT=Wout[c][:, dd * 128:(dd + 1) * 128], rhs=yc[:, c, :],
                                        start=(c == 0), stop=(c == 7)), [Wout[c], yc], [Y_], inc=(c == 7))
                                em.op("dve", lambda e, dd=dd, Y_=Y_, xr=xr: e.scalar_tensor_tensor(
                                    out=xr[:, 0:n], in0=Y_[:, 0:n], scalar=self.modG[:, l, s, dd, b:b + 1], in1=xr[:, 0:n],
                                    op0=ALU.mult, op1=ALU.add), [Y_, self.modG, xr], [xr])
                                em.dma("sp", dap[:, dd, :], xr[:, 0:n], [xr], [trk], xr)
                        em.barrier()

    def gdn_head(self, b, hd, XN, Gt, Bt, nBt, GCt, ETL, NBEG, EGT, cum, mpos, mneg, identb, onesf, CW, OG):
        em, nc = self.em, self.nc
        dG = self.dG
        NT = CTX + SEQ
        NC_ = NT // 64
        NCH = CTX // 64
        col = lambda t, cc, d: t[:, cc * 16 + d * 8 + hd: cc * 16 + d * 8 + hd + 1]
        with contextlib.ExitStack() as st:
            kT = self.sb(st, "h_kT", [128, NT], BF16)
            qT = self.sb(st, "h_qT", [128, SEQ], BF16)
            vT = self.sb(st, "h_vT", [128, NT], BF16)
            zS = self.sb(st, "h_zS", [128, SEQ], BF16)
            stp = contextlib.ExitStack()
            pP = self.ring(stp, "ps_hP", 2, [128, 512], F32, psum=True)
            pQ = self.ring(stp, "ps_hQ", 2, [128, 512], F32, psum=True)
            pR = self.ring(stp, "ps_hR", 2, [128, 512], F32, psum=True)
            pS_ = self.ring(stp, "ps_hS", 2, [128, 512], F32, psum=True)
            st1 = contextlib.ExitStack()
            wr = self.ring(st1, "h_w", 2, [128, 8, 128], BF16)
            PBr = self.ring(st1, "h_PB", 2, [128, NT + 4], F32)
            COr = self.ring(st1, "h_CO", 2, [128, NT + 4], F32)
            sqr = self.ring(st1, "h_sq", 2, [128, 512], BF16)
            sdr = self.ring(st1, "h_sd", 2, [128, 512], F32)
            rsr = self.ring(st1, "h_rs", 2, [128, 512], F32)
            for PB in PBr.tiles:
                em.op("pool", lambda e, PB=PB: e.memset(PB[:, 0:1], 0.0), [], [PB])
                em.op("pool", lambda e, PB=PB: e.memset(PB[:, 257:259], 0.0), [], [PB])
                em.op("pool", lambda e, PB=PB: e.memset(PB[:, NT + 3:NT + 4], 0.0), [], [PB])
            pcol = lambda tok: tok + 1 if tok < CTX else tok + 3
            tiles = [(0, CTX)] + [(CTX + t * 512, 512) for t in range(4)]
            for ty in (1, 2, 0, 3):
                w = wr.next()
                PB = PBr.next()
                CO = COr.next()
                c0 = ty * 1024 + hd * 128
                em.dma("pool", w[:], dG["w_in"][:, c0:c0 + 128].rearrange("(k p) n -> p k n", p=128), [], [w], self.wsem[ty])
                for (o, n) in tiles:
                    if ty in (0, 3) and o < CTX:
                        continue
                    P = pP.next()
                    for k in range(8):
                        em.op("pe", lambda e, k=k, P=P, w=w, o=o, n=n: e.matmul(out=P[:, 0:n], lhsT=w[:, k, :], rhs=XN[:, k, o:o + n],
                                                                         start=(k == 0), stop=(k == 7)), [w, XN], [P], inc=(k == 7))
                    if ty == 3:
                        em.op("act", lambda e, P=P, o=o, n=n: e.activation(out=zS[:, o - CTX:o - CTX + n], in_=P[:, 0:n], func=AF.Silu), [P], [zS])
                    else:
                        em.op("act", lambda e, P=P, o=o, n=n: e.copy(out=PB[:, pcol(o):pcol(o) + n], in_=P[:, 0:n]), [P], [PB])
                if ty == 3:
                    continue
                lo = 1 if ty != 0 else 259
                hi = NT + 3
                cw = lambda tap: CW[:, tap * 24 + ty * 8 + hd: tap * 24 + ty * 8 + hd + 1]
                em.op("dve", lambda e: e.tensor_scalar(out=CO[:, lo:hi], in0=PB[:, lo - 1:hi - 1], scalar1=cw(0), scalar2=None, op0=ALU.mult),
                      [PB, CW], [CO])
                em.op("dve", lambda e: e.scalar_tensor_tensor(out=CO[:, lo:hi], in0=PB[:, lo:hi], scalar=cw(1), in1=CO[:, lo:hi],
                                                              op0=ALU.mult, op1=ALU.add), [PB, CW, CO], [CO])
                em.op("dve", lambda e: e.scalar_tensor_tensor(out=CO[:, lo:hi], in0=PB[:, lo + 1:hi + 1], scalar=cw(2), in1=CO[:, lo:hi],
                                                              op0=ALU.mult, op1=ALU.add), [PB, CW, CO], [CO])
                for (o, n) in tiles:
                    if ty == 0 and o < CTX:
                        continue
                    pc = pcol(o)
                    if ty == 2:
                        em.op("act", lambda e, o=o, n=n, pc=pc: e.activation(out=vT[:, o:o + n], in_=CO[:, pc:pc + n], func=AF.Silu), [CO], [vT])
                        continue
                    em.op("act", lambda e, n=n, pc=pc: e.activation(out=CO[:, pc:pc + n], in_=CO[:, pc:pc + n], func=AF.Silu), [CO], [CO])
                    sq = sqr.next()
                    em.op("act", lambda e, sq=sq, n=n, pc=pc: e.activation(out=sq[:, 0:n], in_=CO[:, pc:pc + n], func=AF.Square), [CO], [sq])
                    S_ = pQ.next()
                    em.op("pe", lambda e, sq=sq, S_=S_, n=n: e.matmul(out=S_[:, 0:n], lhsT=self.ones_bf[:], rhs=sq[:, 0:n], start=True, stop=True),
                          [sq, self.ones_bf], [S_])
                    sd = sdr.next()
                    em.op("act", lambda e, sd=sd, S_=S_, n=n: e.activation(out=sd[:, 0:n], in_=S_[:, 0:n], func=AF.Ln, scale=1.0,
                                                                        bias=self.eps_t[:, 0:1]), [S_, self.eps_t], [sd])
                    rs = rsr.next()
                    em.op("act", lambda e, sd=sd, rs=rs, n=n: e.activation(out=rs[:, 0:n], in_=sd[:, 0:n], func=AF.Exp, scale=-0.5), [sd], [rs])
                    if ty == 1:
                        em.op("dve", lambda e, rs=rs, o=o, n=n, pc=pc: e.tensor_tensor(out=kT[:, o:o + n], in0=CO[:, pc:pc + n], in1=rs[:, 0:n],
                                                                                  op=ALU.mult), [CO, rs], [kT])
                    else:
                        em.op("dve", lambda e, rs=rs, o=o, n=n, pc=pc: e.scalar_tensor_tensor(
                            out=qT[:, o - CTX:o - CTX + n], in0=CO[:, pc:pc + n], scalar=128 ** -0.5, in1=rs[:, 0:n],
                            op0=ALU.mult, op1=ALU.mult), [CO, rs], [qT])
            em.barrier()
            st1.close()
            KTL = [self.sb(st, f"h_KTL{d}", [64, NC_, 128], BF16) for d in range(2)]
            BV = [self.sb(st, f"h_BV{d}", [64, NC_, 128], BF16) for d in range(2)]
            Yv = [self.sb(st, f"h_Y{d}", [64, NC_, 64], BF16) for d in range(2)]
            QK = [self.sb(st, f"h_QK{d}", [64, NC_ - NCH, 64], BF16) for d in range(2)]
            QD = [self.sb(st, f"h_QD{d}", [128, SEQ], BF16) for d in range(2)]
            OA = [self.sb(st, f"h_OA{d}", [128, SEQ], F32) for d in range(2)]
            sqr = self.ring(st, "h_sq_b", 2, [128, 512], BF16)
            sdr = self.ring(st, "h_sd_b", 1, [128, 512], F32)
            rsr = self.ring(st, "h_rs_b", 1, [128, 512], F32)
            for g4 in range(NC_ // 4):
                Pk = pP.next()
                Pv = pQ.next()
                for j in range(4):
                    cc = g4 * 4 + j
                    em.op("pe", lambda e, cc=cc, j=j, Pk=Pk: e.matmul(out=Pk[0:64, j * 128:(j + 1) * 128], lhsT=kT[:, cc * 64:(cc + 1) * 64],
                                                                   rhs=identb[:], start=True, stop=True), [kT, identb], [Pk], inc=(j == 3))
                for j in range(4):
                    cc = g4 * 4 + j
                    em.op("pe", lambda e, cc=cc, j=j, Pv=Pv: e.matmul(out=Pv[0:64, j * 128:(j + 1) * 128], lhsT=vT[:, cc * 64:(cc + 1) * 64],
                                                                   rhs=identb[:], start=True, stop=True), [vT, identb], [Pv], inc=(j == 3))
                if DBG.get('bc_tok'):
                    c0 = g4 * 4
                    cb4 = lambda t, d: t[:, :].rearrange("p (c k) -> p c k", k=16)[:, c0:c0 + 4, d * 8 + hd:d * 8 + hd + 1].broadcast_to([64, 4, 128])
                    for d in range(2):
                        em.op("dve", lambda e, d=d, Pk=Pk: e.tensor_tensor(out=KTL[d][:, c0:c0 + 4, :], in0=Pk[0:64, 0:512].rearrange("p (g n) -> p g n", g=4),
                                                                       in1=cb4(ETL, d), op=ALU.mult), [Pk, ETL], [KTL[d]])
                        em.op("dve", lambda e, d=d, Pv=Pv: e.tensor_tensor(out=BV[d][:, c0:c0 + 4, :], in0=Pv[0:64, 0:512].rearrange("p (g n) -> p g n", g=4),
                                                                       in1=cb4(Bt, d), op=ALU.mult), [Pv, Bt], [BV[d]])
                else:
                    for j in range(4):
                        cc = g4 * 4 + j
                        for d in range(2):
                            em.op("act", lambda e, cc=cc, j=j, d=d, Pk=Pk: e.activation(
                                out=KTL[d][:, cc, :], in_=Pk[0:64, j * 128:(j + 1) * 128], func=AF.Copy, scale=col(ETL, cc, d)), [Pk, ETL], [KTL[d]])
                            em.op("dve", lambda e, cc=cc, j=j, d=d, Pv=Pv: e.tensor_scalar(
                                out=BV[d][:, cc, :], in0=Pv[0:64, j * 128:(j + 1) * 128], scalar1=col(Bt, cc, d), scalar2=None, op0=ALU.mult),
                                [Pv, Bt], [BV[d]])
            em.barrier()
            stp.close()
            GSM = DBG.get("GSM", 8)
            GROUPS = [(0, NCH)] + [(g, GSM) for g in range(NCH, NC_, GSM)]
            st2 = contextlib.ExitStack()
            LN = DBG.get("LN", 2)

            def mkres(ln):
                R_ = {}
                SEP = DBG.get("sepbank", 0)
                banks = [self.ps(st2, f"ps_l{ln}_{i}", [128, 512], F32) for i in range(8 if (SEP or DBG.get("sepnames")) else 4)]
                R_["KK"] = banks[0]
                R_["KQ"] = banks[1] if len(banks) == 8 else banks[0]
                R_["pr"] = Ring(banks[2:] if len(banks) == 8 else banks[1:])
                f = lambda nm, n, p=64: self.ring(st2, f"l{ln}_{nm}", n, [p, GSM, 64], F32)
                R_["gs"], R_["e2"], R_["e1"] = f("gs", 2), f("e2", 2), f("e1", 2)
                R_["N"], R_["M"], R_["P"] = f("N", 4), f("M", 4), f("P", 4)
                R_["eg"] = f("eg", 2, 128)
                R_["Xb"] = self.ring(st2, f"l{ln}_Xb", 2, [64, GSM, 64], BF16)
                return R_
            res = [mkres(ln) for ln in range(LN)]

            def grp_gen(g0, gsz, R_):
                ccs = list(range(g0, g0 + gsz))
                flat = lambda t: t[:, 0:gsz, :].rearrange("p g n -> p (g n)")
                isx = g0 >= NCH
                KK, KQ = R_["KK"], R_["KQ"]
                QO = 0 if KQ is not KK else 256
                SEP = DBG.get("sepbank", 0)
                DO = 0 if SEP else 256
                DOF = lambda nm: 0 if (SEP or nm in DBG.get('sepnames', [])) else 256

                SEPN = DBG.get("sepnames", [])

                class _BK:
                    def __init__(self, nm=""):
                        self.sep = SEP or (nm in SEPN)
                        self.b = None if self.sep else R_["pr"].next()

                    def get(self):
                        return R_["pr"].next() if self.sep else self.b
                for j, cc in enumerate(ccs):
                    em.op("pe", lambda e, j=j, cc=cc: e.matmul(out=KK[0:64, j * 64:(j + 1) * 64], lhsT=kT[:, cc * 64:(cc + 1) * 64],
                                                             rhs=kT[:, cc * 64:(cc + 1) * 64], start=True, stop=True), [kT], [KK], inc=(j == gsz - 1))
                if isx:
                    for j, cc in enumerate(ccs):
                        em.op("pe", lambda e, j=j, cc=cc: e.matmul(out=KQ[0:64, QO + j * 64:QO + (j + 1) * 64], lhsT=kT[:, cc * 64:(cc + 1) * 64],
                                                                 rhs=qT[:, cc * 64 - CTX:(cc + 1) * 64 - CTX], start=True, stop=True),
                              [kT, qT], [KQ], inc=(j == gsz - 1))
                fr = (lambda ap: ap.bitcast(F32R)) if DBG.get('f32r') else (lambda ap: ap)
                SH = [64, gsz, 64]
                colb = lambda t, d: t[:, :].rearrange("p (c k) -> p c k", k=16)[:, g0:g0 + gsz, d * 8 + hd:d * 8 + hd + 1].broadcast_to(SH)
                gb3 = lambda m, d: m[:, d:d + 1, :].broadcast_to(SH)
                ps3 = lambda P_, off=0: P_[0:64, off:off + gsz * 64].rearrange("p (g n) -> p g n", g=gsz)
                id3 = self.ident[0:64, 0:64].rearrange("p (o n) -> p o n", o=1).broadcast_to(SH)
                C = [dict() for _ in range(2)]
                KYM = DBG.get("kym", 1)
                for d in range(2):
                    gs = C[d]["gs"] = R_["gs"].next()
                    if DBG.get('bA'):
                        em.op("dve", lambda e, gs=gs, d=d: e.tensor_tensor(out=gs[:, 0:gsz, :], in0=gb3(cum, d), in1=colb(Gt, d), op=ALU.mult),
                              [cum, Gt], [gs])
                    else:
                        for j, cc in enumerate(ccs):
                            em.op("dve", lambda e, j=j, cc=cc, gs=gs, d=d: e.tensor_scalar(out=gs[:, j, :], in0=cum[:, d, :], scalar1=col(Gt, cc, d),
                                                                                     scalar2=None, op0=ALU.mult), [cum, Gt], [gs])
                yield
                BK_GB = _BK("GB")
                for d in range(2):
                    GB = C[d]["GB"] = BK_GB.get()
                    gs = C[d]["gs"]
                    em.op("pe", lambda e, gs=gs, GB=GB: e.matmul(out=GB[:, d * DOF('GB'):d * DOF('GB') + gsz * 64], lhsT=onesf[:, :], rhs=flat(gs), start=True, stop=True),
                          [onesf, gs], [GB])
                yield
                for d in range(2):
                    GB = C[d]["GB"]
                    e2 = C[d]["e2"] = R_["e2"].next()
                    e1 = C[d]["e1"] = R_["e1"].next()
                    if DBG.get('bB'):
                        em.op("dve", lambda e, e2=e2, GB=GB, d=d: e.tensor_tensor(out=e2[:, 0:gsz, :], in0=ps3(GB, d * DOF('GB')), in1=colb(GCt, d),
                                                                           op=ALU.subtract), [GB, GCt], [e2])
                        if isx:
                            em.op("dve", lambda e, e1=e1, e2=e2, d=d: e.tensor_tensor(out=e1[:, 0:gsz, :], in0=e2[:, 0:gsz, :], in1=gb3(mneg, d), op=ALU.min),
                                  [e2, mneg], [e1])
                        em.op("dve", lambda e, e2=e2, d=d: e.tensor_tensor(out=e2[:, 0:gsz, :], in0=e2[:, 0:gsz, :], in1=gb3(mpos, d), op=ALU.max),
                              [e2, mpos], [e2])
                    else:
                        for j, cc in enumerate(ccs):
                            em.op("dve", lambda e, j=j, cc=cc, e2=e2, GB=GB, d=d: e.scalar_tensor_tensor(
                                out=e2[:, j, :], in0=GB[0:64, d * DOF('GB') + j * 64:d * DOF('GB') + (j + 1) * 64], scalar=col(GCt, cc, d), in1=mpos[:, d, :],
                                op0=ALU.subtract, op1=ALU.max), [GB, GCt, mpos], [e2])
                            if isx:
                                em.op("dve", lambda e, j=j, cc=cc, e1=e1, GB=GB, d=d: e.scalar_tensor_tensor(
                                    out=e1[:, j, :], in0=GB[0:64, d * DOF('GB') + j * 64:d * DOF('GB') + (j + 1) * 64], scalar=col(GCt, cc, d), in1=mneg[:, d, :],
                                    op0=ALU.subtract, op1=ALU.min), [GB, GCt, mneg], [e1])
                yield
                for d in range(2):
                    GB, e2, e1 = C[d]["GB"], C[d]["e2"], C[d]["e1"]
                    em.op("act", lambda e, e2=e2: e.activation(out=e2[:, 0:gsz, :], in_=e2[:, 0:gsz, :], func=AF.Exp, scale=-1.0), [e2], [e2])
                    if isx:
                        em.op("act", lambda e, e1=e1: e.activation(out=e1[:, 0:gsz, :], in_=e1[:, 0:gsz, :], func=AF.Exp), [e1], [e1])
                        eg = C[d]["eg"] = R_["eg"].next()
                        em.op("act", lambda e, eg=eg, GB=GB: e.activation(out=flat(eg), in_=GB[:, d * DOF('GB'):d * DOF('GB') + gsz * 64], func=AF.Exp), [GB], [eg])
                yield
                for d in range(2):
                    e2, e1 = C[d]["e2"], C[d]["e1"]
                    Nn = C[d]["N"] = R_["N"].next()
                    if DBG.get('bB'):
                        em.op("dve", lambda e, e2=e2, d=d: e.tensor_tensor(out=e2[:, 0:gsz, :], in0=e2[:, 0:gsz, :], in1=colb(nBt, d), op=ALU.mult),
                              [e2, nBt], [e2])
                        em.op("dve", lambda e, Nn=Nn, e2=e2: e.tensor_tensor(out=fr(Nn[:, 0:gsz, :]), in0=ps3(KK), in1=e2[:, 0:gsz, :], op=ALU.mult),
                              [KK, e2], [Nn])
                    else:
                        for j, cc in enumerate(ccs):
                            em.op("dve", lambda e, j=j, cc=cc, Nn=Nn, e2=e2, d=d: e.scalar_tensor_tensor(
                                out=fr(Nn[:, j, :]), in0=KK[0:64, j * 64:(j + 1) * 64], scalar=col(nBt, cc, d), in1=e2[:, j, :],
                                op0=ALU.mult, op1=ALU.mult), [KK, nBt, e2], [Nn])
                    if isx:
                        eg = C[d]["eg"]
                        xo = g0 * 64 - CTX
                        em.op("pool", lambda e, eg=eg, d=d, xo=xo: e.tensor_tensor(
                            out=QD[d][:, xo:xo + gsz * 64], in0=qT[:, xo:xo + gsz * 64], in1=flat(eg), op=ALU.mult), [qT, eg], [QD[d]])
                        em.op("dve", lambda e, e1=e1, d=d: e.tensor_tensor(
                            out=QK[d][:, g0 - NCH:g0 - NCH + gsz, :], in0=KQ[0:64, QO:QO + gsz * 64].rearrange("p (g n) -> p g n", g=gsz), in1=e1[:, 0:gsz, :], op=ALU.mult),
                            [KQ, e1], [QK[d]])
                yield
                BK_PT = _BK("PT")
                for d in range(2):
                    Nn = C[d]["N"]
                    PT = C[d]["PT"] = BK_PT.get()
                    for j in range(gsz):
                        em.op("pe", lambda e, j=j, Nn=Nn, PT=PT: e.transpose(out=PT[0:64, d * DOF('PT') + j * 64:d * DOF('PT') + (j + 1) * 64], in_=Nn[:, j, :],
                                                                          identity=self.ident[0:64, 0:64]), [Nn, self.ident], [PT], inc=(j == gsz - 1))
                yield
                for d in range(2):
                    PT = C[d]["PT"]
                    Mm = C[d]["M"] = R_["M"].next()
                    Pc = C[d]["P"] = R_["P"].next()
                    em.op("act", lambda e, Mm=Mm, PT=PT: e.copy(out=fr(flat(Mm)), in_=PT[0:64, d * DOF('PT'):d * DOF('PT') + gsz * 64]), [PT], [Mm])
                    if DBG.get('bC'):
                        em.op("dve", lambda e, Pc=Pc, PT=PT, d=d: e.tensor_tensor(out=fr(Pc[:, 0:gsz, :]), in0=ps3(PT, d * DOF('PT')), in1=id3, op=ALU.add),
                              [PT, self.ident], [Pc])
                    else:
                        for j in range(gsz):
                            em.op("dve", lambda e, j=j, Pc=Pc, PT=PT: e.tensor_tensor(out=fr(Pc[:, j, :]), in0=PT[0:64, d * DOF('PT') + j * 64:d * DOF('PT') + (j + 1) * 64],
                                                                                   in1=self.ident[0:64, 0:64], op=ALU.add), [PT, self.ident], [Pc])
                yield
                for lev in range(5 if not DBG.get("skip_inv") else 0):
                    BK_PN = _BK("PN")
                    BK_PM = _BK("PM") if lev < 4 else None
                    for d in range(2):
                        Nn, Mm = C[d]["N"], C[d]["M"]
                        PN = C[d]["PN"] = BK_PN.get()
                        for j in range(gsz):
                            em.op("pe", lambda e, j=j, PN=PN, Mm=Mm, Nn=Nn: e.matmul(out=PN[0:64, d * DOF('PN') + j * 64:d * DOF('PN') + (j + 1) * 64], lhsT=fr(Mm[:, j, :]), rhs=fr(Nn[:, j, :]),
                                                                                start=True, stop=True), [Mm, Nn], [PN], inc=(j == gsz - 1))
                        if lev < 4:
                            PM = C[d]["PM"] = BK_PM.get()
                            for j in range(gsz):
                                em.op("pe", lambda e, j=j, PM=PM, Mm=Mm, Nn=Nn: e.matmul(out=PM[0:64, d * DOF('PM') + j * 64:d * DOF('PM') + (j + 1) * 64], lhsT=fr(Nn[:, j, :]), rhs=fr(Mm[:, j, :]),
                                                                                    start=True, stop=True), [Mm, Nn], [PM], inc=(j == gsz - 1))
                    yield
                    for d in range(2):
                        N2 = C[d]["N"] = R_["N"].next()
                        PN = C[d]["PN"]
                        em.op("act", lambda e, N2=N2, PN=PN: e.copy(out=fr(flat(N2)), in_=PN[0:64, d * DOF('PN'):d * DOF('PN') + gsz * 64]), [PN], [N2])
                        if lev < 4:
                            M2 = C[d]["M"] = R_["M"].next()
                            PM = C[d]["PM"]
                            em.op("dve", lambda e, M2=M2, PM=PM: e.tensor_copy(out=fr(flat(M2)), in_=PM[0:64, d * DOF('PM'):d * DOF('PM') + gsz * 64]), [PM], [M2])
                    yield
                    BK_PP = _BK("PP")
                    for d in range(2):
                        N2, Pc = C[d]["N"], C[d]["P"]
                        PP = C[d]["PP"] = BK_PP.get()
                        for j in range(gsz):
                            em.op("pe", lambda e, j=j, PP=PP, N2=N2, Pc=Pc: e.matmul(out=PP[0:64, d * DOF('PP') + j * 64:d * DOF('PP') + (j + 1) * 64], lhsT=fr(N2[:, j, :]), rhs=fr(Pc[:, j, :]),
                                                                                start=True, stop=True), [N2, Pc], [PP], inc=(j == gsz - 1))
                    yield
                    for d in range(2):
                        PP, Pc = C[d]["PP"], C[d]["P"]
                        if lev < 4 or KYM:
                            Pn = C[d]["P"] = R_["P"].next()
                            em.op("dve", lambda e, Pn=Pn, PP=PP, Pc=Pc: e.tensor_tensor(out=fr(flat(Pn)), in0=PP[0:64, d * DOF('PP'):d * DOF('PP') + gsz * 64], in1=flat(Pc), op=ALU.add),
                                  [PP, Pc], [Pn])
                            if lev == 4:
                                em.op("act", lambda e, Pn=Pn, d=d: e.copy(out=Yv[d][:, g0:g0 + gsz, :], in_=Pn[:, 0:gsz, :]), [Pn], [Yv[d]])
                        else:
                            em.op("dve", lambda e, PP=PP, Pc=Pc, d=d: e.tensor_tensor(
                                out=Yv[d][:, g0:g0 + gsz, :], in0=PP[0:64, d * DOF('PP'):d * DOF('PP') + gsz * 64].rearrange("p (g n) -> p g n", g=gsz), in1=Pc[:, 0:gsz, :], op=ALU.add),
                                [PP, Pc], [Yv[d]])
                    yield
                if KYM and not DBG.get("skip_inv"):
                    for d in range(2):
                        Pn = C[d]["P"]
                        XT_ = C[d]["XT"] = R_["pr"].next()
                        for j in range(gsz):
                            em.op("pe", lambda e, j=j, Pn=Pn, XT_=XT_: e.transpose(out=XT_[0:64, j * 64:(j + 1) * 64], in_=Pn[:, j, :],
                                                                              identity=self.ident[0:64, 0:64]), [Pn, self.ident], [XT_], inc=(j == gsz - 1))
                    yield
                    for d in range(2):
                        XT_ = C[d]["XT"]
                        Xb = C[d]["Xb"] = R_["Xb"].next()
                        em.op("act", lambda e, Xb=Xb, XT_=XT_: e.copy(out=Xb[:, 0:gsz, :].rearrange("p g n -> p (g n)"), in_=XT_[0:64, 0:gsz * 64]), [XT_], [Xb])
                    yield
                    for d in range(2):
                        Xb = C[d]["Xb"]
                        C[d]["KYp"] = []
                        for h4 in range(0, gsz, 4):
                            KYp = R_["pr"].next()
                            C[d]["KYp"].append((h4, KYp))
                            for j in range(h4, min(gsz, h4 + 4)):
                                em.op("pe", lambda e, j=j, h4=h4, KYp=KYp, Xb=Xb, d=d: e.matmul(
                                    out=KYp[0:64, (j - h4) * 128:(j - h4 + 1) * 128], lhsT=Xb[:, j, :], rhs=KTL[d][:, g0 + j, :], start=True, stop=True),
                                    [Xb, KTL[d]], [KYp], inc=(j == min(gsz, h4 + 4) - 1))
                    yield
                    for d in range(2):
                        for (h4, KYp) in C[d]["KYp"]:
                            n4 = min(gsz, h4 + 4) - h4
                            eng = "dve" if h4 == 0 else "act"
                            if eng == "dve":
                                em.op("dve", lambda e, KYp=KYp, h4=h4, n4=n4, d=d: e.tensor_copy(
                                    out=KTL[d][:, g0 + h4:g0 + h4 + n4, :], in_=KYp[0:64, 0:n4 * 128].rearrange("p (g n) -> p g n", g=n4)), [KYp], [KTL[d]])
                            else:
                                em.op("act", lambda e, KYp=KYp, h4=h4, n4=n4, d=d: e.copy(
                                    out=KTL[d][:, g0 + h4:g0 + h4 + n4, :], in_=KYp[0:64, 0:n4 * 128].rearrange("p (g n) -> p g n", g=n4)), [KYp], [KTL[d]])
                    yield
                if DBG.get("skip_inv"):
                    for d in range(2):
                        Pc = C[d]["P"]
                        em.op("dve", lambda e, Pc=Pc, d=d: e.tensor_copy(out=Yv[d][:, g0:g0 + gsz, :], in_=Pc[:, 0:gsz, :]), [Pc], [Yv[d]])

            def lane_gen(ln):
                for (g0, gsz) in GROUPS[ln::LN]:
                    yield from grp_gen(g0, gsz, res[ln])
            interleave([lane_gen(ln) for ln in range(LN)])
            em.barrier()
            st2.close()
            pP = self.ring(st, "ps_sP", 2, [128, 512], F32, psum=True)
            pQ = self.ring(st, "ps_sQ", 2, [128, 512], F32, psum=True)
            pR = self.ring(st, "ps_sR", 2, [128, 512], F32, psum=True)
            pS_ = self.ring(st, "ps_sS", 2, [128, 512], F32, psum=True)
            Sf = [self.ring(st, f"h_S{d}", 2, [128, 128], F32) for d in range(2)]
            Sbr = [self.ring(st, f"h_Sb{d}", 2, [128, 128], BF16) for d in range(2)]
            rr = self.ring(st, "h_r", 4, [64, 128], BF16)
            vnr = self.ring(st, "h_vn", 4, [64, 128], BF16)
            S_cur, Sb_cur = [], []
            for d in range(2):
                S0 = Sf[d].next()
                Sb0 = Sbr[d].next()
                em.op("pool", lambda e, S0=S0: e.memset(S0[:], 0.0), [], [S0])
                em.op("pool", lambda e, Sb0=Sb0: e.memset(Sb0[:], 0.0), [], [Sb0])
                S_cur.append(S0)
                Sb_cur.append(Sb0)
            order = [list(range(NC_)), list(range(NCH - 1, -1, -1)) + list(range(NC_ - 1, NCH - 1, -1))]
            for step in range(NC_ if not DBG.get("skip_scan") else 0):
                for d in range(2):
                    cc = order[d][step]
                    S_old, Sb_old = S_cur[d], Sb_cur[d]
                    M1 = pP.next()
                    em.op("pe", lambda e, cc=cc, M1=M1, Sb_old=Sb_old: e.matmul(out=M1[0:64, 0:128], lhsT=kT[:, cc * 64:(cc + 1) * 64], rhs=Sb_old[:],
                                                                          start=True, stop=True), [kT, Sb_old], [M1])
                    r_ = rr.next()
                    em.op("dve", lambda e, cc=cc, d=d, M1=M1, r_=r_: e.scalar_tensor_tensor(
                        out=r_[:], in0=M1[0:64, 0:128], scalar=col(NBEG, cc, d), in1=BV[d][:, cc, :], op0=ALU.mult, op1=ALU.add),
                        [M1, NBEG, BV[d]], [r_])
                    dS = pR.next()
                    em.op("pe", lambda e, cc=cc, d=d, dS=dS, r_=r_: e.matmul(out=dS[:, 0:128], lhsT=KTL[d][:, cc, :], rhs=r_[:], start=True, stop=True),
                          [KTL[d], r_], [dS])
                    S_new = Sf[d].next()
                    Sb_new = Sbr[d].next()
                    em.op("dve", lambda e, cc=cc, d=d, Sb_new=Sb_new, S_old=S_old, dS=dS: e.scalar_tensor_tensor(
                        out=Sb_new[:], in0=S_old[:], scalar=col(EGT, cc, d), in1=dS[:, 0:128], op0=ALU.mult, op1=ALU.add),
                        [S_old, EGT, dS], [Sb_new])
                    em.op("dve", lambda e, cc=cc, d=d, S_new=S_new, S_old=S_old, dS=dS: e.scalar_tensor_tensor(
                        out=S_new[:], in0=S_old[:], scalar=col(EGT, cc, d), in1=dS[:, 0:128], op0=ALU.mult, op1=ALU.add),
                        [S_old, EGT, dS], [S_new])
                    if cc >= NCH:
                        VN = pQ.next()
                        em.op("pe", lambda e, cc=cc, d=d, VN=VN, r_=r_: e.matmul(out=VN[0:64, 0:128], lhsT=Yv[d][:, cc, :], rhs=r_[:], start=True, stop=True),
                              [Yv[d], r_], [VN])
                        vn = vnr.next()
                        em.op("act", lambda e, vn=vn, VN=VN: e.copy(out=vn[:], in_=VN[0:64, 0:128]), [VN], [vn])
                        OT = pS_.next()
                        xo = cc * 64 - CTX
                        em.op("pe", lambda e, OT=OT, Sb_old=Sb_old, d=d, xo=xo: e.matmul(out=OT[:, 0:64], lhsT=Sb_old[:], rhs=QD[d][:, xo:xo + 64],
                                                                                   start=True, stop=False), [Sb_old, QD[d]], [OT])
                        em.op("pe", lambda e, OT=OT, vn=vn, d=d, cc=cc: e.matmul(out=OT[:, 0:64], lhsT=vn[:], rhs=QK[d][:, cc - NCH, :],
                                                                             start=False, stop=True), [vn, QK[d]], [OT])
                        em.op("act", lambda e, OT=OT, d=d, xo=xo: e.copy(out=OA[d][:, xo:xo + 64], in_=OT[:, 0:64]), [OT], [OA[d]])
                    S_cur[d], Sb_cur[d] = S_new, Sb_new
            ycr = self.ring(st, "h_yc", 2, [128, 512], BF16)
            tmr = self.ring(st, "h_tm", 2, [128, 512], F32)
            for t in range(4):
                sl = slice(t * 512, (t + 1) * 512)
                em.op("dve", lambda e, sl=sl: e.tensor_tensor(out=OA[0][:, sl], in0=OA[0][:, sl], in1=OA[1][:, sl], op=ALU.add), [OA[0], OA[1]], [OA[0]])
                sq = sqr.next()
                em.op("act", lambda e, sq=sq, sl=sl: e.activation(out=sq[:], in_=OA[0][:, sl], func=AF.Square), [OA[0]], [sq])
                S_ = pQ.next()
                em.op("pe", lambda e, sq=sq, S_=S_: e.matmul(out=S_[:], lhsT=self.ones_bf[:], rhs=sq[:], start=True, stop=True), [sq, self.ones_bf], [S_])
                sd = sdr.next()
                em.op("act", lambda e, sd=sd, S_=S_: e.activation(out=sd[:], in_=S_[:], func=AF.Ln, scale=1.0 / 128, bias=self.eps_t[:, 0:1]),
                      [S_, self.eps_t], [sd])
                rs = rsr.next()
                em.op("act", lambda e, sd=sd, rs=rs: e.activation(out=rs[:], in_=sd[:], func=AF.Exp, scale=-0.5), [sd], [rs])
                tm = tmr.next()
                em.op("dve", lambda e, tm=tm, rs=rs, sl=sl: e.scalar_tensor_tensor(out=tm[:], in0=OA[0][:, sl], scalar=OG[:, 0:1], in1=rs[:],
                                                                                 op0=ALU.mult, op1=ALU.mult), [OA[0], OG, rs], [tm])
                yc = ycr.next()
                em.op("pool", lambda e, tm=tm, yc=yc, sl=sl: e.tensor_tensor(out=yc[:], in0=tm[:], in1=zS[:, sl], op=ALU.mult), [tm, zS], [yc])
                em.dma("sp", self.YC_d[hd, :, sl], yc[:], [yc], [self.YCt], yc)
            em.barrier()


def make_consts():
    c = {"ident": np.eye(128, dtype=np.float32)}
    n = np.arange(SEQ)
    rowi = (n // 64).astype(np.float32)
    coli = (n % 64).astype(np.float32)
    inv = (10000.0 ** (-(np.arange(0, 64, 2, dtype=np.float32)) / 64)).astype(np.float32)
    C = np.zeros((128, SEQ), np.float32)
    S = np.zeros((128, SEQ), np.float32)
    rot = np.zeros((128, 128), np.float32)
    for p in range(128):
        a, h, i = p // 64, (p // 32) % 2, p % 32
        ang = (rowi if a == 0 else coli) * inv[i]
        C[p] = np.cos(ang)
        S[p] = np.sin(ang)
        if h == 0:
            rot[p + 32, p] = -1.0
        else:
            rot[p - 32, p] = 1.0
    c["ropeC"], c["ropeS"], c["rotm"] = C, S, rot
    ed = np.ones((4, 32), np.float32)
    for g, w in enumerate((2, 4, 8, 16)):
        L = w // 2
        for j in range(16):
            ed[g, j] = w / (min(j + L, 10 ** 6) - max(j - L, 0))
            tt = 16 - j
            ed[g, 16 + j] = w / (min(L, tt) + L) if True else 1.0
    c["pool_edge"] = np.ascontiguousarray(np.broadcast_to(ed[None], (128, 4, 32))).astype(np.float32)
    ii = np.arange(64)
    cum = np.zeros((2, 64, 64), np.float32)
    mpos = np.zeros((2, 64, 64), np.float32)
    mneg = np.zeros((2, 64, 64), np.float32)
    cum[0] = (ii[:, None] <= ii[None, :])
    cum[1] = (ii[:, None] >= ii[None, :])
    mpos[0] = np.where(ii[:, None] > ii[None, :], 0.0, 1e4)
    mpos[1] = np.where(ii[:, None] < ii[None, :], 0.0, 1e4)
    mneg[0] = np.where(ii[None, :] >= ii[:, None], 0.0, -1e4)
    mneg[1] = np.where(ii[None, :] <= ii[:, None], 0.0, -1e4)
    c["g_cum"], c["g_mpos"], c["g_mneg"] = cum, mpos, mneg
    return c


_CACHE = {}


def get_program(NB, stop_after=None, dbg_h=False):
    key = (NB, stop_after, dbg_h)
    if key not in _CACHE:
        _CACHE[key] = Builder(NB, stop_after, dbg_h).build()
    return _CACHE[key]


def core_inputs(inputs, b0, NB):
    m = {
        "x": np.ascontiguousarray(inputs["x"][b0:b0 + NB]),
        "ctx": np.ascontiguousarray(inputs["ctx"][b0:b0 + NB]),
        "c": np.ascontiguousarray(np.concatenate([inputs["c"][b0:b0 + NB], inputs["c_ctx"][None]], 0)),
    }
    for k in ("w_mod", "b_mod", "norm_g", "ffn_wg", "ffn_wu", "ffn_wd"):
        m[k] = np.ascontiguousarray(inputs[k])
    m["ab_w_in"] = np.ascontiguousarray(inputs["ab_w_in"][0])
    m["ab_q_norm"] = np.ascontiguousarray(inputs["ab_q_norm"])
    m["ab_k_norm"] = np.ascontiguousarray(inputs["ab_k_norm"])
    m["pool_w"] = np.ascontiguousarray(inputs["pool_w"][0])
    m["pool_scale"] = np.ascontiguousarray(inputs["pool_scale"])
    m["ab_w_out"] = np.ascontiguousarray(inputs["ab_w_out"][0])
    m["gdn_w_in"] = np.ascontiguousarray(inputs["gdn_w_in"][0])
    m["gdn_conv_w"] = np.ascontiguousarray(inputs["gdn_conv_w"][0].reshape(1, 3 * 3072))
    m["alog_rep"] = np.ascontiguousarray(np.tile(inputs["gdn_a_log"][0].reshape(16), 36)[None])
    m["dtb_rep"] = np.ascontiguousarray(np.tile(inputs["gdn_dt_bias"][0].reshape(16), 36)[None])
    m["gdn_o_norm"] = np.ascontiguousarray(inputs["gdn_o_norm"])
    m["gdn_w_out"] = np.ascontiguousarray(inputs["gdn_w_out"][0])
    m.update(make_consts())
    return m


def kernel(**inputs):
    inputs = {k: np.asarray(v) for k, v in inputs.items()}
    B = inputs["x"].shape[0]
    NB = B // NCORES
    nc = get_program(NB)
    in_maps = [core_inputs(inputs, i * NB, NB) for i in range(NCORES)]
    res = run_bass_kernel_spmd(nc, in_maps, core_ids=list(range(NCORES)))
    return np.concatenate([r["y"] for r in res.results], axis=0)
```
